# Optimizing a Trainium2 kernel written in Bass

```python
import math
import jax, jax.numpy as jnp
from jax import lax
import numpy as np

D_MODEL = 1024
BATCH = 8
SEQ = 8192
DEPTH = 4

D_MIX = D_MODEL
MLA_HEADS = 4
MLA_NOPE = 128
MLA_ROPE = 64
MLA_V = 128
Q_LORA = 384
KV_LORA = 256
ROPE_BASE = 10000.0
Q_BLOCK = 128
MLA_W = MLA_HEADS * MLA_V
POOL_WINDOWS = (2, 4, 8, 16)
POOL_GROUPS = 4
POOL_CH = 64
POOL_W = POOL_GROUPS * POOL_CH
M_HEADS = 4
M_HEAD_DIM = 64
M_W = M_HEADS * M_HEAD_DIM
M_CONV = 4
M_CHUNK = 64
D_FF = 2816
FFN_CONV = 3
LN_EPS = 1e-5
RMS_EPS = 1e-6
ALPHA = (2 * DEPTH) ** 0.25
BETA = (8 * DEPTH) ** -0.25

IN_SIZES = (Q_LORA, KV_LORA, MLA_ROPE, POOL_W, M_W, M_W, M_W, M_HEADS, M_HEADS)
D_IN = Q_LORA + KV_LORA + MLA_ROPE + POOL_W + 3 * M_W + 2 * M_HEADS

kernel_name = "hymba_style_mla_pool_mlstm_deepnorm"


def _split_points():
    return [int(v) for v in np.cumsum(np.array(IN_SIZES))[:-1]]


def layer_norm(x, g, b):
    xf = x.astype(jnp.float32)
    mu = jnp.mean(xf, axis=-1, keepdims=True)
    var = jnp.mean(jnp.square(xf - mu), axis=-1, keepdims=True)
    y = (xf - mu) * lax.rsqrt(var + LN_EPS) * g + b
    return y.astype(x.dtype)


def rms_norm(x, g):
    xf = x.astype(jnp.float32)
    y = xf * lax.rsqrt(jnp.mean(jnp.square(xf), axis=-1, keepdims=True) + RMS_EPS) * g
    return y.astype(x.dtype)


def rotary_tables(seq, dim):
    inv = 1.0 / (ROPE_BASE ** (jnp.arange(0, dim, 2, dtype=jnp.float32) / dim))
    ang = jnp.arange(seq, dtype=jnp.float32)[:, None] * inv[None, :]
    return jnp.cos(ang), jnp.sin(ang)


def apply_rotary(x, cos, sin):
    xf = x.astype(jnp.float32)
    half = xf.shape[-1] // 2
    x1, x2 = xf[..., :half], xf[..., half:]
    return jnp.concatenate([x1 * cos - x2 * sin, x2 * cos + x1 * sin], axis=-1).astype(x.dtype)


def causal_dwconv(x, w, b):
    k = w.shape[0]
    y = lax.conv_general_dilated(
        x, w[:, None, :], window_strides=(1,), padding=((k - 1, 0),),
        dimension_numbers=("NWC", "WIO", "NWC"), feature_group_count=x.shape[-1])
    return y + b


def mla(cq, ckv, kr, g_qn, w_uq, g_kvn, w_ukv, cos, sin):
    B, S, _ = cq.shape
    H = MLA_HEADS
    q = (rms_norm(cq, g_qn) @ w_uq).reshape(B, S, H, MLA_NOPE + MLA_ROPE)
    q_nope, q_pe = q[..., :MLA_NOPE], q[..., MLA_NOPE:]
    q_pe = apply_rotary(q_pe, cos[:, None, :], sin[:, None, :])
    kv = (rms_norm(ckv, g_kvn) @ w_ukv).reshape(B, S, H, MLA_NOPE + MLA_V)
    k_nope, v = kv[..., :MLA_NOPE], kv[..., MLA_NOPE:]
    k_pe = apply_rotary(kr, cos, sin)
    scale = (MLA_NOPE + MLA_ROPE) ** -0.5
    nb = S // Q_BLOCK
    qn_b = (q_nope * scale).reshape(B, nb, Q_BLOCK, H, MLA_NOPE).transpose(1, 0, 3, 2, 4)
    qp_b = (q_pe * scale).reshape(B, nb, Q_BLOCK, H, MLA_ROPE).transpose(1, 0, 3, 2, 4)
    k_nope_t = k_nope.transpose(0, 2, 1, 3)
    v_t = v.transpose(0, 2, 1, 3)
    key_pos = jnp.arange(S)

    def block(args):
        qn, qp, start = args
        s = (jnp.einsum("bhqd,bhkd->bhqk", qn, k_nope_t)
             + jnp.einsum("bhqd,bkd->bhqk", qp, k_pe)).astype(jnp.float32)
        q_pos = start + jnp.arange(Q_BLOCK)
        s = jnp.where(key_pos[None, :] <= q_pos[:, None], s, -jnp.inf)
        p = jax.nn.softmax(s, axis=-1).astype(v_t.dtype)
        return jnp.einsum("bhqk,bhkd->bhqd", p, v_t)

    o = lax.map(block, (qn_b, qp_b, jnp.arange(nb) * Q_BLOCK))
    return o.transpose(1, 0, 3, 2, 4).reshape(B, S, H * MLA_V).astype(cq.dtype)


def multiscale_pool(p, w_pool, s_pool):
    B, S, _ = p.shape
    G, C = POOL_GROUPS, POOL_CH
    pg = p.reshape(B, S, G, C).astype(jnp.float32)
    cs = jnp.concatenate([jnp.zeros((B, 1, G, C), jnp.float32), jnp.cumsum(pg, axis=1)], axis=1)
    t = jnp.arange(S)
    win = jnp.array(POOL_WINDOWS, dtype=jnp.int32)
    lo = jnp.maximum(t[:, None] + 1 - win[None, :], 0)
    window_sum = cs[:, 1:] - cs[:, lo, jnp.arange(G)[None, :]]
    count = (t[:, None] + 1 - lo).astype(jnp.float32)
    y = (window_sum / count[None, :, :, None] - pg).astype(p.dtype)
    y = jnp.einsum("bsgc,gcd->bsgd", y, w_pool) * s_pool.reshape(G, C)
    return y.reshape(B, S, G * C).astype(p.dtype)


def mlstm(xqk, xv, xo, ipre, fpre, w_mconv, b_mconv, w_mq, w_mk, b_i, b_f, g_mh):
    B, S, _ = xqk.shape
    H, Dh, L = M_HEADS, M_HEAD_DIM, M_CHUNK
    nc = S // L
    c = jax.nn.silu(causal_dwconv(xqk, w_mconv, b_mconv)).reshape(B, S, H, Dh)
    q = jnp.einsum("bshd,hde->bshe", c, w_mq).astype(jnp.float32)
    k = (jnp.einsum("bshd,hde->bshe", c, w_mk) * (Dh ** -0.5)).astype(jnp.float32)
    v = xv.reshape(B, S, H, Dh).astype(jnp.float32)
    log_i = (ipre + b_i).astype(jnp.float32)
    log_f = jax.nn.log_sigmoid((fpre + b_f).astype(jnp.float32))

    def to_chunks4(a):
        return a.reshape(B, nc, L, H, Dh).transpose(1, 0, 3, 2, 4)

    def to_chunks3(a):
        return a.reshape(B, nc, L, H).transpose(1, 0, 3, 2)

    tri = jnp.tril(jnp.ones((L, L), dtype=bool))

    def step(carry, inp):
        Cm, n, m = carry
        qc, kc, vc, li, lf = inp
        b = jnp.cumsum(lf, axis=-1)
        D = jnp.where(tri, b[..., :, None] - b[..., None, :] + li[..., None, :], -jnp.inf)
        inter = b + m[..., None]
        m_t = jnp.maximum(inter, jnp.max(D, axis=-1))
        g = jnp.exp(inter - m_t)
        sc = jnp.einsum("bhtd,bhsd->bhts", qc, kc) * jnp.exp(D - m_t[..., None])
        num = g[..., None] * jnp.einsum("bhtd,bhde->bhte", qc, Cm) + jnp.einsum("bhts,bhse->bhte", sc, vc)
        den = g * jnp.einsum("bhtd,bhd->bht", qc, n) + jnp.sum(sc, axis=-1)
        h = num / jnp.maximum(jnp.abs(den), jnp.exp(-m_t))[..., None]
        bL = b[..., -1]
        w_log = bL[..., None] - b + li
        m_new = jnp.maximum(bL + m, jnp.max(w_log, axis=-1))
        w = jnp.exp(w_log - m_new[..., None])
        decay = jnp.exp(bL + m - m_new)
        C_new = decay[..., None, None] * Cm + jnp.einsum("bhs,bhsd,bhse->bhde", w, kc, vc)
        n_new = decay[..., None] * n + jnp.einsum("bhs,bhsd->bhd", w, kc)
        return (C_new, n_new, m_new), h

    init = (jnp.zeros((B, H, Dh, Dh), jnp.float32), jnp.zeros((B, H, Dh), jnp.float32),
            jnp.zeros((B, H), jnp.float32))
    _, hs = lax.scan(step, init, (to_chunks4(q), to_chunks4(k), to_chunks4(v),
                                  to_chunks3(log_i), to_chunks3(log_f)))
    h = hs.transpose(1, 0, 3, 2, 4).reshape(B, S, H, Dh)
    mu = jnp.mean(h, axis=-1, keepdims=True)
    var = jnp.mean(jnp.square(h - mu), axis=-1, keepdims=True)
    h = (h - mu) * lax.rsqrt(var + LN_EPS) * g_mh.reshape(H, Dh)
    out = jax.nn.sigmoid(xo.astype(jnp.float32)).reshape(B, S, H, Dh) * h
    return out.reshape(B, S, M_W).astype(xqk.dtype)


def conv_glu_ffn(x, w_up, w_fconv, b_fconv, w_down):
    a, g = jnp.split(x @ w_up, 2, axis=-1)
    g = causal_dwconv(g, w_fconv, b_fconv)
    return (jax.nn.silu(g) * a) @ w_down


def setup_inputs(seed: int = 0) -> dict:
    key = jax.random.key(seed)
    ks = jax.random.split(key, 32)
    f32 = jnp.float32

    def nrm(k, shape, scale):
        return jax.random.normal(k, shape, f32) * scale

    def gain(k, shape):
        return 1.0 + 0.02 * jax.random.normal(k, shape, f32)

    L = DEPTH
    return {
        "x": nrm(ks[0], (BATCH, SEQ, D_MODEL), 1.0),
        "ln0_g": gain(ks[1], (D_MODEL,)),
        "ln0_b": nrm(ks[2], (D_MODEL,), 0.02),
        "w_in": nrm(ks[3], (L, D_MODEL, D_IN), D_MODEL ** -0.5),
        "g_qn": gain(ks[4], (L, Q_LORA)),
        "w_uq": nrm(ks[5], (L, Q_LORA, MLA_HEADS * (MLA_NOPE + MLA_ROPE)), Q_LORA ** -0.5),
        "g_kvn": gain(ks[6], (L, KV_LORA)),
        "w_ukv": nrm(ks[7], (L, KV_LORA, MLA_HEADS * (MLA_NOPE + MLA_V)), KV_LORA ** -0.5),
        "w_pool": nrm(ks[8], (L, POOL_GROUPS, POOL_CH, POOL_CH), POOL_CH ** -0.5),
        "s_pool": gain(ks[9], (L, POOL_W)),
        "w_mconv": nrm(ks[10], (L, M_CONV, M_W), M_CONV ** -0.5),
        "b_mconv": nrm(ks[11], (L, M_W), 0.02),
        "w_mq": nrm(ks[12], (L, M_HEADS, M_HEAD_DIM, M_HEAD_DIM), M_HEAD_DIM ** -0.5),
        "w_mk": nrm(ks[13], (L, M_HEADS, M_HEAD_DIM, M_HEAD_DIM), M_HEAD_DIM ** -0.5),
        "b_i": nrm(ks[14], (L, M_HEADS), 0.1),
        "b_f": jnp.linspace(3.0, 6.0, M_HEADS, dtype=f32)[None, :] + nrm(ks[15], (L, M_HEADS), 0.1),
        "g_mh": gain(ks[16], (L, M_W)),
        "w_out": nrm(ks[17], (L, D_MIX, D_MODEL), BETA * D_MIX ** -0.5),
        "ln1_g": gain(ks[18], (L, D_MODEL)),
        "ln1_b": nrm(ks[19], (L, D_MODEL), 0.02),
        "w_up": nrm(ks[20], (L, D_MODEL, 2 * D_FF), D_MODEL ** -0.5),
        "w_fconv": nrm(ks[21], (L, FFN_CONV, D_FF), FFN_CONV ** -0.5),
        "b_fconv": nrm(ks[22], (L, D_FF), 0.02),
        "w_down": nrm(ks[23], (L, D_FF, D_MODEL), BETA * D_FF ** -0.5),
        "ln2_g": gain(ks[24], (L, D_MODEL)),
        "ln2_b": nrm(ks[25], (L, D_MODEL), 0.02),
    }


def reference(x, ln0_g, ln0_b, w_in, g_qn, w_uq, g_kvn, w_ukv, w_pool, s_pool, w_mconv,
              b_mconv, w_mq, w_mk, b_i, b_f, g_mh, w_out, ln1_g, ln1_b, w_up, w_fconv,
              b_fconv, w_down, ln2_g, ln2_b):
    S = x.shape[1]
    cos, sin = rotary_tables(S, MLA_ROPE)
    h = layer_norm(x, ln0_g, ln0_b)
    for l in range(DEPTH):
        proj = h @ w_in[l]
        cq, ckv, kr, xp, xqk, xv, xo, ipre, fpre = jnp.split(proj, _split_points(), axis=-1)
        a = mla(cq, ckv, kr, g_qn[l], w_uq[l], g_kvn[l], w_ukv[l], cos, sin)
        p = multiscale_pool(xp, w_pool[l], s_pool[l])
        m = mlstm(xqk, xv, xo, ipre, fpre, w_mconv[l], b_mconv[l], w_mq[l], w_mk[l],
                  b_i[l], b_f[l], g_mh[l])
        mix = jnp.concatenate([a.astype(h.dtype), p.astype(h.dtype), m.astype(h.dtype)], axis=-1) @ w_out[l]
        h = layer_norm(ALPHA * h + mix, ln1_g[l], ln1_b[l])
        f = conv_glu_ffn(h, w_up[l], w_fconv[l], b_fconv[l], w_down[l])
        h = layer_norm(ALPHA * h + f, ln2_g[l], ln2_b[l])
    return h
```

```python
import contextlib
import os
import numpy as np
import concourse.bass as bass
import concourse.mybir as mybir
from concourse.bass_utils import run_bass_kernel_spmd

F32 = mybir.dt.float32
BF16 = mybir.dt.bfloat16
ALU = mybir.AluOpType
AF = mybir.ActivationFunctionType
AX = mybir.AxisListType

ENGS = ("pe", "act", "dve", "pool", "sp")

D_MODEL = 1024
DEPTH_FULL = 4
Q_LORA, KV_LORA, MLA_ROPE = 384, 256, 64
D_IN = 1736
D_FF = 2816
NFC = D_FF // 128
LN_EPS = 1e-5
RMS_EPS = 1e-6
ALPHA = float((2 * DEPTH_FULL) ** 0.25)
QSCALE = float((128 + 64) ** -0.5)
WINE = 1800


class Buf:
    __slots__ = ("name", "w", "r", "dsem")

    def __init__(self, name):
        self.name = name
        self.w = {}
        self.r = {}
        self.dsem = None


class Prog:
    def __init__(self, nc):
        self.nc = nc
        self.ops = {e: [] for e in ENGS}
        self.tot = {e: 0 for e in ENGS}
        self.seen = {e: {} for e in ENGS}
        self.dsems = []
        self.free_dsems = []
        self.epoch = 0
        self.nbuf = 0

    def buf(self, name=None):
        self.nbuf += 1
        return Buf(name or "b%d" % self.nbuf)

    def bufs(self, n):
        return [self.buf() for _ in range(n)]

    def _need(self, eng, deps):
        out = {}
        seen = self.seen[eng]
        for (k, v) in deps:
            if seen.get(k, 0) >= v:
                continue
            if out.get(k, 0) < v:
                out[k] = v
        for k, v in out.items():
            seen[k] = v
        return out

    def _track(self, eng, me, reads, writes, acc):
        deps = []
        for b in reads:
            deps += list(b.w.items())
        for b in writes:
            deps += list(b.r.items())
            if not acc:
                deps += list(b.w.items())
        waits = self._need(eng, deps)
        k, v = me
        for b in reads:
            if b.r.get(k, 0) < v:
                b.r[k] = v
        for b in writes:
            if acc:
                if b.w.get(k, 0) < v:
                    b.w[k] = v
            else:
                b.w = {k: v}
                b.r = {}
        return waits

    def begin(self):
        if os.environ.get("SCHED", "1") == "1":
            self.rec = []

    def end(self):
        rec = self.rec
        self.rec = None
        if not rec:
            return
        n = len(rec)
        lastw, readers = {}, {}
        preds = [set() for _ in range(n)]
        for i, r in enumerate(rec):
            reads, writes, acc = r[2], r[3], r[4]
            for b in reads:
                preds[i].update(lastw.get(id(b), ()))
            for b in writes:
                preds[i].update(readers.get(id(b), ()))
                preds[i].update(lastw.get(id(b), ()))
            for b in reads:
                readers.setdefault(id(b), []).append(i)
            for b in writes:
                if acc:
                    lastw.setdefault(id(b), []).append(i)
                else:
                    lastw[id(b)] = [i]
                    readers[id(b)] = []
            preds[i].discard(i)
        succs = [[] for _ in range(n)]
        indeg = [0] * n
        for i in range(n):
            indeg[i] = len(preds[i])
            for p in preds[i]:
                succs[p].append(i)
        rt = [0.0] * n
        free = {e: 0.0 for e in ENGS}
        ready = [i for i in range(n) if indeg[i] == 0]
        order = []
        HOP = 0.25
        while ready:
            best, bs = None, None
            for i in ready:
                st = max(rt[i], free[rec[i][1]])
                key = (st, i)
                if bs is None or key < bs:
                    best, bs = i, key
            ready.remove(best)
            r = rec[best]
            st = bs[0]
            busy, lat = r[5]
            free[r[1]] = st + busy
            fin = st + lat
            order.append(best)
            for j in succs[best]:
                if rt[j] < fin + HOP:
                    rt[j] = fin + HOP
                indeg[j] -= 1
                if indeg[j] == 0:
                    ready.append(j)
        assert len(order) == n
        for i in order:
            r = rec[i]
            if r[0] == "op":
                self.op(r[1], r[6], r[2], r[3])
            else:
                self.dma(r[1], r[6], r[7], r[8], r[2], r[3], r[4])

    rec = None

    def op(self, eng, fn, reads=(), writes=(), cost=None):
        if self.rec is not None:
            c = cost if cost is not None else {"pe": 0.3, "act": 0.5, "dve": 0.5, "pool": 0.7, "sp": 0.5}[eng]
            self.rec.append(("op", eng, tuple(reads), tuple(writes), False, (c, c), fn))
            return
        self.tot[eng] += 1
        me = (eng, self.tot[eng])
        waits = self._track(eng, me, reads, writes, False)
        self.ops[eng].append((waits, fn, (eng, 1)))
        return me

    def dma(self, q, out_ap, in_ap, sb, reads=(), writes=(), acc=False):
        if self.rec is not None:
            self.rec.append(("dma", q, tuple(reads), tuple(writes), acc, (0.1, 2.5), out_ap, in_ap, sb))
            return
        if sb.dsem is None or sb.dsem[2] != self.epoch:
            if self.free_dsems:
                kk = self.free_dsems.pop()
            else:
                kk = ("d", len(self.dsems))
                self.dsems.append(kk)
                self.tot[kk] = 0
            sb.dsem = (kk[0], kk[1], self.epoch)
        key = (sb.dsem[0], sb.dsem[1])
        self.tot[key] += 16
        me = (key, self.tot[key])
        waits = self._track(q, me, reads, writes, acc)

        def fn(e, out_ap=out_ap, in_ap=in_ap):
            return e.dma_start(out=out_ap, in_=in_ap)
        self.ops[q].append((waits, fn, (key, 16)))
        return me

    def barrier(self):
        if self.rec is not None:
            self.end()
        allk = [(k, v) for k, v in self.tot.items() if v > 0]
        for e in ENGS:
            waits = self._need(e, allk)
            if waits:
                self.ops[e].append((waits, None, None))
        self.epoch += 1
        self.free_dsems = list(self.dsems)

    def emit(self):
        nc = self.nc
        with contextlib.ExitStack() as st:
            sems = {}
            for e in ENGS:
                sems[e] = st.enter_context(nc.semaphore("s_" + e))
            for k in self.dsems:
                sems[k] = st.enter_context(nc.semaphore("d%d" % k[1]))
            block = st.enter_context(nc.Block())
            handles = {"pe": block.tensor, "act": block.scalar, "dve": block.vector,
                       "pool": block.gpsimd, "sp": block.sync}

            def mk(ename):
                def body(eng):
                    for (waits, fn, inc) in self.ops[ename]:
                        for k, v in waits.items():
                            eng.wait_ge(sems[k], v)
                        if fn is None:
                            continue
                        ins = fn(eng)
                        ins.then_inc(sems[inc[0]], inc[1])
                return body
            for e in ENGS:
                handles[e](mk(e))


def build(S, DEPTH, dbg=False, phases="0ABCD"):
    nc = bass.Bass("TRN2", target_bir_lowering=False)
    P = Prog(nc)
    NT = S // 128
    NT5 = S // 512
    L = DEPTH

    def din(name, shape):
        return nc.dram_tensor(name, list(shape), F32, kind="ExternalInput").ap()

    def dscr(name, shape, dt):
        kind = "ExternalOutput" if dbg else "Internal"
        return nc.dram_tensor(name, list(shape), dt, kind=kind).ap()

    x = din("x", [S, 1024])
    ln0_g = din("ln0_g", [1024]); ln0_b = din("ln0_b", [1024])
    w_in = din("w_in_e", [L, 1024, WINE])
    w_uq = din("w_uq_e", [L, 384, 1024])
    w_k = din("w_k", [L, 256, 512]); w_v = din("w_v", [L, 256, 512])
    bdp = din("bdp", [L, 128, 2, 128]); bdq = din("bdq", [L, 128, 2, 128]); bdk = din("bdk", [L, 128, 2, 128])
    gq = din("gq", [L, 128, 3]); gkv = din("gkv", [L, 128, 2]); spool = din("spool", [L, 128, 2])
    wmc = din("wmc", [L, 128, 2, 4]); bmc = din("bmc", [L, 128, 2])
    bif = din("bif", [L, 8]); gmh = din("gmh", [L, 256])
    w_out = din("w_out", [L, 1024, 1024])
    ln1_g = din("ln1_g", [L, 1024]); ln1_b = din("ln1_b", [L, 1024])
    w_up = din("w_up", [L, 1024, 2 * D_FF])
    wfc = din("wfc", [L, 128, NFC, 3]); bfc = din("bfc", [L, 128, NFC])
    w_down = din("w_down", [L, D_FF, 1024])
    ln2_g = din("ln2_g", [L, 1024]); ln2_b = din("ln2_b", [L, 1024])
    c_ident = din("c_ident", [128, 128]); c_tri = din("c_tri", [128, 128])
    c_cs = din("c_cs", [64, S]); c_sn = din("c_sn", [64, S])
    c_pinv = din("c_pinv", [2, 128, 2, 512])
    y = nc.dram_tensor("y", [S, 1024], F32, kind="ExternalOutput").ap()

    hA = dscr("hA", [S, 1024], F32); hB = dscr("hB", [S, 1024], F32)
    QN = dscr("QN", [4, 128, S], BF16); QR = dscr("QR", [4, 64, S], BF16)
    KT = dscr("KT", [4, 128, S], BF16); KR = dscr("KR", [64, S], BF16)
    VV = dscr("VV", [S, 512], BF16); MIXT = dscr("MIXT", [1024, S], BF16)
    B_hA, B_hB, B_QN, B_QR, B_KT, B_KR, B_VV, B_MIX = P.bufs(8)

    arena = nc.alloc_sbuf_tensor("arena", [128, 53000], F32)
    ps_all = nc.alloc_psum_tensor("ps", [128, 4096], F32)
    off = [0]

    def reset():
        P.barrier()
        off[0] = 0

    def carve(n, dt=F32):
        nb = n * (2 if dt == BF16 else 4)
        nw = (nb + 3) // 4
        nw = (nw + 7) // 8 * 8
        a = arena[:, off[0]:off[0] + nw]
        off[0] += nw
        assert off[0] <= 53000, "SBUF overflow %d" % off[0]
        if dt == BF16:
            return a.bitcast(BF16)[:, 0:n]
        return a[:, 0:n]

    def c3(n1, n2, dt=F32):
        return carve(n1 * n2, dt).rearrange("p (a b) -> p a b", b=n2)

    banks = [ps_all[:, i * 512:(i + 1) * 512] for i in range(8)]
    Bbank = P.bufs(8)
    ring_state = {"i": 0, "lst": list(range(8))}

    def ring():
        lst = ring_state["lst"]
        i = lst[ring_state["i"] % len(lst)]
        ring_state["i"] += 1
        return banks[i], Bbank[i]

    def set_ring(lst):
        ring_state["lst"] = list(lst)
        ring_state["i"] = 0

    def mm(groups, reads, Bps):
        def fn(e, groups=groups):
            ins = None
            for (o, pairs) in groups:
                n = len(pairs)
                for i, (l, r) in enumerate(pairs):
                    ins = e.matmul(o, l, r, start=(i == 0), stop=(i == n - 1))
            return ins
        cst = 0.0
        for (o, pairs) in groups:
            cst += len(pairs) * (o.shape[-1] / 2400.0 + 0.04)
        P.op("pe", fn, reads=reads, writes=[Bps], cost=cst)

    def tr(groups, reads, Bps, ident):
        def fn(e, groups=groups):
            ins = None
            for (o, i_) in groups:
                ins = e.transpose(o, i_, ident)
            return ins
        P.op("pe", fn, reads=reads, writes=[Bps], cost=0.1 * len(groups))

    CONST_W = 0

    def load_consts():
        ident = carve(128, BF16); tri_b = carve(128, BF16); tri_f = carve(128)
        ones_b = carve(128, BF16); ones_f = carve(128); nh = carve(512)
        Bc = P.buf("consts")
        P.dma("pool", ident, c_ident, Bc, writes=[Bc], acc=True)
        P.dma("pool", tri_b, c_tri, Bc, writes=[Bc], acc=True)
        P.dma("sp", tri_f, c_tri, Bc, writes=[Bc], acc=True)
        B1 = P.buf()
        P.op("pool", lambda e: e.memset(ones_b, 1.0), writes=[B1])
        B2 = P.buf()
        P.op("pool", lambda e: e.memset(ones_f, 1.0), writes=[B2])
        B3 = P.buf()
        P.op("pool", lambda e: e.memset(nh, -0.5), writes=[B3])
        return dict(ident=ident, tri_b=tri_b, tri_f=tri_f, ones_b=ones_b, ones_f=ones_f, nh=nh,
                    B=[Bc, B1, B2, B3])

    C = load_consts()
    const_off = off[0]
    BC = C["B"]

    def reset_phase():
        P.barrier()
        off[0] = const_off

    def ln_tile(src, Bsrc, dst, Bdst, gb, bb, Bgb, tmp):
        st, mv, ve, rs, Bst0, Bst1, Bmv, Bve, Brs = tmp
        P.op("dve", lambda e: e.bn_stats(st[:, 0:6], src[:, 0:512]), reads=[Bsrc], writes=[Bst0])
        P.op("dve", lambda e: e.bn_stats(st[:, 6:12], src[:, 512:1024]), reads=[Bsrc], writes=[Bst1])
        P.op("dve", lambda e: e.bn_aggr(mv, st), reads=[Bst0, Bst1], writes=[Bmv])
        P.op("dve", lambda e: e.tensor_scalar_add(ve, mv[:, 1:2], LN_EPS), reads=[Bmv], writes=[Bve])
        P.op("pool", lambda e: e.tensor_tensor(rs, ve, C["nh"][:, 0:1], ALU.pow), reads=[Bve] + BC, writes=[Brs])
        P.op("dve", lambda e: e.tensor_scalar(dst, src, mv[:, 0:1], rs, ALU.subtract, ALU.mult),
             reads=[Bsrc, Bmv, Brs], writes=[Bdst])
        P.op("pool", lambda e: e.tensor_tensor(dst, dst, gb, ALU.mult), reads=[Bgb], writes=[Bdst])
        P.op("pool", lambda e: e.tensor_tensor(dst, dst, bb, ALU.add), reads=[Bgb], writes=[Bdst])

    def ln_tmp():
        return (carve(12), carve(2), carve(1), carve(1)) + tuple(P.bufs(5))

    def phase0(dst, Bdst_dram):
        reset_phase()
        gb = carve(1024); bb = carve(1024); Bgb = P.buf()
        P.dma("sp", gb, ln0_g.partition_broadcast(128), Bgb, writes=[Bgb], acc=True)
        P.dma("sp", bb, ln0_b.partition_broadcast(128), Bgb, writes=[Bgb], acc=True)
        xt = [carve(1024) for _ in range(2)]; Bx = P.bufs(2)
        yt = [carve(1024) for _ in range(2)]; By = P.bufs(2)
        tmps = [ln_tmp() for _ in range(2)]
        if "0" in os.environ.get("SCHED_PH", "0ACD"):
            P.begin()
        for i in range(NT):
            s = i % 2
            P.dma("sp", xt[s], x[i * 128:(i + 1) * 128, :], Bx[s], writes=[Bx[s]])
            ln_tile(xt[s], Bx[s], yt[s], By[s], gb, bb, Bgb, tmps[s])
            P.dma("sp", dst[i * 128:(i + 1) * 128, :], yt[s], By[s], reads=[By[s]], writes=[Bdst_dram], acc=True)

    def phaseA(l, hin, Bhin):
        reset_phase()
        set_ring(range(8))
        ident = C["ident"]
        win = c3(8, WINE, BF16); wuq = c3(3, 1024, BF16); wk = c3(2, 512, BF16); wv = c3(2, 512, BF16)
        bp = c3(2, 128, BF16); bq = c3(2, 128, BF16); bk = c3(2, 128, BF16)
        Bw = P.buf("wA")
        for k in range(8):
            P.dma("pool", win[:, k, :], w_in[l, k * 128:(k + 1) * 128, :], Bw, writes=[Bw], acc=True)
        P.dma("pool", wuq, w_uq[l].rearrange("(k p) n -> p k n", p=128), Bw, writes=[Bw], acc=True)
        P.dma("pool", wk, w_k[l].rearrange("(k p) n -> p k n", p=128), Bw, writes=[Bw], acc=True)
        P.dma("pool", wv, w_v[l].rearrange("(k p) n -> p k n", p=128), Bw, writes=[Bw], acc=True)
        P.dma("pool", bp, bdp[l], Bw, writes=[Bw], acc=True)
        P.dma("pool", bq, bdq[l], Bw, writes=[Bw], acc=True)
        P.dma("pool", bk, bdk[l], Bw, writes=[Bw], acc=True)
        gq_t = carve(3); gkv_t = carve(2); sp_t = carve(2); wmc_t = c3(2, 4); bmc_t = carve(2)
        bif_t = carve(8); gmh_t = carve(256); pinv = [c3(2, 512) for _ in range(2)]
        Bsm = P.buf("smallA")
        P.dma("sp", gq_t, gq[l], Bsm, writes=[Bsm], acc=True)
        P.dma("sp", gkv_t, gkv[l], Bsm, writes=[Bsm], acc=True)
        P.dma("sp", sp_t, spool[l], Bsm, writes=[Bsm], acc=True)
        P.dma("sp", wmc_t, wmc[l], Bsm, writes=[Bsm], acc=True)
        P.dma("sp", bmc_t, bmc[l], Bsm, writes=[Bsm], acc=True)
        P.dma("sp", bif_t, bif[l].partition_broadcast(128), Bsm, writes=[Bsm], acc=True)
        P.dma("sp", gmh_t, gmh[l].partition_broadcast(128), Bsm, writes=[Bsm], acc=True)
        P.dma("sp", pinv[0], c_pinv[0], Bsm, writes=[Bsm], acc=True)
        P.dma("sp", pinv[1], c_pinv[1], Bsm, writes=[Bsm], acc=True)
        gqs = carve(3); Bgqs = P.buf()
        P.op("dve", lambda e: e.tensor_scalar_mul(gqs, gq_t, QSCALE), reads=[Bsm], writes=[Bgqs])

        if int(os.environ.get('ASTOP', '99')) == 1:
            return
        hx = c3(4, 1024); Bhx = P.buf()
        hb = c3(4, 1024, BF16); Bhb = P.bufs(2)
        hT = c3(8, 512, BF16); BhT = P.bufs(4)
        cs_t = [carve(512) for _ in range(1)]; sn_t = [carve(512) for _ in range(1)]; Bcs = P.bufs(1)
        craw = c3(5, 512); Bcraw = P.bufs(5)
        sq = c3(5, 512, BF16); Bsq = P.bufs(5)
        cn = c3(5, 512, BF16); Bcn = P.bufs(5)
        rr = c3(2, 512); Brr = P.bufs(2)
        t1 = [carve(512) for _ in range(1)]; t2 = [carve(512) for _ in range(1)]; Bt1 = P.bufs(1); Bt2 = P.bufs(1)
        xpb = c3(2, 528); Bxp = P.buf(); T1 = c3(2, 528); T2 = c3(2, 528); BT1 = P.buf(); BT2 = P.buf()
        ybf = c3(2, 512, BF16); Bybf = P.buf()
        xqb = c3(2, 515); Bxq = P.buf(); acc = c3(2, 512); Bacc = P.bufs(2)
        cT = c3(2, 512, BF16); BcT = P.bufs(2)
        mqT = c3(2, 512, BF16); mkTe = c3(2, 512, BF16); mkTo = c3(2, 512, BF16); BmqT = P.bufs(2); BmkT = P.bufs(2)
        mktok = c3(4, 256, BF16); Bmktok = P.bufs(4)
        xvo = c3(4, 512); Bxvo = P.bufs(4)
        gz = c3(4, 8); Bgz = P.bufs(4)
        QNs = [c3(4, 512, BF16) for _ in range(1)]; QRs = [c3(4, 512, BF16) for _ in range(1)]
        KTs = [c3(4, 512, BF16) for _ in range(1)]; KRs = [carve(512, BF16) for _ in range(1)]
        Vs = [c3(4, 512, BF16) for _ in range(1)]; PTs = [c3(2, 512, BF16) for _ in range(1)]
        MTs = [c3(2, 512, BF16) for _ in range(1)]
        BQN = P.bufs(2); BQR = P.bufs(2); BKT = P.bufs(2); BKR = P.bufs(2); BV = P.bufs(2); BPT = P.bufs(2); BMT = P.bufs(2)
        Sf = c3(2, 66); BSf = P.buf(); Sbe = [c3(2, 66, BF16) for _ in range(2)]; Sbo = [c3(2, 66, BF16) for _ in range(2)]; BSb = P.bufs(2)
        Stmp = c3(2, 66); BStmp = P.buf()
        vaug = [c3(4, 66, BF16) for _ in range(4)]; Bvaug = P.bufs(4)
        scm = [c3(4, 128, BF16) for _ in range(4)]; Bscm = P.bufs(4)
        sm = [carve(64) for _ in range(4)]; Bsm2 = [P.bufs(12) for _ in range(4)]
        hh = [c3(4, 64) for _ in range(4)]; Bhh = P.bufs(4); hc = [c3(4, 64) for _ in range(4)]; Bhc = P.bufs(4)
        hsq = [c3(4, 64) for _ in range(4)]; Bhsq = P.bufs(4)
        sig = [carve(256) for _ in range(4)]; Bsig = P.bufs(4)
        mtok = [carve(256, BF16) for _ in range(4)]; Bmtok = P.bufs(4)
        afull = [carve(2) for _ in range(4)]; Baf = P.bufs(4)

        P.op("dve", lambda e: e.memset(Sf, 0.0), writes=[BSf])
        for i_ in range(2):
            P.op("pool", lambda e, i_=i_: e.memset(Sbe[i_], 0.0), writes=[BSb[i_]])
            P.op("pool", lambda e, i_=i_: e.memset(Sbo[i_], 0.0), writes=[BSb[i_]])
        P.op("pool", lambda e: e.memset(mkTe, 0.0), writes=BmkT)
        P.op("pool", lambda e: e.memset(mkTo, 0.0), writes=BmkT)
        P.op("pool", lambda e: e.memset(xqb[:, :, 0:3], 0.0), writes=[Bxq])
        P.op("pool", lambda e: e.memset(xpb[:, :, 0:16], 0.0), writes=[Bxp])
        sb_i = [0]

        def proj_fm(col0, ncols):
            ps, Bps = ring()
            mm([(ps[0:ncols, :], [(win[:, k, col0:col0 + ncols], hT[:, k, :]) for k in range(8)])],
               [Bw] + BhT, Bps)
            return ps, Bps

        if "A" in os.environ.get("SCHED_PH", "0ACD"):
            P.begin()
        for ti in range(NT5):
            t0 = ti * 512
            s2 = 0
            P.dma("sp", hx, hin[t0:t0 + 512, :].rearrange("(j p) d -> p j d", p=128), Bhx,
                  reads=[Bhin], writes=[Bhx])
            P.dma("sp", cs_t[s2][0:64, :], c_cs[:, t0:t0 + 512], Bcs[s2], writes=[Bcs[s2]], acc=False)
            P.dma("sp", sn_t[s2][0:64, :], c_sn[:, t0:t0 + 512], Bcs[s2], reads=[], writes=[Bcs[s2]], acc=True)
            cs = cs_t[s2][0:64, :]; sn = sn_t[s2][0:64, :]
            P.op("act", lambda e: e.copy(hb[:, 0:2, :], hx[:, 0:2, :]), reads=[Bhx], writes=[Bhb[0]])
            P.op("dve", lambda e: e.tensor_copy(hb[:, 2:4, :], hx[:, 2:4, :]), reads=[Bhx], writes=[Bhb[1]])
            for kp in range(4):
                ps, Bps = ring()
                psb = ps.bitcast(BF16)
                groups = []
                for kk in range(2):
                    k = 2 * kp + kk
                    for j in range(4):
                        groups.append((psb[:, kk * 512 + j * 128: kk * 512 + (j + 1) * 128],
                                       hb[:, j, k * 128:(k + 1) * 128]))
                tr(groups, Bhb + BC, Bps, ident)
                dstv = hT[:, 2 * kp:2 * kp + 2, :]
                srcv = psb.rearrange("p (a b) -> p a b", b=512)
                if kp % 2 == 0:
                    P.op("act", lambda e, d=dstv, s_=srcv: e.copy(d, s_), reads=[Bps], writes=[BhT[kp]])
                else:
                    P.op("dve", lambda e, d=dstv, s_=srcv: e.tensor_copy(d, s_), reads=[Bps], writes=[BhT[kp]])
            if int(os.environ.get('ASTOP', '99')) == 2:
                return
            for c in range(5):
                ps, Bps = proj_fm(c * 128, 128)
                P.op("act", lambda e, c=c, ps=ps: e.copy(craw[:, c, :], ps), reads=[Bps], writes=[Bcraw[c]])
                P.op("act", lambda e, c=c, ps=ps: e.activation(sq[:, c, :], ps, AF.Square), reads=[Bps], writes=[Bsq[c]])
            for (i, cl, n) in ((0, (0, 1, 2), 384.0), (1, (3, 4), 256.0)):
                ps, Bps = ring()
                mm([(ps, [(C["ones_b"], sq[:, c, :]) for c in cl])], [Bsq[c] for c in cl] + BC, Bps)
                P.op("dve", lambda e, i=i, ps=ps, n=n: e.tensor_scalar(rr[:, i, :], ps, 1.0 / n, RMS_EPS, ALU.mult, ALU.add),
                     reads=[Bps], writes=[Brr[i]])
                P.op("act", lambda e, i=i: e.activation(rr[:, i, :], rr[:, i, :], AF.Sqrt), reads=[Brr[i]], writes=[Brr[i]])
                P.op("dve", lambda e, i=i: e.reciprocal(rr[:, i, :], rr[:, i, :]), reads=[Brr[i]], writes=[Brr[i]])
            if int(os.environ.get('ASTOP', '99')) == 3:
                return
            psa, Bpsa = proj_fm(640, 64)
            psb_, Bpsb = proj_fm(1736, 64)
            P.op("dve", lambda e, psa=psa: e.tensor_tensor(t1[0][0:64, :], psa[0:64, :], cs, ALU.mult), reads=[Bpsa, Bcs[s2]], writes=[Bt1[0]])
            P.op("dve", lambda e, psb_=psb_: e.tensor_tensor(t2[0][0:64, :], psb_[0:64, :], sn, ALU.mult), reads=[Bpsb, Bcs[s2]], writes=[Bt2[0]])
            P.op("dve", lambda e: e.tensor_tensor(KRs[s2][0:64, :], t1[0][0:64, :], t2[0][0:64, :], ALU.add),
                 reads=[Bt1[0], Bt2[0]], writes=[BKR[s2]])
            if int(os.environ.get('ASTOP', '99')) == 4:
                return
            if ti > 0:
                P.op("pool", lambda e: e.tensor_copy(xpb[:, :, 0:16], xpb[:, :, 512:528]), reads=[Bxp], writes=[Bxp])
                P.op("pool", lambda e: e.tensor_copy(xqb[:, :, 0:3], xqb[:, :, 512:515]), reads=[Bxq], writes=[Bxq])
            for c in range(2):
                ps, Bps = proj_fm(704 + c * 128, 128)
                P.op("act", lambda e, c=c, ps=ps: e.copy(xpb[:, c, 16:528], ps), reads=[Bps], writes=[Bxp])
            P.op("dve", lambda e: e.tensor_tensor(T1[:, :, 1:528], xpb[:, :, 1:528], xpb[:, :, 0:527], ALU.add), reads=[Bxp], writes=[BT1])
            P.op("dve", lambda e: e.tensor_tensor(T2[:, :, 3:528], T1[:, :, 3:528], T1[:, :, 1:526], ALU.add), reads=[BT1], writes=[BT2])
            P.op("dve", lambda e: e.tensor_tensor(T1[:, 1, 7:528], T2[:, 1, 7:528], T2[:, 1, 3:524], ALU.add), reads=[BT2], writes=[BT1])
            P.op("dve", lambda e: e.tensor_tensor(T2[64:128, 1, 15:528], T1[64:128, 1, 15:528], T1[64:128, 1, 7:520], ALU.add), reads=[BT1], writes=[BT2])
            pv = pinv[0] if ti == 0 else pinv[1]
            P.op("dve", lambda e, pv=pv: e.tensor_tensor(T1[0:64, :, 16:528], T1[0:64, :, 16:528], pv[0:64, :, :], ALU.mult), reads=[BT1, Bsm], writes=[BT1])
            P.op("dve", lambda e, pv=pv: e.tensor_tensor(T2[64:128, :, 16:528], T2[64:128, :, 16:528], pv[64:128, :, :], ALU.mult), reads=[BT2, Bsm], writes=[BT2])
            P.op("dve", lambda e: e.tensor_tensor(ybf[0:64, :, :], T1[0:64, :, 16:528], xpb[0:64, :, 16:528], ALU.subtract), reads=[BT1, Bxp], writes=[Bybf])
            P.op("dve", lambda e: e.tensor_tensor(ybf[64:128, :, :], T2[64:128, :, 16:528], xpb[64:128, :, 16:528], ALU.subtract), reads=[BT2, Bxp], writes=[Bybf])
            if int(os.environ.get('ASTOP', '99')) == 5:
                return
            for c in range(2):
                ps, Bps = proj_fm(960 + c * 128, 128)
                P.op("act", lambda e, c=c, ps=ps: e.copy(xqb[:, c, 3:515], ps), reads=[Bps], writes=[Bxq])
            for c in range(2):
                P.op("dve", lambda e, c=c: e.tensor_scalar(acc[:, c, :], xqb[:, c, 0:512], wmc_t[:, c, 0:1], bmc_t[:, c:c + 1], ALU.mult, ALU.add),
                     reads=[Bxq, Bsm], writes=[Bacc[c]])
                for k in range(1, 4):
                    P.op("dve", lambda e, c=c, k=k: e.scalar_tensor_tensor(acc[:, c, :], xqb[:, c, k:k + 512], wmc_t[:, c, k:k + 1], acc[:, c, :], ALU.mult, ALU.add),
                         reads=[Bxq, Bsm], writes=[Bacc[c]])
                P.op("act", lambda e, c=c: e.activation(cT[:, c, :], acc[:, c, :], AF.Silu), reads=[Bacc[c]], writes=[BcT[c]])
            if int(os.environ.get('ASTOP', '99')) == 6:
                return
            for j in range(4):
                ps, Bps = ring()
                mm([(ps, [(hT[:, k, j * 128:(j + 1) * 128], win[:, k, 1216:1728]) for k in range(8)])], [Bw] + BhT, Bps)
                P.op("act", lambda e, j=j, ps=ps: e.copy(xvo[:, j, :], ps), reads=[Bps], writes=[Bxvo[j]])
                ps, Bps = ring()
                mm([(ps[:, 0:8], [(hT[:, k, j * 128:(j + 1) * 128], win[:, k, 1728:1736]) for k in range(8)])], [Bw] + BhT, Bps)
                P.op("dve", lambda e, j=j, ps=ps: e.tensor_tensor(gz[:, j, :], ps[:, 0:8], bif_t, ALU.add), reads=[Bps, Bsm], writes=[Bgz[j]])
            if int(os.environ.get('ASTOP', '99')) == 7:
                return
            for c in range(5):
                i = 0 if c < 3 else 1
                gcol = gqs[:, c:c + 1] if c < 3 else gkv_t[:, c - 3:c - 2]
                P.op("dve", lambda e, c=c, i=i, gcol=gcol: e.scalar_tensor_tensor(cn[:, c, :], craw[:, c, :], gcol, rr[:, i, :], ALU.mult, ALU.mult),
                     reads=[Bcraw[c], Brr[i], Bgqs, Bsm], writes=[Bcn[c]])
            for h in range(4):
                ps, Bps = ring()
                mm([(ps, [(wuq[:, c, h * 256:h * 256 + 128], cn[:, c, :]) for c in range(3)])], [Bw] + Bcn[0:3], Bps)
                P.op("act", lambda e, h=h, ps=ps: e.copy(QNs[s2][:, h, :], ps), reads=[Bps], writes=[BQN[s2]])
                psa, Bpsa = ring()
                mm([(psa[0:64, :], [(wuq[:, c, h * 256 + 128:h * 256 + 192], cn[:, c, :]) for c in range(3)])], [Bw] + Bcn[0:3], Bpsa)
                psb_, Bpsb = ring()
                mm([(psb_[0:64, :], [(wuq[:, c, h * 256 + 192:h * 256 + 256], cn[:, c, :]) for c in range(3)])], [Bw] + Bcn[0:3], Bpsb)
                a_ = 0
                P.op("dve", lambda e, psa=psa, a_=a_: e.tensor_tensor(t1[a_][0:64, :], psa[0:64, :], cs, ALU.mult), reads=[Bpsa, Bcs[s2]], writes=[Bt1[a_]])
                P.op("dve", lambda e, psb_=psb_, a_=a_: e.tensor_tensor(t2[a_][0:64, :], psb_[0:64, :], sn, ALU.mult), reads=[Bpsb, Bcs[s2]], writes=[Bt2[a_]])
                P.op("dve", lambda e, h=h, a_=a_: e.tensor_tensor(QRs[s2][0:64, h, :], t1[a_][0:64, :], t2[a_][0:64, :], ALU.add),
                     reads=[Bt1[a_], Bt2[a_]], writes=[BQR[s2]])
            for h in range(4):
                ps, Bps = ring()
                mm([(ps, [(wk[:, c, h * 128:(h + 1) * 128], cn[:, 3 + c, :]) for c in range(2)])], [Bw] + Bcn[3:5], Bps)
                if h % 2 == 0:
                    P.op("act", lambda e, h=h, ps=ps: e.copy(KTs[s2][:, h, :], ps), reads=[Bps], writes=[BKT[s2]])
                else:
                    P.op("dve", lambda e, h=h, ps=ps: e.tensor_copy(KTs[s2][:, h, :], ps), reads=[Bps], writes=[BKT[s2]])
            for j in range(4):
                ps, Bps = ring()
                mm([(ps, [(cn[:, 3 + c, j * 128:(j + 1) * 128], wv[:, c, :]) for c in range(2)])], [Bw] + Bcn[3:5], Bps)
                if j % 2 == 0:
                    P.op("act", lambda e, j=j, ps=ps: e.copy(Vs[s2][:, j, :], ps), reads=[Bps], writes=[BV[s2]])
                else:
                    P.op("dve", lambda e, j=j, ps=ps: e.tensor_copy(Vs[s2][:, j, :], ps), reads=[Bps], writes=[BV[s2]])
            if int(os.environ.get('ASTOP', '99')) == 8:
                return
            for c in range(2):
                ps, Bps = ring()
                mm([(ps, [(bp[:, c, :], ybf[:, c, :])])], [Bw, Bybf], Bps)
                P.op("act", lambda e, c=c, ps=ps: e.activation(PTs[s2][:, c, :], ps, AF.Copy, scale=sp_t[:, c:c + 1]),
                     reads=[Bps, Bsm], writes=[BPT[s2]])
            for c in range(2):
                ps, Bps = ring()
                mm([(ps, [(bq[:, c, :], cT[:, c, :])])], [Bw, BcT[c]], Bps)
                P.op("act", lambda e, c=c, ps=ps: e.copy(mqT[:, c, :], ps), reads=[Bps], writes=[BmqT[c]])
                ps, Bps = ring()
                mm([(ps, [(bk[:, c, :], cT[:, c, :])])], [Bw, BcT[c]], Bps)
                P.op("act", lambda e, c=c, ps=ps: e.mul(mkTe[0:64, c, :], ps[0:64, :], 0.125), reads=[Bps], writes=[BmkT[c]])
                P.op("act", lambda e, c=c, ps=ps: e.mul(mkTo[64:128, c, :], ps[64:128, :], 0.125), reads=[Bps], writes=[BmkT[c]])
            for j in range(4):
                ps, Bps = ring()
                mm([(ps[:, c * 128:(c + 1) * 128], [(cT[:, c, j * 128:(j + 1) * 128], bk[:, c, :])]) for c in range(2)], [Bw] + BcT, Bps)
                P.op("act", lambda e, j=j, ps=ps: e.mul(mktok[:, j, :], ps[:, 0:256], 0.125), reads=[Bps], writes=[Bmktok[j]])
            NJ = 4
            X = [dict() for _ in range(NJ)]
            for j in range(NJ):
                u = j
                x_ = X[j]
                smv = sm[u]
                x_.update(u=u, b=Bsm2[u], e1=smv[:, 0:4], spv=smv[:, 4:8], qsc=smv[:, 8:12], ksc=smv[:, 12:16], ktm=smv[:, 16:20],
                          dnt=smv[:, 20:24], rec=smv[:, 24:28], ff=smv[:, 28:32], msum=smv[:, 32:36], vsum=smv[:, 36:40],
                          rst=smv[:, 40:44], vtm=smv[:, 44:48], ngt=smv[:, 48:52], zi=gz[:, j, 0:4], zf=gz[:, j, 4:8],
                          af=afull[u], va=vaug[u], sc=scm[u], hv=hh[u], hcv=hc[u], hq=hsq[u])

            def st_gate1(j):
                x_ = X[j]; b_ = x_["b"]
                P.op("act", lambda e, x_=x_: e.activation(x_["e1"], x_["zf"], AF.Exp, scale=-1.0), reads=[Bgz[j]], writes=[b_[0]])

            def st_gate2(j):
                x_ = X[j]; b_ = x_["b"]
                P.op("act", lambda e, x_=x_: e.activation(x_["spv"], x_["e1"], AF.Ln, bias=1.0), reads=[b_[0]], writes=[b_[1]])

            def st_cum(j):
                x_ = X[j]; b_ = x_["b"]
                psc, Bpsc = ring()
                mm([(psc[:, 0:4], [(C["tri_f"], x_["spv"])]), (psc[:, 8:12], [(C["ones_f"], x_["spv"])])], [b_[1]] + BC, Bpsc)
                x_["psc"] = psc; x_["Bpsc"] = Bpsc

            def st_scales(j):
                x_ = X[j]; b_ = x_["b"]; psc = x_["psc"]; Bpsc = x_["Bpsc"]; u = x_["u"]
                P.op("act", lambda e, x_=x_, psc=psc: e.activation(x_["qsc"], psc[:, 0:4], AF.Exp, scale=-1.0), reads=[Bpsc], writes=[b_[2]])
                P.op("dve", lambda e, x_=x_, psc=psc: e.tensor_tensor(x_["ktm"], psc[:, 0:4], x_["zi"], ALU.add), reads=[Bpsc, Bgz[j]], writes=[b_[3]])
                P.op("act", lambda e, x_=x_, psc=psc: e.activation(x_["af"][0:64, 0:2], psc[0:64, 8:12:2], AF.Exp, scale=-1.0), reads=[Bpsc], writes=[Baf[u]])
                P.op("act", lambda e, x_=x_, psc=psc: e.activation(x_["af"][64:128, 0:2], psc[64:128, 9:12:2], AF.Exp, scale=-1.0), reads=[Bpsc], writes=[Baf[u]])

            def st_ksc(j):
                x_ = X[j]; b_ = x_["b"]
                P.op("act", lambda e, x_=x_: e.activation(x_["ksc"], x_["ktm"], AF.Exp), reads=[b_[3]], writes=[b_[4]])

            def st_vaug(j):
                x_ = X[j]; b_ = x_["b"]; u = x_["u"]; va = x_["va"]; ksc = x_["ksc"]
                P.op("dve", lambda e, va=va, j=j, ksc=ksc: e.tensor_tensor(va[:, :, 0:64], xvo[:, j, 0:256].rearrange("p (h d) -> p h d", d=64),
                                                                            ksc.unsqueeze(2).broadcast_to([128, 4, 64]), ALU.mult),
                     reads=[Bxvo[j], b_[4]], writes=[Bvaug[u]])
                P.op("dve", lambda e, va=va, ksc=ksc: e.tensor_copy(va[:, :, 64:66], ksc.unsqueeze(2).broadcast_to([128, 4, 2])), reads=[b_[4]], writes=[Bvaug[u]])
                P.op("act", lambda e, u=u, j=j: e.activation(sig[u], xvo[:, j, 256:512], AF.Sigmoid), reads=[Bxvo[j]], writes=[Bsig[u]])

            def st_scores(j):
                x_ = X[j]; u = x_["u"]; sc = x_["sc"]
                pss, Bpss = ring()
                groups = []
                for h in range(4):
                    c = h // 2
                    mkx = mkTe if h % 2 == 0 else mkTo
                    groups.append((pss[:, h * 128:(h + 1) * 128],
                                   [(mkx[:, c, j * 128:(j + 1) * 128], mqT[:, c, j * 128:(j + 1) * 128])]))
                mm(groups, BmkT + BmqT, Bpss)
                P.op("dve", lambda e, sc=sc, pss=pss: e.tensor_tensor(sc, pss.rearrange("p (h t) -> p h t", t=128),
                                                                      C["tri_b"].unsqueeze(1).broadcast_to([128, 4, 128]), ALU.mult),
                     reads=[Bpss] + BC, writes=[Bscm[u]])

            def st_state(j):
                x_ = X[j]; u = x_["u"]; sc = x_["sc"]; va = x_["va"]; af = x_["af"]
                cur = sb_i[0]
                psn, Bpsn = ring()
                groups = []
                for h in range(4):
                    c = h // 2
                    groups.append((psn[:, h * 66:(h + 1) * 66],
                                   [(sc[:, h, :], va[:, h, :]),
                                    (mqT[:, c, j * 128:(j + 1) * 128], (Sbe if h % 2 == 0 else Sbo)[cur][:, c, :])]))
                mm(groups, [Bscm[u], Bvaug[u], BSb[cur]] + BmqT, Bpsn)
                x_["psn"] = psn; x_["Bpsn"] = Bpsn
                psu, Bpsu = ring()
                va2 = va.rearrange("p h n -> p (h n)")
                mm([(psu[:, c * 132:(c + 1) * 132], [(mktok[:, j, c * 128:(c + 1) * 128], va2[:, c * 132:(c + 1) * 132])]) for c in range(2)],
                   [Bmktok[j], Bvaug[u]], Bpsu)
                psu3 = psu[:, 0:264].rearrange("p (c n) -> p c n", n=132)
                P.op("dve", lambda e, psu3=psu3: e.tensor_tensor(Stmp[0:64, :, :], psu3[0:64, :, 0:66], Sf[0:64, :, :], ALU.add), reads=[Bpsu, BSf], writes=[BStmp])
                P.op("dve", lambda e, psu3=psu3: e.tensor_tensor(Stmp[64:128, :, :], psu3[64:128, :, 66:132], Sf[64:128, :, :], ALU.add), reads=[Bpsu, BSf], writes=[BStmp])
                P.op("dve", lambda e, af=af: e.tensor_tensor(Sf, Stmp, af.unsqueeze(2).broadcast_to([128, 2, 66]), ALU.mult), reads=[BStmp, Baf[u]], writes=[BSf])
                nxt = 1 - cur
                P.op("pool", lambda e, nxt=nxt: e.tensor_copy(Sbe[nxt][0:64, :, :], Sf[0:64, :, :]), reads=[BSf], writes=[BSb[nxt]])
                P.op("pool", lambda e, nxt=nxt: e.tensor_copy(Sbo[nxt][64:128, :, :], Sf[64:128, :, :]), reads=[BSf], writes=[BSb[nxt]])
                sb_i[0] = nxt

            def mk_post():
                steps = []

                def s1(j):
                    x_ = X[j]; b_ = x_["b"]
                    psn3 = x_["psn"][:, 0:264].rearrange("p (h n) -> p h n", n=66)
                    x_["psn3"] = psn3
                    P.op("dve", lambda e, x_=x_, psn3=psn3: e.tensor_tensor(x_["dnt"].unsqueeze(2), psn3[:, :, 64:65], x_["qsc"].unsqueeze(2), ALU.mult), reads=[x_["Bpsn"], b_[2]], writes=[b_[5]])
                steps.append(s1)
                steps.append(lambda j: P.op("dve", lambda e, x_=X[j]: e.tensor_scalar_mul(x_["ngt"], x_["dnt"], -1.0), reads=[X[j]["b"][5]], writes=[X[j]["b"][6]]))
                steps.append(lambda j: P.op("dve", lambda e, x_=X[j]: e.tensor_tensor(x_["dnt"], x_["dnt"], x_["ngt"], ALU.max), reads=[X[j]["b"][6]], writes=[X[j]["b"][5]]))
                steps.append(lambda j: P.op("dve", lambda e, x_=X[j]: e.tensor_scalar_max(x_["dnt"], x_["dnt"], 1.0), reads=[X[j]["b"][5]], writes=[X[j]["b"][5]]))
                steps.append(lambda j: P.op("dve", lambda e, x_=X[j]: e.reciprocal(x_["rec"], x_["dnt"]), reads=[X[j]["b"][5]], writes=[X[j]["b"][6]]))
                steps.append(lambda j: P.op("dve", lambda e, x_=X[j]: e.tensor_tensor(x_["ff"], x_["rec"], x_["qsc"], ALU.mult), reads=[X[j]["b"][6], X[j]["b"][2]], writes=[X[j]["b"][7]]))
                steps.append(lambda j: P.op("dve", lambda e, x_=X[j]: e.tensor_tensor(x_["hv"], x_["psn3"][:, :, 0:64], x_["ff"].unsqueeze(2).broadcast_to([128, 4, 64]), ALU.mult),
                                            reads=[X[j]["Bpsn"], X[j]["b"][7]], writes=[Bhh[X[j]["u"]]]))
                steps.append(lambda j: P.op("dve", lambda e, x_=X[j]: e.reduce_sum(x_["msum"], x_["hv"], AX.X), reads=[Bhh[X[j]["u"]]], writes=[X[j]["b"][8]]))
                steps.append(lambda j: P.op("dve", lambda e, x_=X[j]: e.tensor_scalar_mul(x_["msum"], x_["msum"], 1.0 / 64.0), reads=[X[j]["b"][8]], writes=[X[j]["b"][8]]))
                steps.append(lambda j: P.op("dve", lambda e, x_=X[j]: e.tensor_tensor(x_["hcv"], x_["hv"], x_["msum"].unsqueeze(2).broadcast_to([128, 4, 64]), ALU.subtract),
                                            reads=[Bhh[X[j]["u"]], X[j]["b"][8]], writes=[Bhc[X[j]["u"]]]))
                steps.append(lambda j: P.op("pool", lambda e, x_=X[j]: e.tensor_tensor(x_["hq"], x_["hcv"], x_["hcv"], ALU.mult), reads=[Bhc[X[j]["u"]]], writes=[Bhsq[X[j]["u"]]]))
                steps.append(lambda j: P.op("dve", lambda e, x_=X[j]: e.reduce_sum(x_["vsum"], x_["hq"], AX.X), reads=[Bhsq[X[j]["u"]]], writes=[X[j]["b"][9]]))
                steps.append(lambda j: P.op("dve", lambda e, x_=X[j]: e.tensor_scalar(x_["vtm"], x_["vsum"], 1.0 / 64.0, LN_EPS, ALU.mult, ALU.add), reads=[X[j]["b"][9]], writes=[X[j]["b"][10]]))
                steps.append(lambda j: P.op("pool", lambda e, x_=X[j]: e.tensor_tensor(x_["rst"], x_["vtm"], C["nh"][:, 0:4], ALU.pow), reads=[X[j]["b"][10]] + BC, writes=[X[j]["b"][11]]))
                steps.append(lambda j: P.op("dve", lambda e, x_=X[j]: e.tensor_tensor(x_["hcv"], x_["hcv"], x_["rst"].unsqueeze(2).broadcast_to([128, 4, 64]), ALU.mult),
                                            reads=[X[j]["b"][11]], writes=[Bhc[X[j]["u"]]]))
                steps.append(lambda j: P.op("pool", lambda e, x_=X[j]: e.tensor_tensor(x_["hcv"].rearrange("p h d -> p (h d)"), x_["hcv"].rearrange("p h d -> p (h d)"), gmh_t, ALU.mult),
                                            reads=[Bsm], writes=[Bhc[X[j]["u"]]]))
                steps.append(lambda j: P.op("pool", lambda e, x_=X[j]: e.tensor_tensor(mtok[x_["u"]], x_["hcv"].rearrange("p h d -> p (h d)"), sig[x_["u"]], ALU.mult),
                                            reads=[Bhc[X[j]["u"]], Bsig[X[j]["u"]]], writes=[Bmtok[X[j]["u"]]]))

                def s_tr(j):
                    u = X[j]["u"]
                    pst, Bpst = ring()
                    pstb = pst.bitcast(BF16)
                    tr([(pstb[:, c * 128:(c + 1) * 128], mtok[u][:, c * 128:(c + 1) * 128]) for c in range(2)], [Bmtok[u]] + BC, Bpst, ident)
                    P.op("act", lambda e, j=j, pstb=pstb: e.copy(MTs[s2][:, :, j * 128:(j + 1) * 128], pstb[:, 0:256].rearrange("p (c t) -> p c t", t=128)),
                         reads=[Bpst], writes=[BMT[s2]])
                steps.append(s_tr)
                return steps

            for st in (st_gate1, st_gate2, st_cum, st_scales, st_ksc, st_vaug, st_scores):
                for j in range(NJ):
                    st(j)
            for j in range(NJ):
                st_state(j)
            for st in mk_post():
                for j in range(NJ):
                    st(j)
            if int(os.environ.get('ASTOP', '99')) == 15:
                return
            P.dma("sp", QN[:, :, t0:t0 + 512].rearrange("h p t -> p h t"), QNs[s2], BQN[s2], reads=[BQN[s2]], writes=[B_QN], acc=True)
            P.dma("sp", QR[:, :, t0:t0 + 512].rearrange("h p t -> p h t"), QRs[s2][0:64, :, :], BQR[s2], reads=[BQR[s2]], writes=[B_QR], acc=True)
            P.dma("sp", KT[:, :, t0:t0 + 512].rearrange("h p t -> p h t"), KTs[s2], BKT[s2], reads=[BKT[s2]], writes=[B_KT], acc=True)
            P.dma("sp", KR[:, t0:t0 + 512], KRs[s2][0:64, :], BKR[s2], reads=[BKR[s2]], writes=[B_KR], acc=True)
            P.dma("sp", VV[t0:t0 + 512, :].rearrange("(j p) c -> p j c", p=128), Vs[s2], BV[s2], reads=[BV[s2]], writes=[B_VV], acc=True)
            P.dma("sp", MIXT[512:768, t0:t0 + 512].rearrange("(c p) t -> p c t", p=128), PTs[s2], BPT[s2], reads=[BPT[s2]], writes=[B_MIX], acc=True)
            P.dma("sp", MIXT[768:1024, t0:t0 + 512].rearrange("(c p) t -> p c t", p=128), MTs[s2], BMT[s2], reads=[BMT[s2]], writes=[B_MIX], acc=True)

    def phaseB(l):
        reset_phase()
        NKT = S // 128
        Kt = [carve(S, BF16) for _ in range(2)]; Vt = [c3(NKT, 128, BF16) for _ in range(2)]
        BK = P.bufs(2); BVt = P.bufs(2)
        Kr = carve(S, BF16); BKr = P.buf()
        qn = [carve(512, BF16) for _ in range(2)]; qr = [carve(512, BF16) for _ in range(2)]; Bq = P.bufs(2)
        NP = int(os.environ.get('B_NP', '8'))
        Pt = [carve(512, BF16) for _ in range(NP)]; BPt = P.bufs(NP)
        rec = carve(512); Brec = P.buf()
        Pacc = [[carve(512) for _ in range(2)] for _ in range(2)]
        BPacc = [P.bufs(2) for _ in range(2)]
        ost = [carve(512, BF16) for _ in range(2)]; Bost = P.bufs(2)
        P.dma("sp", Kr[0:64, :], KR, BKr, reads=[B_KR], writes=[BKr])
        bcfg = [int(v) for v in os.environ.get("B_BANKS", "0,1,2,3,4,5,6,7").split(",")]
        set_ring(bcfg[4:8])
        pi = 0
        qi = 0
        if "B" in os.environ.get("SCHED_PH", "0ACD"):
            P.begin()
        def loadkv(h_):
            hs_ = h_ % 2
            P.dma("sp", Kt[hs_], KT[h_], BK[hs_], reads=[B_KT], writes=[BK[hs_]])
            P.dma("sp", Vt[hs_], VV[:, h_ * 128:(h_ + 1) * 128].rearrange("(n p) c -> p n c", p=128), BVt[hs_], reads=[B_VV], writes=[BVt[hs_]])
        loadkv(0)
        for h in range(4):
            hs = h % 2
            if h + 1 < 4:
                loadkv(h + 1)
            def loadq(h_, qb_, qs_):
                q0_ = qb_ * 512
                P.dma("sp", qn[qs_], QN[h_, :, q0_:q0_ + 512], Bq[qs_], reads=[B_QN], writes=[Bq[qs_]])
                P.dma("sp", qr[qs_][0:64, :], QR[h_, :, q0_:q0_ + 512], Bq[qs_], reads=[B_QR], writes=[Bq[qs_]], acc=True)
            NQB = S // 512
            if h == 0:
                loadq(0, 0, 0)
            for qb in range(NQB):
                q0 = qb * 512
                qs = qi % 2
                qi += 1
                if qb + 1 < NQB:
                    loadq(h, qb + 1, qi % 2)
                elif h + 1 < 4:
                    loadq(h + 1, 0, qi % 2)
                Obank = banks[bcfg[qs]]; BO = Bbank[bcfg[qs]]
                Dbank = banks[bcfg[2 + qs]]; BD = Bbank[bcfg[2 + qs]]
                nkt = 4 * qb + 4
                LA = int(os.environ.get('B_LA', '4'))
                DEN_PE = os.environ.get('B_DEN', 'pe') == 'pe'
                for ch in range(2):
                    P.op("pool", lambda e, qs=qs, ch=ch: e.memset(Pacc[qs][ch], 0.0), writes=[BPacc[qs][ch]])
                info = {}

                def stage1(kt):
                    nonlocal pi
                    jd = kt - 4 * qb
                    c0 = 128 * jd if jd > 0 else 0
                    ps, Bps = ring()
                    mm([(ps[:, c0:512], [(Kt[hs][:, kt * 128:(kt + 1) * 128], qn[qs][:, c0:512]),
                                         (Kr[0:64, kt * 128:(kt + 1) * 128], qr[qs][0:64, c0:512])])],
                       [BK[hs], BKr, Bq[qs]], Bps)
                    p_ = pi % NP
                    pi += 1
                    P.op("act", lambda e, p_=p_, ps=ps, c0=c0: e.activation(Pt[p_][:, c0:512], ps[:, c0:512], AF.Exp), reads=[Bps], writes=[BPt[p_]])
                    if jd >= 0:
                        P.op("pool", lambda e, p_=p_, c0=c0: e.tensor_tensor(Pt[p_][:, c0:c0 + 128], Pt[p_][:, c0:c0 + 128], C["tri_b"], ALU.mult),
                             reads=BC, writes=[BPt[p_]])
                    info[kt] = (p_, c0)

                def stage2(kt):
                    p_, c0 = info.pop(kt)
                    first = (kt == 0); last = (kt == nkt - 1)

                    def fnO(e, p_=p_, c0=c0, first=first, last=last, kt=kt, Obank=Obank, hs=hs):
                        return e.matmul(Obank[:, c0:512], Vt[hs][:, kt, :], Pt[p_][:, c0:512], start=first, stop=last)
                    P.op("pe", fnO, reads=[BVt[hs], BPt[p_]], writes=[BO])

                    ch = kt % 2
                    if DEN_PE:
                        def fnD(e, p_=p_, c0=c0, first=first, last=last, Dbank=Dbank):
                            return e.matmul(Dbank[:, c0:512], C["ones_b"], Pt[p_][:, c0:512], start=first, stop=last)
                        P.op("pe", fnD, reads=[BPt[p_]] + BC, writes=[BD])
                    else:
                        P.op("dve", lambda e, p_=p_, c0=c0, ch=ch, qs=qs: e.tensor_tensor(Pacc[qs][ch][:, c0:512], Pacc[qs][ch][:, c0:512], Pt[p_][:, c0:512], ALU.add),
                             reads=[BPt[p_]], writes=[BPacc[qs][ch]])

                for kt in range(nkt):
                    stage1(kt)
                    if kt >= LA:
                        stage2(kt - LA)
                for kt in range(max(0, nkt - LA), nkt):
                    stage2(kt)
                def fnD(e, Dbank=Dbank, qs=qs):
                    e.matmul(Dbank, C["ones_f"], Pacc[qs][0], start=True, stop=False)
                    return e.matmul(Dbank, C["ones_f"], Pacc[qs][1], start=False, stop=True)
                if not DEN_PE:
                    P.op("pe", fnD, reads=BPacc[qs] + BC, writes=[BD])
                P.op("dve", lambda e, Dbank=Dbank: e.reciprocal(rec, Dbank), reads=[BD], writes=[Brec])
                os_ = qi % 2
                P.op("dve", lambda e, os_=os_, Obank=Obank: e.tensor_tensor(ost[os_], Obank, rec, ALU.mult), reads=[BO, Brec], writes=[Bost[os_]])
                P.dma("sp", MIXT[h * 128:(h + 1) * 128, q0:q0 + 512], ost[os_], Bost[os_], reads=[Bost[os_]], writes=[B_MIX], acc=True)

    def phaseC1(l, hin, Bhin, hmid, Bhmid):
        reset_phase()
        set_ring(range(8))
        wo = c3(8, 1024, BF16)
        Bw = P.buf("wC1")
        for k in range(8):
            P.dma("pool", wo[:, k, :], w_out[l, k * 128:(k + 1) * 128, :], Bw, writes=[Bw], acc=True)
        g1 = carve(1024); b1 = carve(1024)
        Bsm = P.buf("smallC1")
        P.dma("sp", g1, ln1_g[l].partition_broadcast(128), Bsm, writes=[Bsm], acc=True)
        P.dma("sp", b1, ln1_b[l].partition_broadcast(128), Bsm, writes=[Bsm], acc=True)
        mx = [c3(8, 512, BF16) for _ in range(2)]; Bmx = P.bufs(2)
        hx = [c3(4, 1024) for _ in range(2)]; Bhx = P.bufs(2)
        r1 = [carve(1024) for _ in range(2)]; Br1 = P.bufs(2)
        h1 = [carve(1024) for _ in range(2)]; Bh1 = P.bufs(2)
        tmps = [ln_tmp() for _ in range(2)]

        def load(ti):
            s = ti % 2
            t0 = ti * 512
            P.dma("sp", mx[s], MIXT[:, t0:t0 + 512].rearrange("(k p) t -> p k t", p=128), Bmx[s], reads=[B_MIX], writes=[Bmx[s]])
            P.dma("sp", hx[s], hin[t0:t0 + 512, :].rearrange("(j p) d -> p j d", p=128), Bhx[s], reads=[Bhin], writes=[Bhx[s]])

        if "C" in os.environ.get("SCHED_PH", "0ACD"):
            P.begin()
        load(0)
        n = 0
        for ti in range(NT5):
            s = ti % 2
            t0 = ti * 512
            if ti + 1 < NT5:
                load(ti + 1)
            for j in range(4):
                u = n % 2
                n += 1
                for hf in range(2):
                    ps, Bps = ring()
                    mm([(ps, [(mx[s][:, k, j * 128:(j + 1) * 128], wo[:, k, hf * 512:(hf + 1) * 512]) for k in range(8)])], [Bw, Bmx[s]], Bps)
                    P.op("dve", lambda e, j=j, hf=hf, ps=ps, s=s, u=u: e.scalar_tensor_tensor(r1[u][:, hf * 512:(hf + 1) * 512], hx[s][:, j, hf * 512:(hf + 1) * 512], ALPHA, ps, ALU.mult, ALU.add),
                         reads=[Bhx[s], Bps], writes=[Br1[u]])
                ln_tile(r1[u], Br1[u], h1[u], Bh1[u], g1, b1, Bsm, tmps[u])
                P.dma("sp", hmid[t0 + j * 128:t0 + (j + 1) * 128, :], h1[u], Bh1[u], reads=[Bh1[u]], writes=[Bhmid], acc=True)

    def phaseC2(l, hin, Bhin, hout, Bhout):
        reset_phase()
        set_ring(range(8))
        ident = C["ident"]
        TW = 256
        wu = c3(8, 2 * D_FF, BF16); wd = c3(NFC, 1024, BF16)
        Bw = P.buf("wC2")
        for k in range(8):
            for hf in range(2):
                P.dma("pool", wu[:, k, hf * D_FF:(hf + 1) * D_FF], w_up[l, k * 128:(k + 1) * 128, hf * D_FF:(hf + 1) * D_FF], Bw, writes=[Bw], acc=True)
        for c in range(NFC):
            P.dma("pool", wd[:, c, :], w_down[l, c * 128:(c + 1) * 128, :], Bw, writes=[Bw], acc=True)
        g2 = carve(1024); b2 = carve(1024)
        wf = c3(NFC, 3); bf_ = carve(NFC)
        Bsm = P.buf("smallC2")
        P.dma("sp", g2, ln2_g[l].partition_broadcast(128), Bsm, writes=[Bsm], acc=True)
        P.dma("sp", b2, ln2_b[l].partition_broadcast(128), Bsm, writes=[Bsm], acc=True)
        P.dma("sp", wf, wfc[l], Bsm, writes=[Bsm], acc=True)
        P.dma("sp", bf_, bfc[l], Bsm, writes=[Bsm], acc=True)
        hx = [c3(2, 1024) for _ in range(2)]; Bhx = P.bufs(2)
        h1b = carve(1024, BF16); Bh1b = P.buf()
        h1T = c3(8, TW, BF16); Bh1T = P.bufs(2)
        uT = [c3(NFC, TW, BF16) for _ in range(2)]; BuT = [P.bufs(NFC) for _ in range(2)]
        carry = c3(NFC, 2); Bcar = P.bufs(NFC)
        NG = 3
        gb_ = [carve(TW + 2) for _ in range(NG)]; Bgb = P.bufs(NG)
        tt = [carve(TW) for _ in range(NG)]; Btt = P.bufs(NG)
        ss = [carve(TW) for _ in range(NG)]; Bss = P.bufs(NG)
        r2 = [carve(1024) for _ in range(2)]; Br2 = P.bufs(2)
        tmpB = [ln_tmp() for _ in range(2)]
        P.op("pool", lambda e: e.memset(carry, 0.0), writes=Bcar)
        NTC = S // TW
        gi = [0]

        def load(ti):
            s = ti % 2
            t0 = ti * TW
            P.dma("sp", hx[s], hin[t0:t0 + TW, :].rearrange("(j p) d -> p j d", p=128), Bhx[s], reads=[Bhin], writes=[Bhx[s]])

        def up(ti):
            s = ti % 2
            us = ti % 2
            stA = {}
            for j in range(2):
                P.op("act", lambda e, j=j, s=s: e.copy(h1b, hx[s][:, j, :]), reads=[Bhx[s]], writes=[Bh1b])
                ps, Bps = ring()
                psb = ps.bitcast(BF16)
                tr([(psb[:, k * 128:(k + 1) * 128], h1b[:, k * 128:(k + 1) * 128]) for k in range(8)], [Bh1b] + BC, Bps, ident)
                P.op("act", lambda e, j=j, psb=psb: e.copy(h1T[:, :, j * 128:(j + 1) * 128], psb.rearrange("p (k t) -> p k t", t=128)),
                     reads=[Bps], writes=[Bh1T[j]])

            def s_mm(c):
                ps, Bps = ring()
                mm([(ps[:, 0:TW], [(wu[:, k, c * 128:(c + 1) * 128], h1T[:, k, :]) for k in range(8)]),
                    (ps[:, TW:2 * TW], [(wu[:, k, D_FF + c * 128:D_FF + (c + 1) * 128], h1T[:, k, :]) for k in range(8)])],
                   [Bw] + Bh1T, Bps)
                g_ = gi[0] % NG
                gi[0] += 1
                stA[c] = (ps, Bps, g_)

            def s_copy(c):
                ps, Bps, g_ = stA[c]
                gbv = gb_[g_]
                P.op("pool", lambda e, gbv=gbv, c=c: e.tensor_copy(gbv[:, 0:2], carry[:, c, :]), reads=[Bcar[c]], writes=[Bgb[g_]])
                P.op("act", lambda e, gbv=gbv, ps=ps: e.copy(gbv[:, 2:TW + 2], ps[:, TW:2 * TW]), reads=[Bps], writes=[Bgb[g_]])
                P.op("pool", lambda e, gbv=gbv, c=c: e.tensor_copy(carry[:, c, :], gbv[:, TW:TW + 2]), reads=[Bgb[g_]], writes=[Bcar[c]])

            def s_conv(c):
                ps, Bps, g_ = stA[c]
                gbv = gb_[g_]; tv = tt[g_]
                P.op("dve", lambda e, tv=tv, gbv=gbv, c=c: e.tensor_scalar(tv, gbv[:, 0:TW], wf[:, c, 0:1], bf_[:, c:c + 1], ALU.mult, ALU.add),
                     reads=[Bgb[g_], Bsm], writes=[Btt[g_]])
                for k in (1, 2):
                    P.op("dve", lambda e, tv=tv, gbv=gbv, c=c, k=k: e.scalar_tensor_tensor(tv, gbv[:, k:k + TW], wf[:, c, k:k + 1], tv, ALU.mult, ALU.add),
                         reads=[Bgb[g_], Bsm], writes=[Btt[g_]])

            def s_silu(c):
                ps, Bps, g_ = stA[c]
                P.op("act", lambda e, sv=ss[g_], tv=tt[g_]: e.activation(sv, tv, AF.Silu), reads=[Btt[g_]], writes=[Bss[g_]])

            def s_mult(c):
                ps, Bps, g_ = stA[c]
                P.op("dve", lambda e, sv=ss[g_], ps=ps, c=c: e.tensor_tensor(uT[us][:, c, :], ps[:, 0:TW], sv, ALU.mult), reads=[Bps, Bss[g_]], writes=[BuT[us][c]])

            for step in range(NFC + 3):
                if step < NFC:
                    s_mm(step)
                    s_copy(step)
                if 0 <= step - 1 < NFC:
                    s_conv(step - 1)
                if 0 <= step - 2 < NFC:
                    s_silu(step - 2)
                if 0 <= step - 3 < NFC:
                    s_mult(step - 3)

        def down(ti):
            us = ti % 2
            t0 = ti * TW
            for j in range(2):
                P.dma("sp", r2[j], hin[t0 + j * 128:t0 + (j + 1) * 128, :], Br2[j], reads=[Bhin], writes=[Br2[j]])
                for hf in range(2):
                    ps, Bps = ring()
                    mm([(ps, [(uT[us][:, c, j * 128:(j + 1) * 128], wd[:, c, hf * 512:(hf + 1) * 512]) for c in range(NFC)])], [Bw] + BuT[us], Bps)
                    P.op("dve", lambda e, j=j, hf=hf, ps=ps: e.scalar_tensor_tensor(r2[j][:, hf * 512:(hf + 1) * 512], r2[j][:, hf * 512:(hf + 1) * 512], ALPHA, ps, ALU.mult, ALU.add),
                         reads=[Bps], writes=[Br2[j]])
                ln_tile(r2[j], Br2[j], r2[j], Br2[j], g2, b2, Bsm, tmpB[j])
                P.dma("sp", hout[t0 + j * 128:t0 + (j + 1) * 128, :], r2[j], Br2[j], reads=[Br2[j]], writes=[Bhout], acc=True)

        if "D" in os.environ.get("SCHED_PH", "0ACD"):
            P.begin()
        load(0)
        for ti in range(NTC):
            if ti + 1 < NTC:
                load(ti + 1)
            up(ti)
            if ti >= 1:
                down(ti - 1)
        down(NTC - 1)

    if "0" in phases:
        phase0(hA, B_hA)
    B_y = P.buf("y")
    for l in range(L):
        if "A" in phases:
            phaseA(l, hA, B_hA)
        if "B" in phases:
            phaseB(l)
        if "C" in phases:
            phaseC1(l, hA, B_hA, hB, B_hB)
        if "D" in phases:
            if l == L - 1:
                phaseC2(l, hB, B_hB, y, B_y)
            else:
                phaseC2(l, hB, B_hB, hA, B_hA)
    P.barrier()
    P.emit()
    return nc


def prep_weights(inp, S, L):
    f = np.float32
    g = lambda k: np.asarray(inp[k], dtype=f)
    d = {}
    d["ln0_g"] = g("ln0_g"); d["ln0_b"] = g("ln0_b")
    w_in = g("w_in")[:L]
    kr = w_in[:, :, 640:704]
    krsw = np.concatenate([kr[:, :, 32:64], kr[:, :, 0:32]], axis=-1)
    d["w_in_e"] = np.ascontiguousarray(np.concatenate([w_in, krsw], axis=-1))
    w_uq = g("w_uq")[:L].reshape(L, 384, 4, 192)
    nope = w_uq[..., 0:128]; rope = w_uq[..., 128:192]
    ropesw = np.concatenate([rope[..., 32:64], rope[..., 0:32]], axis=-1)
    d["w_uq_e"] = np.ascontiguousarray(np.concatenate([nope, rope, ropesw], axis=-1).reshape(L, 384, 1024))
    w_ukv = g("w_ukv")[:L].reshape(L, 256, 4, 256)
    d["w_k"] = np.ascontiguousarray(w_ukv[..., 0:128].reshape(L, 256, 512))
    d["w_v"] = np.ascontiguousarray(w_ukv[..., 128:256].reshape(L, 256, 512))

    def blockdiag(w):
        o = np.zeros((L, 128, 2, 128), f)
        for h in range(4):
            c, a = h // 2, (h % 2) * 64
            o[:, a:a + 64, c, a:a + 64] = w[:, h]
        return o
    d["bdp"] = blockdiag(g("w_pool")[:L]); d["bdq"] = blockdiag(g("w_mq")[:L]); d["bdk"] = blockdiag(g("w_mk")[:L])

    def pk(v, n):
        return np.ascontiguousarray(v.reshape(L, n, 128).transpose(0, 2, 1))
    d["gq"] = pk(g("g_qn")[:L], 3); d["gkv"] = pk(g("g_kvn")[:L], 2); d["spool"] = pk(g("s_pool")[:L], 2)
    d["wmc"] = np.ascontiguousarray(g("w_mconv")[:L].reshape(L, 4, 2, 128).transpose(0, 3, 2, 1))
    d["bmc"] = pk(g("b_mconv")[:L], 2)
    d["bif"] = np.ascontiguousarray(np.concatenate([g("b_i")[:L], g("b_f")[:L]], axis=-1))
    d["gmh"] = g("g_mh")[:L]
    d["w_out"] = g("w_out")[:L]; d["ln1_g"] = g("ln1_g")[:L]; d["ln1_b"] = g("ln1_b")[:L]
    d["w_up"] = g("w_up")[:L]
    d["wfc"] = np.ascontiguousarray(g("w_fconv")[:L].reshape(L, 3, NFC, 128).transpose(0, 3, 2, 1))
    d["bfc"] = pk(g("b_fconv")[:L], NFC)
    d["w_down"] = g("w_down")[:L]; d["ln2_g"] = g("ln2_g")[:L]; d["ln2_b"] = g("ln2_b")[:L]
    d["c_ident"] = np.eye(128, dtype=f)
    d["c_tri"] = np.triu(np.ones((128, 128), f))
    inv = (1.0 / (10000.0 ** (np.arange(0, 64, 2, dtype=f) / f(64)))).astype(f)
    ang = np.arange(S, dtype=f)[:, None] * inv[None, :]
    cos = np.cos(ang).astype(f).T; sin = np.sin(ang).astype(f).T
    d["c_cs"] = np.ascontiguousarray(np.concatenate([cos, cos], axis=0))
    d["c_sn"] = np.ascontiguousarray(np.concatenate([-sin, sin], axis=0))
    pinv = np.zeros((2, 128, 2, 512), f)
    wins = (2, 4, 8, 16)
    t = np.arange(512)
    for gi in range(4):
        c, a = gi // 2, (gi % 2) * 64
        pinv[0, a:a + 64, c, :] = (1.0 / np.minimum(t + 1, wins[gi]).astype(f))[None, :]
        pinv[1, a:a + 64, c, :] = f(1.0) / f(wins[gi])
    d["c_pinv"] = pinv
    return d


_CACHE = {}


def kernel(**inputs):
    x = np.asarray(inputs["x"], dtype=np.float32)
    B, S, D = x.shape
    L = DEPTH_FULL
    key = (S, L)
    if key not in _CACHE:
        _CACHE[key] = build(S, L)
    nc = _CACHE[key]
    w = prep_weights(inputs, S, L)
    in_maps = []
    for b in range(B):
        m = dict(w)
        m["x"] = np.ascontiguousarray(x[b])
        in_maps.append(m)
    res = run_bass_kernel_spmd(nc, in_maps, core_ids=list(range(B)))
    return np.stack([np.asarray(r["y"], dtype=np.float32) for r in res.results], axis=0)
```

```python
import contextlib
import os
import numpy as np
import concourse.bass as bass
import concourse.mybir as mybir
from concourse.bass_utils import run_bass_kernel_spmd

F32 = mybir.dt.float32
BF16 = mybir.dt.bfloat16
ALU = mybir.AluOpType
AF = mybir.ActivationFunctionType
AX = mybir.AxisListType

ENGS = ("pe", "act", "dve", "pool", "sp")

D_MODEL = 1024
DEPTH_FULL = 4
Q_LORA, KV_LORA, MLA_ROPE = 384, 256, 64
D_IN = 1736
D_FF = 2816
NFC = D_FF // 128
LN_EPS = 1e-5
RMS_EPS = 1e-6
ALPHA = float((2 * DEPTH_FULL) ** 0.25)
QSCALE = float((128 + 64) ** -0.5)
WINE = 1800


class Buf:
    __slots__ = ("name", "w", "r", "dsem")

    def __init__(self, name):
        self.name = name
        self.w = {}
        self.r = {}
        self.dsem = None


class Prog:
    def __init__(self, nc):
        self.nc = nc
        self.ops = {e: [] for e in ENGS}
        self.tot = {e: 0 for e in ENGS}
        self.seen = {e: {} for e in ENGS}
        self.dsems = []
        self.free_dsems = []
        self.epoch = 0
        self.nbuf = 0

    def buf(self, name=None):
        self.nbuf += 1
        return Buf(name or "b%d" % self.nbuf)

    def bufs(self, n):
        return [self.buf() for _ in range(n)]

    def _need(self, eng, deps):
        out = {}
        seen = self.seen[eng]
        for (k, v) in deps:
            if seen.get(k, 0) >= v:
                continue
            if out.get(k, 0) < v:
                out[k] = v
        for k, v in out.items():
            seen[k] = v
        return out

    def _track(self, eng, me, reads, writes, acc):
        deps = []
        for b in reads:
            deps += list(b.w.items())
        for b in writes:
            deps += list(b.r.items())
            if not acc:
                deps += list(b.w.items())
        waits = self._need(eng, deps)
        k, v = me
        for b in reads:
            if b.r.get(k, 0) < v:
                b.r[k] = v
        for b in writes:
            if acc:
                if b.w.get(k, 0) < v:
                    b.w[k] = v
            else:
                b.w = {k: v}
                b.r = {}
        return waits

    def begin(self):
        if os.environ.get("SCHED", "1") == "1":
            self.rec = []

    def end(self):
        rec = self.rec
        self.rec = None
        if not rec:
            return
        n = len(rec)
        lastw, readers = {}, {}
        preds = [set() for _ in range(n)]
        for i, r in enumerate(rec):
            reads, writes, acc = r[2], r[3], r[4]
            for b in reads:
                preds[i].update(lastw.get(id(b), ()))
            for b in writes:
                preds[i].update(readers.get(id(b), ()))
                preds[i].update(lastw.get(id(b), ()))
            for b in reads:
                readers.setdefault(id(b), []).append(i)
            for b in writes:
                if acc:
                    lastw.setdefault(id(b), []).append(i)
                else:
                    lastw[id(b)] = [i]
                    readers[id(b)] = []
            preds[i].discard(i)
        succs = [[] for _ in range(n)]
        indeg = [0] * n
        for i in range(n):
            indeg[i] = len(preds[i])
            for p in preds[i]:
                succs[p].append(i)
        rt = [0.0] * n
        free = {e: 0.0 for e in ENGS}
        ready = [i for i in range(n) if indeg[i] == 0]
        order = []
        HOP = 0.25
        while ready:
            best, bs = None, None
            for i in ready:
                st = max(rt[i], free[rec[i][1]])
                key = (st, i)
                if bs is None or key < bs:
                    best, bs = i, key
            ready.remove(best)
            r = rec[best]
            st = bs[0]
            busy, lat = r[5]
            free[r[1]] = st + busy
            fin = st + lat
            order.append(best)
            for j in succs[best]:
                if rt[j] < fin + HOP:
                    rt[j] = fin + HOP
                indeg[j] -= 1
                if indeg[j] == 0:
                    ready.append(j)
        assert len(order) == n
        for i in order:
            r = rec[i]
            if r[0] == "op":
                self.op(r[1], r[6], r[2], r[3])
            else:
                self.dma(r[1], r[6], r[7], r[8], r[2], r[3], r[4])

    rec = None

    def op(self, eng, fn, reads=(), writes=(), cost=None):
        if self.rec is not None:
            c = cost if cost is not None else {"pe": 0.3, "act": 0.5, "dve": 0.5, "pool": 0.7, "sp": 0.5}[eng]
            self.rec.append(("op", eng, tuple(reads), tuple(writes), False, (c, c), fn))
            return
        self.tot[eng] += 1
        me = (eng, self.tot[eng])
        waits = self._track(eng, me, reads, writes, False)
        self.ops[eng].append((waits, fn, (eng, 1)))
        return me

    def dma(self, q, out_ap, in_ap, sb, reads=(), writes=(), acc=False):
        if self.rec is not None:
            self.rec.append(("dma", q, tuple(reads), tuple(writes), acc, (0.1, 2.5), out_ap, in_ap, sb))
            return
        if sb.dsem is None or sb.dsem[2] != self.epoch:
            if self.free_dsems:
                kk = self.free_dsems.pop()
            else:
                kk = ("d", len(self.dsems))
                self.dsems.append(kk)
                self.tot[kk] = 0
            sb.dsem = (kk[0], kk[1], self.epoch)
        key = (sb.dsem[0], sb.dsem[1])
        self.tot[key] += 16
        me = (key, self.tot[key])
        waits = self._track(q, me, reads, writes, acc)

        def fn(e, out_ap=out_ap, in_ap=in_ap):
            return e.dma_start(out=out_ap, in_=in_ap)
        self.ops[q].append((waits, fn, (key, 16)))
        return me

    def barrier(self):
        if self.rec is not None:
            self.end()
        allk = [(k, v) for k, v in self.tot.items() if v > 0]
        for e in ENGS:
            waits = self._need(e, allk)
            if waits:
                self.ops[e].append((waits, None, None))
        self.epoch += 1
        self.free_dsems = list(self.dsems)

    def emit(self):
        nc = self.nc
        with contextlib.ExitStack() as st:
            sems = {}
            for e in ENGS:
                sems[e] = st.enter_context(nc.semaphore("s_" + e))
            for k in self.dsems:
                sems[k] = st.enter_context(nc.semaphore("d%d" % k[1]))
            block = st.enter_context(nc.Block())
            handles = {"pe": block.tensor, "act": block.scalar, "dve": block.vector,
                       "pool": block.gpsimd, "sp": block.sync}

            def mk(ename):
                def body(eng):
                    for (waits, fn, inc) in self.ops[ename]:
                        for k, v in waits.items():
                            eng.wait_ge(sems[k], v)
                        if fn is None:
                            continue
                        ins = fn(eng)
                        ins.then_inc(sems[inc[0]], inc[1])
                return body
            for e in ENGS:
                handles[e](mk(e))


def build(S, DEPTH, dbg=False, phases="0ABCD"):
    nc = bass.Bass("TRN2", target_bir_lowering=False)
    P = Prog(nc)
    NT = S // 128
    NT5 = S // 512
    L = DEPTH

    def din(name, shape):
        return nc.dram_tensor(name, list(shape), F32, kind="ExternalInput").ap()

    def dscr(name, shape, dt):
        kind = "ExternalOutput" if dbg else "Internal"
        return nc.dram_tensor(name, list(shape), dt, kind=kind).ap()

    x = din("x", [S, 1024])
    ln0_g = din("ln0_g", [1024]); ln0_b = din("ln0_b", [1024])
    w_in = din("w_in_e", [L, 1024, WINE])
    w_uq = din("w_uq_e", [L, 384, 1024])
    w_k = din("w_k", [L, 256, 512]); w_v = din("w_v", [L, 256, 512])
    bdp = din("bdp", [L, 128, 2, 128]); bdq = din("bdq", [L, 128, 2, 128]); bdk = din("bdk", [L, 128, 2, 128])
    gq = din("gq", [L, 128, 3]); gkv = din("gkv", [L, 128, 2]); spool = din("spool", [L, 128, 2])
    wmc = din("wmc", [L, 128, 2, 4]); bmc = din("bmc", [L, 128, 2])
    bif = din("bif", [L, 8]); gmh = din("gmh", [L, 256])
    w_out = din("w_out", [L, 1024, 1024])
    ln1_g = din("ln1_g", [L, 1024]); ln1_b = din("ln1_b", [L, 1024])
    w_up = din("w_up", [L, 1024, 2 * D_FF])
    wfc = din("wfc", [L, 128, NFC, 3]); bfc = din("bfc", [L, 128, NFC])
    w_down = din("w_down", [L, D_FF, 1024])
    ln2_g = din("ln2_g", [L, 1024]); ln2_b = din("ln2_b", [L, 1024])
    c_ident = din("c_ident", [128, 128]); c_tri = din("c_tri", [128, 128])
    c_cs = din("c_cs", [64, S]); c_sn = din("c_sn", [64, S])
    c_pinv = din("c_pinv", [2, 128, 2, 512])
    y = nc.dram_tensor("y", [S, 1024], F32, kind="ExternalOutput").ap()

    hA = dscr("hA", [S, 1024], F32); hB = dscr("hB", [S, 1024], F32)
    QN = dscr("QN", [4, 128, S], BF16); QR = dscr("QR", [4, 64, S], BF16)
    KT = dscr("KT", [4, 128, S], BF16); KR = dscr("KR", [64, S], BF16)
    VV = dscr("VV", [S, 512], BF16); MIXT = dscr("MIXT", [1024, S], BF16)
    B_hA, B_hB, B_QN, B_QR, B_KT, B_KR, B_VV, B_MIX = P.bufs(8)

    arena = nc.alloc_sbuf_tensor("arena", [128, 53000], F32)
    ps_all = nc.alloc_psum_tensor("ps", [128, 4096], F32)
    off = [0]

    def reset():
        P.barrier()
        off[0] = 0

    def carve(n, dt=F32):
        nb = n * (2 if dt == BF16 else 4)
        nw = (nb + 3) // 4
        nw = (nw + 7) // 8 * 8
        a = arena[:, off[0]:off[0] + nw]
        off[0] += nw
        assert off[0] <= 53000, "SBUF overflow %d" % off[0]
        if dt == BF16:
            return a.bitcast(BF16)[:, 0:n]
        return a[:, 0:n]

    def c3(n1, n2, dt=F32):
        return carve(n1 * n2, dt).rearrange("p (a b) -> p a b", b=n2)

    banks = [ps_all[:, i * 512:(i + 1) * 512] for i in range(8)]
    Bbank = P.bufs(8)
    ring_state = {"i": 0, "lst": list(range(8))}

    def ring():
        lst = ring_state["lst"]
        i = lst[ring_state["i"] % len(lst)]
        ring_state["i"] += 1
        return banks[i], Bbank[i]

    def set_ring(lst):
        ring_state["lst"] = list(lst)
        ring_state["i"] = 0

    def mm(groups, reads, Bps):
        def fn(e, groups=groups):
            ins = None
            for (o, pairs) in groups:
                n = len(pairs)
                for i, (l, r) in enumerate(pairs):
                    ins = e.matmul(o, l, r, start=(i == 0), stop=(i == n - 1))
            return ins
        cst = 0.0
        for (o, pairs) in groups:
            cst += len(pairs) * (o.shape[-1] / 2400.0 + 0.04)
        P.op("pe", fn, reads=reads, writes=[Bps], cost=cst)

    def tr(groups, reads, Bps, ident):
        def fn(e, groups=groups):
            ins = None
            for (o, i_) in groups:
                ins = e.transpose(o, i_, ident)
            return ins
        P.op("pe", fn, reads=reads, writes=[Bps], cost=0.1 * len(groups))

    CONST_W = 0

    def load_consts():
        ident = carve(128, BF16); tri_b = carve(128, BF16); tri_f = carve(128)
        ones_b = carve(128, BF16); ones_f = carve(128); nh = carve(512)
        Bc = P.buf("consts")
        P.dma("pool", ident, c_ident, Bc, writes=[Bc], acc=True)
        P.dma("pool", tri_b, c_tri, Bc, writes=[Bc], acc=True)
        P.dma("sp", tri_f, c_tri, Bc, writes=[Bc], acc=True)
        B1 = P.buf()
        P.op("pool", lambda e: e.memset(ones_b, 1.0), writes=[B1])
        B2 = P.buf()
        P.op("pool", lambda e: e.memset(ones_f, 1.0), writes=[B2])
        B3 = P.buf()
        P.op("pool", lambda e: e.memset(nh, -0.5), writes=[B3])
        return dict(ident=ident, tri_b=tri_b, tri_f=tri_f, ones_b=ones_b, ones_f=ones_f, nh=nh,
                    B=[Bc, B1, B2, B3])

    C = load_consts()
    const_off = off[0]
    BC = C["B"]

    def reset_phase():
        P.barrier()
        off[0] = const_off

    def ln_tile(src, Bsrc, dst, Bdst, gb, bb, Bgb, tmp):
        st, mv, ve, rs, Bst0, Bst1, Bmv, Bve, Brs = tmp
        P.op("dve", lambda e: e.bn_stats(st[:, 0:6], src[:, 0:512]), reads=[Bsrc], writes=[Bst0])
        P.op("dve", lambda e: e.bn_stats(st[:, 6:12], src[:, 512:1024]), reads=[Bsrc], writes=[Bst1])
        P.op("dve", lambda e: e.bn_aggr(mv, st), reads=[Bst0, Bst1], writes=[Bmv])
        P.op("dve", lambda e: e.tensor_scalar_add(ve, mv[:, 1:2], LN_EPS), reads=[Bmv], writes=[Bve])
        P.op("pool", lambda e: e.tensor_tensor(rs, ve, C["nh"][:, 0:1], ALU.pow), reads=[Bve] + BC, writes=[Brs])
        P.op("dve", lambda e: e.tensor_scalar(dst, src, mv[:, 0:1], rs, ALU.subtract, ALU.mult),
             reads=[Bsrc, Bmv, Brs], writes=[Bdst])
        P.op("pool", lambda e: e.tensor_tensor(dst, dst, gb, ALU.mult), reads=[Bgb], writes=[Bdst])
        P.op("pool", lambda e: e.tensor_tensor(dst, dst, bb, ALU.add), reads=[Bgb], writes=[Bdst])

    def ln_tmp():
        return (carve(12), carve(2), carve(1), carve(1)) + tuple(P.bufs(5))

    def phase0(dst, Bdst_dram):
        reset_phase()
        gb = carve(1024); bb = carve(1024); Bgb = P.buf()
        P.dma("sp", gb, ln0_g.partition_broadcast(128), Bgb, writes=[Bgb], acc=True)
        P.dma("sp", bb, ln0_b.partition_broadcast(128), Bgb, writes=[Bgb], acc=True)
        xt = [carve(1024) for _ in range(2)]; Bx = P.bufs(2)
        yt = [carve(1024) for _ in range(2)]; By = P.bufs(2)
        tmps = [ln_tmp() for _ in range(2)]
        if "0" in os.environ.get("SCHED_PH", "0ABCD"):
            P.begin()
        for i in range(NT):
            s = i % 2
            P.dma("sp", xt[s], x[i * 128:(i + 1) * 128, :], Bx[s], writes=[Bx[s]])
            ln_tile(xt[s], Bx[s], yt[s], By[s], gb, bb, Bgb, tmps[s])
            P.dma("sp", dst[i * 128:(i + 1) * 128, :], yt[s], By[s], reads=[By[s]], writes=[Bdst_dram], acc=True)

    def phaseA(l, hin, Bhin):
        reset_phase()
        set_ring(range(8))
        ident = C["ident"]
        win = c3(8, WINE, BF16); wuq = c3(3, 1024, BF16); wk = c3(2, 512, BF16); wv = c3(2, 512, BF16)
        bp = c3(2, 128, BF16); bq = c3(2, 128, BF16); bk = c3(2, 128, BF16)
        Bw = P.buf("wA")
        for k in range(8):
            P.dma("pool", win[:, k, :], w_in[l, k * 128:(k + 1) * 128, :], Bw, writes=[Bw], acc=True)
        P.dma("pool", wuq, w_uq[l].rearrange("(k p) n -> p k n", p=128), Bw, writes=[Bw], acc=True)
        P.dma("pool", wk, w_k[l].rearrange("(k p) n -> p k n", p=128), Bw, writes=[Bw], acc=True)
        P.dma("pool", wv, w_v[l].rearrange("(k p) n -> p k n", p=128), Bw, writes=[Bw], acc=True)
        P.dma("pool", bp, bdp[l], Bw, writes=[Bw], acc=True)
        P.dma("pool", bq, bdq[l], Bw, writes=[Bw], acc=True)
        P.dma("pool", bk, bdk[l], Bw, writes=[Bw], acc=True)
        gq_t = carve(3); gkv_t = carve(2); sp_t = carve(2); wmc_t = c3(2, 4); bmc_t = carve(2)
        bif_t = carve(8); gmh_t = carve(256); pinv = [c3(2, 512) for _ in range(2)]
        Bsm = P.buf("smallA")
        P.dma("sp", gq_t, gq[l], Bsm, writes=[Bsm], acc=True)
        P.dma("sp", gkv_t, gkv[l], Bsm, writes=[Bsm], acc=True)
        P.dma("sp", sp_t, spool[l], Bsm, writes=[Bsm], acc=True)
        P.dma("sp", wmc_t, wmc[l], Bsm, writes=[Bsm], acc=True)
        P.dma("sp", bmc_t, bmc[l], Bsm, writes=[Bsm], acc=True)
        P.dma("sp", bif_t, bif[l].partition_broadcast(128), Bsm, writes=[Bsm], acc=True)
        P.dma("sp", gmh_t, gmh[l].partition_broadcast(128), Bsm, writes=[Bsm], acc=True)
        P.dma("sp", pinv[0], c_pinv[0], Bsm, writes=[Bsm], acc=True)
        P.dma("sp", pinv[1], c_pinv[1], Bsm, writes=[Bsm], acc=True)
        gqs = carve(3); Bgqs = P.buf()
        P.op("dve", lambda e: e.tensor_scalar_mul(gqs, gq_t, QSCALE), reads=[Bsm], writes=[Bgqs])

        if int(os.environ.get('ASTOP', '99')) == 1:
            return
        hx = c3(4, 1024); Bhx = P.buf()
        hb = c3(4, 1024, BF16); Bhb = P.bufs(2)
        hT = c3(8, 512, BF16); BhT = P.bufs(4)
        cs_t = [carve(512) for _ in range(1)]; sn_t = [carve(512) for _ in range(1)]; Bcs = P.bufs(1)
        craw = c3(5, 512); Bcraw = P.bufs(5)
        sq = c3(5, 512, BF16); Bsq = P.bufs(5)
        cn = c3(5, 512, BF16); Bcn = P.bufs(5)
        rr = c3(2, 512); Brr = P.bufs(2)
        t1 = [carve(512) for _ in range(1)]; t2 = [carve(512) for _ in range(1)]; Bt1 = P.bufs(1); Bt2 = P.bufs(1)
        xpb = c3(2, 528); Bxp = P.buf(); T1 = c3(2, 528); T2 = c3(2, 528); BT1 = P.buf(); BT2 = P.buf()
        ybf = c3(2, 512, BF16); Bybf = P.buf()
        xqb = c3(2, 515); Bxq = P.buf(); acc = c3(2, 512); Bacc = P.bufs(2)
        cT = c3(2, 512, BF16); BcT = P.bufs(2)
        mqT = c3(2, 512, BF16); mkTe = c3(2, 512, BF16); mkTo = c3(2, 512, BF16); BmqT = P.bufs(2); BmkT = P.bufs(2)
        mktok = c3(4, 256, BF16); Bmktok = P.bufs(4)
        xvo = c3(4, 512); Bxvo = P.bufs(4)
        gz = c3(4, 8); Bgz = P.bufs(4)
        QNs = [c3(4, 512, BF16) for _ in range(1)]; QRs = [c3(4, 512, BF16) for _ in range(1)]
        KTs = [c3(4, 512, BF16) for _ in range(1)]; KRs = [carve(512, BF16) for _ in range(1)]
        Vs = [c3(4, 512, BF16) for _ in range(1)]; PTs = [c3(2, 512, BF16) for _ in range(1)]
        MTs = [c3(2, 512, BF16) for _ in range(1)]
        BQN = P.bufs(2); BQR = P.bufs(2); BKT = P.bufs(2); BKR = P.bufs(2); BV = P.bufs(2); BPT = P.bufs(2); BMT = P.bufs(2)
        Sf = c3(2, 66); BSf = P.buf(); Sbe = [c3(2, 66, BF16) for _ in range(2)]; Sbo = [c3(2, 66, BF16) for _ in range(2)]; BSb = P.bufs(2)
        Stmp = c3(2, 66); BStmp = P.buf()
        vaug = [c3(4, 66, BF16) for _ in range(4)]; Bvaug = P.bufs(4)
        scm = [c3(4, 128, BF16) for _ in range(4)]; Bscm = P.bufs(4)
        sm = [carve(64) for _ in range(4)]; Bsm2 = [P.bufs(12) for _ in range(4)]
        hh = [c3(4, 64) for _ in range(4)]; Bhh = P.bufs(4); hc = [c3(4, 64) for _ in range(4)]; Bhc = P.bufs(4)
        hsq = [c3(4, 64) for _ in range(4)]; Bhsq = P.bufs(4)
        sig = [carve(256) for _ in range(4)]; Bsig = P.bufs(4)
        mtok = [carve(256, BF16) for _ in range(4)]; Bmtok = P.bufs(4)
        afull = [carve(2) for _ in range(4)]; Baf = P.bufs(4)

        P.op("dve", lambda e: e.memset(Sf, 0.0), writes=[BSf])
        for i_ in range(2):
            P.op("pool", lambda e, i_=i_: e.memset(Sbe[i_], 0.0), writes=[BSb[i_]])
            P.op("pool", lambda e, i_=i_: e.memset(Sbo[i_], 0.0), writes=[BSb[i_]])
        P.op("pool", lambda e: e.memset(mkTe, 0.0), writes=BmkT)
        P.op("pool", lambda e: e.memset(mkTo, 0.0), writes=BmkT)
        P.op("pool", lambda e: e.memset(xqb[:, :, 0:3], 0.0), writes=[Bxq])
        P.op("pool", lambda e: e.memset(xpb[:, :, 0:16], 0.0), writes=[Bxp])
        sb_i = [0]

        def proj_fm(col0, ncols):
            ps, Bps = ring()
            mm([(ps[0:ncols, :], [(win[:, k, col0:col0 + ncols], hT[:, k, :]) for k in range(8)])],
               [Bw] + BhT, Bps)
            return ps, Bps

        if "A" in os.environ.get("SCHED_PH", "0ABCD"):
            P.begin()
        for ti in range(NT5):
            t0 = ti * 512
            s2 = 0
            P.dma("sp", hx, hin[t0:t0 + 512, :].rearrange("(j p) d -> p j d", p=128), Bhx,
                  reads=[Bhin], writes=[Bhx])
            P.dma("sp", cs_t[s2][0:64, :], c_cs[:, t0:t0 + 512], Bcs[s2], writes=[Bcs[s2]], acc=False)
            P.dma("sp", sn_t[s2][0:64, :], c_sn[:, t0:t0 + 512], Bcs[s2], reads=[], writes=[Bcs[s2]], acc=True)
            cs = cs_t[s2][0:64, :]; sn = sn_t[s2][0:64, :]
            P.op("act", lambda e: e.copy(hb[:, 0:2, :], hx[:, 0:2, :]), reads=[Bhx], writes=[Bhb[0]])
            P.op("dve", lambda e: e.tensor_copy(hb[:, 2:4, :], hx[:, 2:4, :]), reads=[Bhx], writes=[Bhb[1]])
            for kp in range(4):
                ps, Bps = ring()
                psb = ps.bitcast(BF16)
                groups = []
                for kk in range(2):
                    k = 2 * kp + kk
                    for j in range(4):
                        groups.append((psb[:, kk * 512 + j * 128: kk * 512 + (j + 1) * 128],
                                       hb[:, j, k * 128:(k + 1) * 128]))
                tr(groups, Bhb + BC, Bps, ident)
                dstv = hT[:, 2 * kp:2 * kp + 2, :]
                srcv = psb.rearrange("p (a b) -> p a b", b=512)
                if kp % 2 == 0:
                    P.op("act", lambda e, d=dstv, s_=srcv: e.copy(d, s_), reads=[Bps], writes=[BhT[kp]])
                else:
                    P.op("dve", lambda e, d=dstv, s_=srcv: e.tensor_copy(d, s_), reads=[Bps], writes=[BhT[kp]])
            if int(os.environ.get('ASTOP', '99')) == 2:
                return
            for c in range(5):
                ps, Bps = proj_fm(c * 128, 128)
                P.op("act", lambda e, c=c, ps=ps: e.copy(craw[:, c, :], ps), reads=[Bps], writes=[Bcraw[c]])
                P.op("act", lambda e, c=c, ps=ps: e.activation(sq[:, c, :], ps, AF.Square), reads=[Bps], writes=[Bsq[c]])
            for (i, cl, n) in ((0, (0, 1, 2), 384.0), (1, (3, 4), 256.0)):
                ps, Bps = ring()
                mm([(ps, [(C["ones_b"], sq[:, c, :]) for c in cl])], [Bsq[c] for c in cl] + BC, Bps)
                P.op("dve", lambda e, i=i, ps=ps, n=n: e.tensor_scalar(rr[:, i, :], ps, 1.0 / n, RMS_EPS, ALU.mult, ALU.add),
                     reads=[Bps], writes=[Brr[i]])
                P.op("act", lambda e, i=i: e.activation(rr[:, i, :], rr[:, i, :], AF.Sqrt), reads=[Brr[i]], writes=[Brr[i]])
                P.op("dve", lambda e, i=i: e.reciprocal(rr[:, i, :], rr[:, i, :]), reads=[Brr[i]], writes=[Brr[i]])
            if int(os.environ.get('ASTOP', '99')) == 3:
                return
            psa, Bpsa = proj_fm(640, 64)
            psb_, Bpsb = proj_fm(1736, 64)
            P.op("dve", lambda e, psa=psa: e.tensor_tensor(t1[0][0:64, :], psa[0:64, :], cs, ALU.mult), reads=[Bpsa, Bcs[s2]], writes=[Bt1[0]])
            P.op("dve", lambda e, psb_=psb_: e.tensor_tensor(t2[0][0:64, :], psb_[0:64, :], sn, ALU.mult), reads=[Bpsb, Bcs[s2]], writes=[Bt2[0]])
            P.op("dve", lambda e: e.tensor_tensor(KRs[s2][0:64, :], t1[0][0:64, :], t2[0][0:64, :], ALU.add),
                 reads=[Bt1[0], Bt2[0]], writes=[BKR[s2]])
            if int(os.environ.get('ASTOP', '99')) == 4:
                return
            if ti > 0:
                P.op("pool", lambda e: e.tensor_copy(xpb[:, :, 0:16], xpb[:, :, 512:528]), reads=[Bxp], writes=[Bxp])
                P.op("pool", lambda e: e.tensor_copy(xqb[:, :, 0:3], xqb[:, :, 512:515]), reads=[Bxq], writes=[Bxq])
            for c in range(2):
                ps, Bps = proj_fm(704 + c * 128, 128)
                P.op("act", lambda e, c=c, ps=ps: e.copy(xpb[:, c, 16:528], ps), reads=[Bps], writes=[Bxp])
            P.op("dve", lambda e: e.tensor_tensor(T1[:, :, 1:528], xpb[:, :, 1:528], xpb[:, :, 0:527], ALU.add), reads=[Bxp], writes=[BT1])
            P.op("dve", lambda e: e.tensor_tensor(T2[:, :, 3:528], T1[:, :, 3:528], T1[:, :, 1:526], ALU.add), reads=[BT1], writes=[BT2])
            P.op("dve", lambda e: e.tensor_tensor(T1[:, 1, 7:528], T2[:, 1, 7:528], T2[:, 1, 3:524], ALU.add), reads=[BT2], writes=[BT1])
            P.op("dve", lambda e: e.tensor_tensor(T2[64:128, 1, 15:528], T1[64:128, 1, 15:528], T1[64:128, 1, 7:520], ALU.add), reads=[BT1], writes=[BT2])
            pv = pinv[0] if ti == 0 else pinv[1]
            P.op("dve", lambda e, pv=pv: e.tensor_tensor(T1[0:64, :, 16:528], T1[0:64, :, 16:528], pv[0:64, :, :], ALU.mult), reads=[BT1, Bsm], writes=[BT1])
            P.op("dve", lambda e, pv=pv: e.tensor_tensor(T2[64:128, :, 16:528], T2[64:128, :, 16:528], pv[64:128, :, :], ALU.mult), reads=[BT2, Bsm], writes=[BT2])
            P.op("dve", lambda e: e.tensor_tensor(ybf[0:64, :, :], T1[0:64, :, 16:528], xpb[0:64, :, 16:528], ALU.subtract), reads=[BT1, Bxp], writes=[Bybf])
            P.op("dve", lambda e: e.tensor_tensor(ybf[64:128, :, :], T2[64:128, :, 16:528], xpb[64:128, :, 16:528], ALU.subtract), reads=[BT2, Bxp], writes=[Bybf])
            if int(os.environ.get('ASTOP', '99')) == 5:
                return
            for c in range(2):
                ps, Bps = proj_fm(960 + c * 128, 128)
                P.op("act", lambda e, c=c, ps=ps: e.copy(xqb[:, c, 3:515], ps), reads=[Bps], writes=[Bxq])
            for c in range(2):
                P.op("dve", lambda e, c=c: e.tensor_scalar(acc[:, c, :], xqb[:, c, 0:512], wmc_t[:, c, 0:1], bmc_t[:, c:c + 1], ALU.mult, ALU.add),
                     reads=[Bxq, Bsm], writes=[Bacc[c]])
                for k in range(1, 4):
                    P.op("dve", lambda e, c=c, k=k: e.scalar_tensor_tensor(acc[:, c, :], xqb[:, c, k:k + 512], wmc_t[:, c, k:k + 1], acc[:, c, :], ALU.mult, ALU.add),
                         reads=[Bxq, Bsm], writes=[Bacc[c]])
                P.op("act", lambda e, c=c: e.activation(cT[:, c, :], acc[:, c, :], AF.Silu), reads=[Bacc[c]], writes=[BcT[c]])
            if int(os.environ.get('ASTOP', '99')) == 6:
                return
            for j in range(4):
                ps, Bps = ring()
                mm([(ps, [(hT[:, k, j * 128:(j + 1) * 128], win[:, k, 1216:1728]) for k in range(8)])], [Bw] + BhT, Bps)
                P.op("act", lambda e, j=j, ps=ps: e.copy(xvo[:, j, :], ps), reads=[Bps], writes=[Bxvo[j]])
                ps, Bps = ring()
                mm([(ps[:, 0:8], [(hT[:, k, j * 128:(j + 1) * 128], win[:, k, 1728:1736]) for k in range(8)])], [Bw] + BhT, Bps)
                P.op("dve", lambda e, j=j, ps=ps: e.tensor_tensor(gz[:, j, :], ps[:, 0:8], bif_t, ALU.add), reads=[Bps, Bsm], writes=[Bgz[j]])
            if int(os.environ.get('ASTOP', '99')) == 7:
                return
            for c in range(5):
                i = 0 if c < 3 else 1
                gcol = gqs[:, c:c + 1] if c < 3 else gkv_t[:, c - 3:c - 2]
                P.op("dve", lambda e, c=c, i=i, gcol=gcol: e.scalar_tensor_tensor(cn[:, c, :], craw[:, c, :], gcol, rr[:, i, :], ALU.mult, ALU.mult),
                     reads=[Bcraw[c], Brr[i], Bgqs, Bsm], writes=[Bcn[c]])
            for h in range(4):
                ps, Bps = ring()
                mm([(ps, [(wuq[:, c, h * 256:h * 256 + 128], cn[:, c, :]) for c in range(3)])], [Bw] + Bcn[0:3], Bps)
                P.op("act", lambda e, h=h, ps=ps: e.copy(QNs[s2][:, h, :], ps), reads=[Bps], writes=[BQN[s2]])
                psa, Bpsa = ring()
                mm([(psa[0:64, :], [(wuq[:, c, h * 256 + 128:h * 256 + 192], cn[:, c, :]) for c in range(3)])], [Bw] + Bcn[0:3], Bpsa)
                psb_, Bpsb = ring()
                mm([(psb_[0:64, :], [(wuq[:, c, h * 256 + 192:h * 256 + 256], cn[:, c, :]) for c in range(3)])], [Bw] + Bcn[0:3], Bpsb)
                a_ = 0
                P.op("dve", lambda e, psa=psa, a_=a_: e.tensor_tensor(t1[a_][0:64, :], psa[0:64, :], cs, ALU.mult), reads=[Bpsa, Bcs[s2]], writes=[Bt1[a_]])
                P.op("dve", lambda e, psb_=psb_, a_=a_: e.tensor_tensor(t2[a_][0:64, :], psb_[0:64, :], sn, ALU.mult), reads=[Bpsb, Bcs[s2]], writes=[Bt2[a_]])
                P.op("dve", lambda e, h=h, a_=a_: e.tensor_tensor(QRs[s2][0:64, h, :], t1[a_][0:64, :], t2[a_][0:64, :], ALU.add),
                     reads=[Bt1[a_], Bt2[a_]], writes=[BQR[s2]])
            for h in range(4):
                ps, Bps = ring()
                mm([(ps, [(wk[:, c, h * 128:(h + 1) * 128], cn[:, 3 + c, :]) for c in range(2)])], [Bw] + Bcn[3:5], Bps)
                if h % 2 == 0:
                    P.op("act", lambda e, h=h, ps=ps: e.copy(KTs[s2][:, h, :], ps), reads=[Bps], writes=[BKT[s2]])
                else:
                    P.op("dve", lambda e, h=h, ps=ps: e.tensor_copy(KTs[s2][:, h, :], ps), reads=[Bps], writes=[BKT[s2]])
            for j in range(4):
                ps, Bps = ring()
                mm([(ps, [(cn[:, 3 + c, j * 128:(j + 1) * 128], wv[:, c, :]) for c in range(2)])], [Bw] + Bcn[3:5], Bps)
                if j % 2 == 0:
                    P.op("act", lambda e, j=j, ps=ps: e.copy(Vs[s2][:, j, :], ps), reads=[Bps], writes=[BV[s2]])
                else:
                    P.op("dve", lambda e, j=j, ps=ps: e.tensor_copy(Vs[s2][:, j, :], ps), reads=[Bps], writes=[BV[s2]])
            if int(os.environ.get('ASTOP', '99')) == 8:
                return
            for c in range(2):
                ps, Bps = ring()
                mm([(ps, [(bp[:, c, :], ybf[:, c, :])])], [Bw, Bybf], Bps)
                P.op("act", lambda e, c=c, ps=ps: e.activation(PTs[s2][:, c, :], ps, AF.Copy, scale=sp_t[:, c:c + 1]),
                     reads=[Bps, Bsm], writes=[BPT[s2]])
            for c in range(2):
                ps, Bps = ring()
                mm([(ps, [(bq[:, c, :], cT[:, c, :])])], [Bw, BcT[c]], Bps)
                P.op("act", lambda e, c=c, ps=ps: e.copy(mqT[:, c, :], ps), reads=[Bps], writes=[BmqT[c]])
                ps, Bps = ring()
                mm([(ps, [(bk[:, c, :], cT[:, c, :])])], [Bw, BcT[c]], Bps)
                P.op("act", lambda e, c=c, ps=ps: e.mul(mkTe[0:64, c, :], ps[0:64, :], 0.125), reads=[Bps], writes=[BmkT[c]])
                P.op("act", lambda e, c=c, ps=ps: e.mul(mkTo[64:128, c, :], ps[64:128, :], 0.125), reads=[Bps], writes=[BmkT[c]])
            for j in range(4):
                ps, Bps = ring()
                mm([(ps[:, c * 128:(c + 1) * 128], [(cT[:, c, j * 128:(j + 1) * 128], bk[:, c, :])]) for c in range(2)], [Bw] + BcT, Bps)
                P.op("act", lambda e, j=j, ps=ps: e.mul(mktok[:, j, :], ps[:, 0:256], 0.125), reads=[Bps], writes=[Bmktok[j]])
            NJ = 4
            X = [dict() for _ in range(NJ)]
            for j in range(NJ):
                u = j
                x_ = X[j]
                smv = sm[u]
                x_.update(u=u, b=Bsm2[u], e1=smv[:, 0:4], spv=smv[:, 4:8], qsc=smv[:, 8:12], ksc=smv[:, 12:16], ktm=smv[:, 16:20],
                          dnt=smv[:, 20:24], rec=smv[:, 24:28], ff=smv[:, 28:32], msum=smv[:, 32:36], vsum=smv[:, 36:40],
                          rst=smv[:, 40:44], vtm=smv[:, 44:48], ngt=smv[:, 48:52], zi=gz[:, j, 0:4], zf=gz[:, j, 4:8],
                          af=afull[u], va=vaug[u], sc=scm[u], hv=hh[u], hcv=hc[u], hq=hsq[u])

            def st_gate1(j):
                x_ = X[j]; b_ = x_["b"]
                P.op("act", lambda e, x_=x_: e.activation(x_["e1"], x_["zf"], AF.Exp, scale=-1.0), reads=[Bgz[j]], writes=[b_[0]])

            def st_gate2(j):
                x_ = X[j]; b_ = x_["b"]
                P.op("act", lambda e, x_=x_: e.activation(x_["spv"], x_["e1"], AF.Ln, bias=1.0), reads=[b_[0]], writes=[b_[1]])

            def st_cum(j):
                x_ = X[j]; b_ = x_["b"]
                psc, Bpsc = ring()
                mm([(psc[:, 0:4], [(C["tri_f"], x_["spv"])]), (psc[:, 8:12], [(C["ones_f"], x_["spv"])])], [b_[1]] + BC, Bpsc)
                x_["psc"] = psc; x_["Bpsc"] = Bpsc

            def st_scales(j):
                x_ = X[j]; b_ = x_["b"]; psc = x_["psc"]; Bpsc = x_["Bpsc"]; u = x_["u"]
                P.op("act", lambda e, x_=x_, psc=psc: e.activation(x_["qsc"], psc[:, 0:4], AF.Exp, scale=-1.0), reads=[Bpsc], writes=[b_[2]])
                P.op("dve", lambda e, x_=x_, psc=psc: e.tensor_tensor(x_["ktm"], psc[:, 0:4], x_["zi"], ALU.add), reads=[Bpsc, Bgz[j]], writes=[b_[3]])
                P.op("act", lambda e, x_=x_, psc=psc: e.activation(x_["af"][0:64, 0:2], psc[0:64, 8:12:2], AF.Exp, scale=-1.0), reads=[Bpsc], writes=[Baf[u]])
                P.op("act", lambda e, x_=x_, psc=psc: e.activation(x_["af"][64:128, 0:2], psc[64:128, 9:12:2], AF.Exp, scale=-1.0), reads=[Bpsc], writes=[Baf[u]])

            def st_ksc(j):
                x_ = X[j]; b_ = x_["b"]
                P.op("act", lambda e, x_=x_: e.activation(x_["ksc"], x_["ktm"], AF.Exp), reads=[b_[3]], writes=[b_[4]])

            def st_vaug(j):
                x_ = X[j]; b_ = x_["b"]; u = x_["u"]; va = x_["va"]; ksc = x_["ksc"]
                P.op("dve", lambda e, va=va, j=j, ksc=ksc: e.tensor_tensor(va[:, :, 0:64], xvo[:, j, 0:256].rearrange("p (h d) -> p h d", d=64),
                                                                            ksc.unsqueeze(2).broadcast_to([128, 4, 64]), ALU.mult),
                     reads=[Bxvo[j], b_[4]], writes=[Bvaug[u]])
                P.op("dve", lambda e, va=va, ksc=ksc: e.tensor_copy(va[:, :, 64:66], ksc.unsqueeze(2).broadcast_to([128, 4, 2])), reads=[b_[4]], writes=[Bvaug[u]])
                P.op("act", lambda e, u=u, j=j: e.activation(sig[u], xvo[:, j, 256:512], AF.Sigmoid), reads=[Bxvo[j]], writes=[Bsig[u]])

            def st_scores(j):
                x_ = X[j]; u = x_["u"]; sc = x_["sc"]
                pss, Bpss = ring()
                groups = []
                for h in range(4):
                    c = h // 2
                    mkx = mkTe if h % 2 == 0 else mkTo
                    groups.append((pss[:, h * 128:(h + 1) * 128],
                                   [(mkx[:, c, j * 128:(j + 1) * 128], mqT[:, c, j * 128:(j + 1) * 128])]))
                mm(groups, BmkT + BmqT, Bpss)
                P.op("dve", lambda e, sc=sc, pss=pss: e.tensor_tensor(sc, pss.rearrange("p (h t) -> p h t", t=128),
                                                                      C["tri_b"].unsqueeze(1).broadcast_to([128, 4, 128]), ALU.mult),
                     reads=[Bpss] + BC, writes=[Bscm[u]])

            def st_state(j):
                x_ = X[j]; u = x_["u"]; sc = x_["sc"]; va = x_["va"]; af = x_["af"]
                cur = sb_i[0]
                psn, Bpsn = ring()
                groups = []
                for h in range(4):
                    c = h // 2
                    groups.append((psn[:, h * 66:(h + 1) * 66],
                                   [(sc[:, h, :], va[:, h, :]),
                                    (mqT[:, c, j * 128:(j + 1) * 128], (Sbe if h % 2 == 0 else Sbo)[cur][:, c, :])]))
                mm(groups, [Bscm[u], Bvaug[u], BSb[cur]] + BmqT, Bpsn)
                x_["psn"] = psn; x_["Bpsn"] = Bpsn
                psu, Bpsu = ring()
                va2 = va.rearrange("p h n -> p (h n)")
                mm([(psu[:, c * 132:(c + 1) * 132], [(mktok[:, j, c * 128:(c + 1) * 128], va2[:, c * 132:(c + 1) * 132])]) for c in range(2)],
                   [Bmktok[j], Bvaug[u]], Bpsu)
                psu3 = psu[:, 0:264].rearrange("p (c n) -> p c n", n=132)
                P.op("dve", lambda e, psu3=psu3: e.tensor_tensor(Stmp[0:64, :, :], psu3[0:64, :, 0:66], Sf[0:64, :, :], ALU.add), reads=[Bpsu, BSf], writes=[BStmp])
                P.op("dve", lambda e, psu3=psu3: e.tensor_tensor(Stmp[64:128, :, :], psu3[64:128, :, 66:132], Sf[64:128, :, :], ALU.add), reads=[Bpsu, BSf], writes=[BStmp])
                P.op("dve", lambda e, af=af: e.tensor_tensor(Sf, Stmp, af.unsqueeze(2).broadcast_to([128, 2, 66]), ALU.mult), reads=[BStmp, Baf[u]], writes=[BSf])
                nxt = 1 - cur
                P.op("pool", lambda e, nxt=nxt: e.tensor_copy(Sbe[nxt][0:64, :, :], Sf[0:64, :, :]), reads=[BSf], writes=[BSb[nxt]])
                P.op("pool", lambda e, nxt=nxt: e.tensor_copy(Sbo[nxt][64:128, :, :], Sf[64:128, :, :]), reads=[BSf], writes=[BSb[nxt]])
                sb_i[0] = nxt

            def mk_post():
                steps = []

                def s1(j):
                    x_ = X[j]; b_ = x_["b"]
                    psn3 = x_["psn"][:, 0:264].rearrange("p (h n) -> p h n", n=66)
                    x_["psn3"] = psn3
                    P.op("dve", lambda e, x_=x_, psn3=psn3: e.tensor_tensor(x_["dnt"].unsqueeze(2), psn3[:, :, 64:65], x_["qsc"].unsqueeze(2), ALU.mult), reads=[x_["Bpsn"], b_[2]], writes=[b_[5]])
                steps.append(s1)
                steps.append(lambda j: P.op("dve", lambda e, x_=X[j]: e.tensor_scalar_mul(x_["ngt"], x_["dnt"], -1.0), reads=[X[j]["b"][5]], writes=[X[j]["b"][6]]))
                steps.append(lambda j: P.op("dve", lambda e, x_=X[j]: e.tensor_tensor(x_["dnt"], x_["dnt"], x_["ngt"], ALU.max), reads=[X[j]["b"][6]], writes=[X[j]["b"][5]]))
                steps.append(lambda j: P.op("dve", lambda e, x_=X[j]: e.tensor_scalar_max(x_["dnt"], x_["dnt"], 1.0), reads=[X[j]["b"][5]], writes=[X[j]["b"][5]]))
                steps.append(lambda j: P.op("dve", lambda e, x_=X[j]: e.reciprocal(x_["rec"], x_["dnt"]), reads=[X[j]["b"][5]], writes=[X[j]["b"][6]]))
                steps.append(lambda j: P.op("dve", lambda e, x_=X[j]: e.tensor_tensor(x_["ff"], x_["rec"], x_["qsc"], ALU.mult), reads=[X[j]["b"][6], X[j]["b"][2]], writes=[X[j]["b"][7]]))
                steps.append(lambda j: P.op("dve", lambda e, x_=X[j]: e.tensor_tensor(x_["hv"], x_["psn3"][:, :, 0:64], x_["ff"].unsqueeze(2).broadcast_to([128, 4, 64]), ALU.mult),
                                            reads=[X[j]["Bpsn"], X[j]["b"][7]], writes=[Bhh[X[j]["u"]]]))
                steps.append(lambda j: P.op("dve", lambda e, x_=X[j]: e.reduce_sum(x_["msum"], x_["hv"], AX.X), reads=[Bhh[X[j]["u"]]], writes=[X[j]["b"][8]]))
                steps.append(lambda j: P.op("dve", lambda e, x_=X[j]: e.tensor_scalar_mul(x_["msum"], x_["msum"], 1.0 / 64.0), reads=[X[j]["b"][8]], writes=[X[j]["b"][8]]))
                steps.append(lambda j: P.op("dve", lambda e, x_=X[j]: e.tensor_tensor(x_["hcv"], x_["hv"], x_["msum"].unsqueeze(2).broadcast_to([128, 4, 64]), ALU.subtract),
                                            reads=[Bhh[X[j]["u"]], X[j]["b"][8]], writes=[Bhc[X[j]["u"]]]))
                steps.append(lambda j: P.op("pool", lambda e, x_=X[j]: e.tensor_tensor(x_["hq"], x_["hcv"], x_["hcv"], ALU.mult), reads=[Bhc[X[j]["u"]]], writes=[Bhsq[X[j]["u"]]]))
                steps.append(lambda j: P.op("dve", lambda e, x_=X[j]: e.reduce_sum(x_["vsum"], x_["hq"], AX.X), reads=[Bhsq[X[j]["u"]]], writes=[X[j]["b"][9]]))
                steps.append(lambda j: P.op("dve", lambda e, x_=X[j]: e.tensor_scalar(x_["vtm"], x_["vsum"], 1.0 / 64.0, LN_EPS, ALU.mult, ALU.add), reads=[X[j]["b"][9]], writes=[X[j]["b"][10]]))
                steps.append(lambda j: P.op("pool", lambda e, x_=X[j]: e.tensor_tensor(x_["rst"], x_["vtm"], C["nh"][:, 0:4], ALU.pow), reads=[X[j]["b"][10]] + BC, writes=[X[j]["b"][11]]))
                steps.append(lambda j: P.op("dve", lambda e, x_=X[j]: e.tensor_tensor(x_["hcv"], x_["hcv"], x_["rst"].unsqueeze(2).broadcast_to([128, 4, 64]), ALU.mult),
                                            reads=[X[j]["b"][11]], writes=[Bhc[X[j]["u"]]]))
                steps.append(lambda j: P.op("pool", lambda e, x_=X[j]: e.tensor_tensor(x_["hcv"].rearrange("p h d -> p (h d)"), x_["hcv"].rearrange("p h d -> p (h d)"), gmh_t, ALU.mult),
                                            reads=[Bsm], writes=[Bhc[X[j]["u"]]]))
                steps.append(lambda j: P.op("pool", lambda e, x_=X[j]: e.tensor_tensor(mtok[x_["u"]], x_["hcv"].rearrange("p h d -> p (h d)"), sig[x_["u"]], ALU.mult),
                                            reads=[Bhc[X[j]["u"]], Bsig[X[j]["u"]]], writes=[Bmtok[X[j]["u"]]]))

                def s_tr(j):
                    u = X[j]["u"]
                    pst, Bpst = ring()
                    pstb = pst.bitcast(BF16)
                    tr([(pstb[:, c * 128:(c + 1) * 128], mtok[u][:, c * 128:(c + 1) * 128]) for c in range(2)], [Bmtok[u]] + BC, Bpst, ident)
                    P.op("act", lambda e, j=j, pstb=pstb: e.copy(MTs[s2][:, :, j * 128:(j + 1) * 128], pstb[:, 0:256].rearrange("p (c t) -> p c t", t=128)),
                         reads=[Bpst], writes=[BMT[s2]])
                steps.append(s_tr)
                return steps

            for st in (st_gate1, st_gate2, st_cum, st_scales, st_ksc, st_vaug, st_scores):
                for j in range(NJ):
                    st(j)
            for j in range(NJ):
                st_state(j)
            for st in mk_post():
                for j in range(NJ):
                    st(j)
            if int(os.environ.get('ASTOP', '99')) == 15:
                return
            P.dma("sp", QN[:, :, t0:t0 + 512].rearrange("h p t -> p h t"), QNs[s2], BQN[s2], reads=[BQN[s2]], writes=[B_QN], acc=True)
            P.dma("sp", QR[:, :, t0:t0 + 512].rearrange("h p t -> p h t"), QRs[s2][0:64, :, :], BQR[s2], reads=[BQR[s2]], writes=[B_QR], acc=True)
            P.dma("sp", KT[:, :, t0:t0 + 512].rearrange("h p t -> p h t"), KTs[s2], BKT[s2], reads=[BKT[s2]], writes=[B_KT], acc=True)
            P.dma("sp", KR[:, t0:t0 + 512], KRs[s2][0:64, :], BKR[s2], reads=[BKR[s2]], writes=[B_KR], acc=True)
            P.dma("sp", VV[t0:t0 + 512, :].rearrange("(j p) c -> p j c", p=128), Vs[s2], BV[s2], reads=[BV[s2]], writes=[B_VV], acc=True)
            P.dma("sp", MIXT[512:768, t0:t0 + 512].rearrange("(c p) t -> p c t", p=128), PTs[s2], BPT[s2], reads=[BPT[s2]], writes=[B_MIX], acc=True)
            P.dma("sp", MIXT[768:1024, t0:t0 + 512].rearrange("(c p) t -> p c t", p=128), MTs[s2], BMT[s2], reads=[BMT[s2]], writes=[B_MIX], acc=True)

    def phaseB(l):
        reset_phase()
        NKT = S // 128
        Kt = [carve(S, BF16) for _ in range(2)]; Vt = [c3(NKT, 128, BF16) for _ in range(2)]
        BK = P.bufs(2); BVt = P.bufs(2)
        Kr = carve(S, BF16); BKr = P.buf()
        qn = [carve(512, BF16) for _ in range(2)]; qr = [carve(512, BF16) for _ in range(2)]; Bq = P.bufs(2)
        NP = int(os.environ.get('B_NP', '8'))
        Pt = [carve(512, BF16) for _ in range(NP)]; BPt = P.bufs(NP)
        rec = carve(512); Brec = P.buf()
        Pacc = [[carve(512) for _ in range(2)] for _ in range(2)]
        BPacc = [P.bufs(2) for _ in range(2)]
        ost = [carve(512, BF16) for _ in range(2)]; Bost = P.bufs(2)
        P.dma("sp", Kr[0:64, :], KR, BKr, reads=[B_KR], writes=[BKr])
        bcfg = [int(v) for v in os.environ.get("B_BANKS", "0,1,2,3,4,5,6,7").split(",")]
        set_ring(bcfg[4:8])
        pi = 0
        qi = 0
        if "B" in os.environ.get("SCHED_PH", "0ABCD"):
            P.begin()
        def loadkv(h_):
            hs_ = h_ % 2
            P.dma("sp", Kt[hs_], KT[h_], BK[hs_], reads=[B_KT], writes=[BK[hs_]])
            P.dma("sp", Vt[hs_], VV[:, h_ * 128:(h_ + 1) * 128].rearrange("(n p) c -> p n c", p=128), BVt[hs_], reads=[B_VV], writes=[BVt[hs_]])
        loadkv(0)
        for h in range(4):
            hs = h % 2
            if h + 1 < 4:
                loadkv(h + 1)
            def loadq(h_, qb_, qs_):
                q0_ = qb_ * 512
                P.dma("sp", qn[qs_], QN[h_, :, q0_:q0_ + 512], Bq[qs_], reads=[B_QN], writes=[Bq[qs_]])
                P.dma("sp", qr[qs_][0:64, :], QR[h_, :, q0_:q0_ + 512], Bq[qs_], reads=[B_QR], writes=[Bq[qs_]], acc=True)
            NQB = S // 512
            if h == 0:
                loadq(0, 0, 0)
            for qb in range(NQB):
                q0 = qb * 512
                qs = qi % 2
                qi += 1
                if qb + 1 < NQB:
                    loadq(h, qb + 1, qi % 2)
                elif h + 1 < 4:
                    loadq(h + 1, 0, qi % 2)
                Obank = banks[bcfg[qs]]; BO = Bbank[bcfg[qs]]
                Dbank = banks[bcfg[2 + qs]]; BD = Bbank[bcfg[2 + qs]]
                nkt = 4 * qb + 4
                LA = int(os.environ.get('B_LA', '4'))
                DEN_PE = os.environ.get('B_DEN', 'pe') == 'pe'
                for ch in range(2):
                    P.op("pool", lambda e, qs=qs, ch=ch: e.memset(Pacc[qs][ch], 0.0), writes=[BPacc[qs][ch]])
                info = {}

                def stage1(kt):
                    nonlocal pi
                    jd = kt - 4 * qb
                    c0 = 128 * jd if jd > 0 else 0
                    ps, Bps = ring()
                    mm([(ps[:, c0:512], [(Kt[hs][:, kt * 128:(kt + 1) * 128], qn[qs][:, c0:512]),
                                         (Kr[0:64, kt * 128:(kt + 1) * 128], qr[qs][0:64, c0:512])])],
                       [BK[hs], BKr, Bq[qs]], Bps)
                    p_ = pi % NP
                    pi += 1
                    P.op("act", lambda e, p_=p_, ps=ps, c0=c0: e.activation(Pt[p_][:, c0:512], ps[:, c0:512], AF.Exp), reads=[Bps], writes=[BPt[p_]])
                    if jd >= 0:
                        P.op("pool", lambda e, p_=p_, c0=c0: e.tensor_tensor(Pt[p_][:, c0:c0 + 128], Pt[p_][:, c0:c0 + 128], C["tri_b"], ALU.mult),
                             reads=BC, writes=[BPt[p_]])
                    info[kt] = (p_, c0)

                def stage2(kt):
                    p_, c0 = info.pop(kt)
                    first = (kt == 0); last = (kt == nkt - 1)

                    def fnO(e, p_=p_, c0=c0, first=first, last=last, kt=kt, Obank=Obank, hs=hs):
                        return e.matmul(Obank[:, c0:512], Vt[hs][:, kt, :], Pt[p_][:, c0:512], start=first, stop=last)
                    P.op("pe", fnO, reads=[BVt[hs], BPt[p_]], writes=[BO])

                    ch = kt % 2
                    if DEN_PE:
                        def fnD(e, p_=p_, c0=c0, first=first, last=last, Dbank=Dbank):
                            return e.matmul(Dbank[:, c0:512], C["ones_b"], Pt[p_][:, c0:512], start=first, stop=last)
                        P.op("pe", fnD, reads=[BPt[p_]] + BC, writes=[BD])
                    else:
                        P.op("dve", lambda e, p_=p_, c0=c0, ch=ch, qs=qs: e.tensor_tensor(Pacc[qs][ch][:, c0:512], Pacc[qs][ch][:, c0:512], Pt[p_][:, c0:512], ALU.add),
                             reads=[BPt[p_]], writes=[BPacc[qs][ch]])

                for kt in range(nkt):
                    stage1(kt)
                    if kt >= LA:
                        stage2(kt - LA)
                for kt in range(max(0, nkt - LA), nkt):
                    stage2(kt)
                def fnD(e, Dbank=Dbank, qs=qs):
                    e.matmul(Dbank, C["ones_f"], Pacc[qs][0], start=True, stop=False)
                    return e.matmul(Dbank, C["ones_f"], Pacc[qs][1], start=False, stop=True)
                if not DEN_PE:
                    P.op("pe", fnD, reads=BPacc[qs] + BC, writes=[BD])
                P.op("dve", lambda e, Dbank=Dbank: e.reciprocal(rec, Dbank), reads=[BD], writes=[Brec])
                os_ = qi % 2
                P.op("dve", lambda e, os_=os_, Obank=Obank: e.tensor_tensor(ost[os_], Obank, rec, ALU.mult), reads=[BO, Brec], writes=[Bost[os_]])
                P.dma("sp", MIXT[h * 128:(h + 1) * 128, q0:q0 + 512], ost[os_], Bost[os_], reads=[Bost[os_]], writes=[B_MIX], acc=True)

    def phaseC1(l, hin, Bhin, hmid, Bhmid):
        reset_phase()
        set_ring(range(8))
        wo = c3(8, 1024, BF16)
        Bw = P.buf("wC1")
        for k in range(8):
            P.dma("pool", wo[:, k, :], w_out[l, k * 128:(k + 1) * 128, :], Bw, writes=[Bw], acc=True)
        g1 = carve(1024); b1 = carve(1024)
        Bsm = P.buf("smallC1")
        P.dma("sp", g1, ln1_g[l].partition_broadcast(128), Bsm, writes=[Bsm], acc=True)
        P.dma("sp", b1, ln1_b[l].partition_broadcast(128), Bsm, writes=[Bsm], acc=True)
        mx = [c3(8, 512, BF16) for _ in range(2)]; Bmx = P.bufs(2)
        hx = [c3(4, 1024) for _ in range(2)]; Bhx = P.bufs(2)
        r1 = [carve(1024) for _ in range(2)]; Br1 = P.bufs(2)
        h1 = [carve(1024) for _ in range(2)]; Bh1 = P.bufs(2)
        tmps = [ln_tmp() for _ in range(2)]

        def load(ti):
            s = ti % 2
            t0 = ti * 512
            P.dma("sp", mx[s], MIXT[:, t0:t0 + 512].rearrange("(k p) t -> p k t", p=128), Bmx[s], reads=[B_MIX], writes=[Bmx[s]])
            P.dma("sp", hx[s], hin[t0:t0 + 512, :].rearrange("(j p) d -> p j d", p=128), Bhx[s], reads=[Bhin], writes=[Bhx[s]])

        if "C" in os.environ.get("SCHED_PH", "0ABCD"):
            P.begin()
        load(0)
        n = 0
        for ti in range(NT5):
            s = ti % 2
            t0 = ti * 512
            if ti + 1 < NT5:
                load(ti + 1)
            for j in range(4):
                u = n % 2
                n += 1
                for hf in range(2):
                    ps, Bps = ring()
                    mm([(ps, [(mx[s][:, k, j * 128:(j + 1) * 128], wo[:, k, hf * 512:(hf + 1) * 512]) for k in range(8)])], [Bw, Bmx[s]], Bps)
                    P.op("dve", lambda e, j=j, hf=hf, ps=ps, s=s, u=u: e.scalar_tensor_tensor(r1[u][:, hf * 512:(hf + 1) * 512], hx[s][:, j, hf * 512:(hf + 1) * 512], ALPHA, ps, ALU.mult, ALU.add),
                         reads=[Bhx[s], Bps], writes=[Br1[u]])
                ln_tile(r1[u], Br1[u], h1[u], Bh1[u], g1, b1, Bsm, tmps[u])
                P.dma("sp", hmid[t0 + j * 128:t0 + (j + 1) * 128, :], h1[u], Bh1[u], reads=[Bh1[u]], writes=[Bhmid], acc=True)

    def phaseC2(l, hin, Bhin, hout, Bhout):
        reset_phase()
        set_ring(range(8))
        ident = C["ident"]
        TW = 256
        wu = c3(8, 2 * D_FF, BF16); wd = c3(NFC, 1024, BF16)
        Bw = P.buf("wC2")
        for k in range(8):
            for hf in range(2):
                P.dma("pool", wu[:, k, hf * D_FF:(hf + 1) * D_FF], w_up[l, k * 128:(k + 1) * 128, hf * D_FF:(hf + 1) * D_FF], Bw, writes=[Bw], acc=True)
        for c in range(NFC):
            P.dma("pool", wd[:, c, :], w_down[l, c * 128:(c + 1) * 128, :], Bw, writes=[Bw], acc=True)
        g2 = carve(1024); b2 = carve(1024)
        wf = c3(NFC, 3); bf_ = carve(NFC)
        Bsm = P.buf("smallC2")
        P.dma("sp", g2, ln2_g[l].partition_broadcast(128), Bsm, writes=[Bsm], acc=True)
        P.dma("sp", b2, ln2_b[l].partition_broadcast(128), Bsm, writes=[Bsm], acc=True)
        P.dma("sp", wf, wfc[l], Bsm, writes=[Bsm], acc=True)
        P.dma("sp", bf_, bfc[l], Bsm, writes=[Bsm], acc=True)
        hx = [c3(2, 1024) for _ in range(2)]; Bhx = P.bufs(2)
        h1b = carve(1024, BF16); Bh1b = P.buf()
        h1T = c3(8, TW, BF16); Bh1T = P.bufs(2)
        uT = [c3(NFC, TW, BF16) for _ in range(2)]; BuT = [P.bufs(NFC) for _ in range(2)]
        carry = c3(NFC, 2); Bcar = P.bufs(NFC)
        NG = 3
        gb_ = [carve(TW + 2) for _ in range(NG)]; Bgb = P.bufs(NG)
        tt = [carve(TW) for _ in range(NG)]; Btt = P.bufs(NG)
        ss = [carve(TW) for _ in range(NG)]; Bss = P.bufs(NG)
        r2 = [carve(1024) for _ in range(2)]; Br2 = P.bufs(2)
        tmpB = [ln_tmp() for _ in range(2)]
        P.op("pool", lambda e: e.memset(carry, 0.0), writes=Bcar)
        NTC = S // TW
        gi = [0]

        def load(ti):
            s = ti % 2
            t0 = ti * TW
            P.dma("sp", hx[s], hin[t0:t0 + TW, :].rearrange("(j p) d -> p j d", p=128), Bhx[s], reads=[Bhin], writes=[Bhx[s]])

        def up(ti):
            s = ti % 2
            us = ti % 2
            stA = {}
            for j in range(2):
                P.op("act", lambda e, j=j, s=s: e.copy(h1b, hx[s][:, j, :]), reads=[Bhx[s]], writes=[Bh1b])
                ps, Bps = ring()
                psb = ps.bitcast(BF16)
                tr([(psb[:, k * 128:(k + 1) * 128], h1b[:, k * 128:(k + 1) * 128]) for k in range(8)], [Bh1b] + BC, Bps, ident)
                P.op("act", lambda e, j=j, psb=psb: e.copy(h1T[:, :, j * 128:(j + 1) * 128], psb.rearrange("p (k t) -> p k t", t=128)),
                     reads=[Bps], writes=[Bh1T[j]])

            def s_mm(c):
                ps, Bps = ring()
                mm([(ps[:, 0:TW], [(wu[:, k, c * 128:(c + 1) * 128], h1T[:, k, :]) for k in range(8)]),
                    (ps[:, TW:2 * TW], [(wu[:, k, D_FF + c * 128:D_FF + (c + 1) * 128], h1T[:, k, :]) for k in range(8)])],
                   [Bw] + Bh1T, Bps)
                g_ = gi[0] % NG
                gi[0] += 1
                stA[c] = (ps, Bps, g_)

            def s_copy(c):
                ps, Bps, g_ = stA[c]
                gbv = gb_[g_]
                P.op("pool", lambda e, gbv=gbv, c=c: e.tensor_copy(gbv[:, 0:2], carry[:, c, :]), reads=[Bcar[c]], writes=[Bgb[g_]])
                P.op("act", lambda e, gbv=gbv, ps=ps: e.copy(gbv[:, 2:TW + 2], ps[:, TW:2 * TW]), reads=[Bps], writes=[Bgb[g_]])
                P.op("pool", lambda e, gbv=gbv, c=c: e.tensor_copy(carry[:, c, :], gbv[:, TW:TW + 2]), reads=[Bgb[g_]], writes=[Bcar[c]])

            def s_conv(c):
                ps, Bps, g_ = stA[c]
                gbv = gb_[g_]; tv = tt[g_]
                P.op("dve", lambda e, tv=tv, gbv=gbv, c=c: e.tensor_scalar(tv, gbv[:, 0:TW], wf[:, c, 0:1], bf_[:, c:c + 1], ALU.mult, ALU.add),
                     reads=[Bgb[g_], Bsm], writes=[Btt[g_]])
                for k in (1, 2):
                    P.op("dve", lambda e, tv=tv, gbv=gbv, c=c, k=k: e.scalar_tensor_tensor(tv, gbv[:, k:k + TW], wf[:, c, k:k + 1], tv, ALU.mult, ALU.add),
                         reads=[Bgb[g_], Bsm], writes=[Btt[g_]])

            def s_silu(c):
                ps, Bps, g_ = stA[c]
                P.op("act", lambda e, sv=ss[g_], tv=tt[g_]: e.activation(sv, tv, AF.Silu), reads=[Btt[g_]], writes=[Bss[g_]])

            def s_mult(c):
                ps, Bps, g_ = stA[c]
                P.op("dve", lambda e, sv=ss[g_], ps=ps, c=c: e.tensor_tensor(uT[us][:, c, :], ps[:, 0:TW], sv, ALU.mult), reads=[Bps, Bss[g_]], writes=[BuT[us][c]])

            for step in range(NFC + 3):
                if step < NFC:
                    s_mm(step)
                    s_copy(step)
                if 0 <= step - 1 < NFC:
                    s_conv(step - 1)
                if 0 <= step - 2 < NFC:
                    s_silu(step - 2)
                if 0 <= step - 3 < NFC:
                    s_mult(step - 3)

        def down(ti):
            us = ti % 2
            t0 = ti * TW
            for j in range(2):
                P.dma("sp", r2[j], hin[t0 + j * 128:t0 + (j + 1) * 128, :], Br2[j], reads=[Bhin], writes=[Br2[j]])
                for hf in range(2):
                    ps, Bps = ring()
                    mm([(ps, [(uT[us][:, c, j * 128:(j + 1) * 128], wd[:, c, hf * 512:(hf + 1) * 512]) for c in range(NFC)])], [Bw] + BuT[us], Bps)
                    P.op("dve", lambda e, j=j, hf=hf, ps=ps: e.scalar_tensor_tensor(r2[j][:, hf * 512:(hf + 1) * 512], r2[j][:, hf * 512:(hf + 1) * 512], ALPHA, ps, ALU.mult, ALU.add),
                         reads=[Bps], writes=[Br2[j]])
                ln_tile(r2[j], Br2[j], r2[j], Br2[j], g2, b2, Bsm, tmpB[j])
                P.dma("sp", hout[t0 + j * 128:t0 + (j + 1) * 128, :], r2[j], Br2[j], reads=[Br2[j]], writes=[Bhout], acc=True)

        if "D" in os.environ.get("SCHED_PH", "0ABCD"):
            P.begin()
        load(0)
        for ti in range(NTC):
            if ti + 1 < NTC:
                load(ti + 1)
            up(ti)
            if ti >= 1:
                down(ti - 1)
        down(NTC - 1)

    if "0" in phases:
        phase0(hA, B_hA)
    B_y = P.buf("y")
    for l in range(L):
        if "A" in phases:
            phaseA(l, hA, B_hA)
        if "B" in phases:
            phaseB(l)
        if "C" in phases:
            phaseC1(l, hA, B_hA, hB, B_hB)
        if "D" in phases:
            if l == L - 1:
                phaseC2(l, hB, B_hB, y, B_y)
            else:
                phaseC2(l, hB, B_hB, hA, B_hA)
    P.barrier()
    P.emit()
    return nc


def prep_weights(inp, S, L):
    f = np.float32
    g = lambda k: np.asarray(inp[k], dtype=f)
    d = {}
    d["ln0_g"] = g("ln0_g"); d["ln0_b"] = g("ln0_b")
    w_in = g("w_in")[:L]
    kr = w_in[:, :, 640:704]
    krsw = np.concatenate([kr[:, :, 32:64], kr[:, :, 0:32]], axis=-1)
    d["w_in_e"] = np.ascontiguousarray(np.concatenate([w_in, krsw], axis=-1))
    w_uq = g("w_uq")[:L].reshape(L, 384, 4, 192)
    nope = w_uq[..., 0:128]; rope = w_uq[..., 128:192]
    ropesw = np.concatenate([rope[..., 32:64], rope[..., 0:32]], axis=-1)
    d["w_uq_e"] = np.ascontiguousarray(np.concatenate([nope, rope, ropesw], axis=-1).reshape(L, 384, 1024))
    w_ukv = g("w_ukv")[:L].reshape(L, 256, 4, 256)
    d["w_k"] = np.ascontiguousarray(w_ukv[..., 0:128].reshape(L, 256, 512))
    d["w_v"] = np.ascontiguousarray(w_ukv[..., 128:256].reshape(L, 256, 512))

    def blockdiag(w):
        o = np.zeros((L, 128, 2, 128), f)
        for h in range(4):
            c, a = h // 2, (h % 2) * 64
            o[:, a:a + 64, c, a:a + 64] = w[:, h]
        return o
    d["bdp"] = blockdiag(g("w_pool")[:L]); d["bdq"] = blockdiag(g("w_mq")[:L]); d["bdk"] = blockdiag(g("w_mk")[:L])

    def pk(v, n):
        return np.ascontiguousarray(v.reshape(L, n, 128).transpose(0, 2, 1))
    d["gq"] = pk(g("g_qn")[:L], 3); d["gkv"] = pk(g("g_kvn")[:L], 2); d["spool"] = pk(g("s_pool")[:L], 2)
    d["wmc"] = np.ascontiguousarray(g("w_mconv")[:L].reshape(L, 4, 2, 128).transpose(0, 3, 2, 1))
    d["bmc"] = pk(g("b_mconv")[:L], 2)
    d["bif"] = np.ascontiguousarray(np.concatenate([g("b_i")[:L], g("b_f")[:L]], axis=-1))
    d["gmh"] = g("g_mh")[:L]
    d["w_out"] = g("w_out")[:L]; d["ln1_g"] = g("ln1_g")[:L]; d["ln1_b"] = g("ln1_b")[:L]
    d["w_up"] = g("w_up")[:L]
    d["wfc"] = np.ascontiguousarray(g("w_fconv")[:L].reshape(L, 3, NFC, 128).transpose(0, 3, 2, 1))
    d["bfc"] = pk(g("b_fconv")[:L], NFC)
    d["w_down"] = g("w_down")[:L]; d["ln2_g"] = g("ln2_g")[:L]; d["ln2_b"] = g("ln2_b")[:L]
    d["c_ident"] = np.eye(128, dtype=f)
    d["c_tri"] = np.triu(np.ones((128, 128), f))
    inv = (1.0 / (10000.0 ** (np.arange(0, 64, 2, dtype=f) / f(64)))).astype(f)
    ang = np.arange(S, dtype=f)[:, None] * inv[None, :]
    cos = np.cos(ang).astype(f).T; sin = np.sin(ang).astype(f).T
    d["c_cs"] = np.ascontiguousarray(np.concatenate([cos, cos], axis=0))
    d["c_sn"] = np.ascontiguousarray(np.concatenate([-sin, sin], axis=0))
    pinv = np.zeros((2, 128, 2, 512), f)
    wins = (2, 4, 8, 16)
    t = np.arange(512)
    for gi in range(4):
        c, a = gi // 2, (gi % 2) * 64
        pinv[0, a:a + 64, c, :] = (1.0 / np.minimum(t + 1, wins[gi]).astype(f))[None, :]
        pinv[1, a:a + 64, c, :] = f(1.0) / f(wins[gi])
    d["c_pinv"] = pinv
    return d


_CACHE = {}


def kernel(**inputs):
    x = np.asarray(inputs["x"], dtype=np.float32)
    B, S, D = x.shape
    L = DEPTH_FULL
    key = (S, L)
    if key not in _CACHE:
        _CACHE[key] = build(S, L)
    nc = _CACHE[key]
    w = prep_weights(inputs, S, L)
    in_maps = []
    for b in range(B):
        m = dict(w)
        m["x"] = np.ascontiguousarray(x[b])
        in_maps.append(m)
    res = run_bass_kernel_spmd(nc, in_maps, core_ids=list(range(B)))
    return np.stack([np.asarray(r["y"], dtype=np.float32) for r in res.results], axis=0)
```

```python
import contextlib
import os
import numpy as np
import concourse.bass as bass
import concourse.mybir as mybir
from concourse.bass_utils import run_bass_kernel_spmd

F32 = mybir.dt.float32
BF16 = mybir.dt.bfloat16
ALU = mybir.AluOpType
AF = mybir.ActivationFunctionType
AX = mybir.AxisListType

ENGS = ("pe", "act", "dve", "pool", "sp")

D_MODEL = 1024
DEPTH_FULL = 4
Q_LORA, KV_LORA, MLA_ROPE = 384, 256, 64
D_IN = 1736
D_FF = 2816
NFC = D_FF // 128
LN_EPS = 1e-5
RMS_EPS = 1e-6
ALPHA = float((2 * DEPTH_FULL) ** 0.25)
QSCALE = float((128 + 64) ** -0.5)
WINE = 1800


class Buf:
    __slots__ = ("name", "w", "r", "dsem")

    def __init__(self, name):
        self.name = name
        self.w = {}
        self.r = {}
        self.dsem = None


class _CostStub:
    def __init__(self, per_elem):
        self.t = 0.0
        self.k = per_elem

    def __getattr__(self, name):
        def f(*a, **kw):
            el = 1
            for x in list(a[:2]) + [kw.get("out"), kw.get("in_")]:
                sh = getattr(x, "shape", None)
                if sh is not None and len(sh) >= 2:
                    n = 1
                    for d in sh[1:]:
                        n *= int(d)
                    el = max(el, n)
            mult = 8.0 if name == "reciprocal" else 1.0
            self.t += 0.12 + el * mult * self.k
            return self
        return f


class Prog:
    def __init__(self, nc):
        self.nc = nc
        self.ops = {e: [] for e in ENGS}
        self.tot = {e: 0 for e in ENGS}
        self.seen = {e: {} for e in ENGS}
        self.dsems = []
        self.free_dsems = []
        self.epoch = 0
        self.nbuf = 0

    def buf(self, name=None):
        self.nbuf += 1
        return Buf(name or "b%d" % self.nbuf)

    def bufs(self, n):
        return [self.buf() for _ in range(n)]

    def _need(self, eng, deps):
        out = {}
        seen = self.seen[eng]
        for (k, v) in deps:
            if seen.get(k, 0) >= v:
                continue
            if out.get(k, 0) < v:
                out[k] = v
        for k, v in out.items():
            seen[k] = v
        return out

    def _track(self, eng, me, reads, writes, acc):
        deps = []
        for b in reads:
            deps += list(b.w.items())
        for b in writes:
            deps += list(b.r.items())
            if not acc:
                deps += list(b.w.items())
        waits = self._need(eng, deps)
        k, v = me
        for b in reads:
            if b.r.get(k, 0) < v:
                b.r[k] = v
        for b in writes:
            if acc:
                if b.w.get(k, 0) < v:
                    b.w[k] = v
            else:
                b.w = {k: v}
                b.r = {}
        return waits

    def begin(self):
        if os.environ.get("SCHED", "1") == "1":
            self.rec = []

    def end(self):
        rec = self.rec
        self.rec = None
        if not rec:
            return
        n = len(rec)
        lastw, readers = {}, {}
        preds = [set() for _ in range(n)]
        for i, r in enumerate(rec):
            reads, writes, acc = r[2], r[3], r[4]
            for b in reads:
                preds[i].update(lastw.get(id(b), ()))
            for b in writes:
                preds[i].update(readers.get(id(b), ()))
                preds[i].update(lastw.get(id(b), ()))
            for b in reads:
                readers.setdefault(id(b), []).append(i)
            for b in writes:
                if acc:
                    lastw.setdefault(id(b), []).append(i)
                else:
                    lastw[id(b)] = [i]
                    readers[id(b)] = []
            preds[i].discard(i)
        succs = [[] for _ in range(n)]
        indeg = [0] * n
        for i in range(n):
            indeg[i] = len(preds[i])
            for p in preds[i]:
                succs[p].append(i)
        rt = [0.0] * n
        free = {e: 0.0 for e in ENGS}
        ready = [i for i in range(n) if indeg[i] == 0]
        order = []
        HOP = 0.25
        while ready:
            best, bs = None, None
            for i in ready:
                st = max(rt[i], free[rec[i][1]])
                key = (st, i)
                if bs is None or key < bs:
                    best, bs = i, key
            ready.remove(best)
            r = rec[best]
            st = bs[0]
            busy, lat = r[5]
            free[r[1]] = st + busy
            fin = st + lat
            order.append(best)
            for j in succs[best]:
                if rt[j] < fin + HOP:
                    rt[j] = fin + HOP
                indeg[j] -= 1
                if indeg[j] == 0:
                    ready.append(j)
        assert len(order) == n
        for i in order:
            r = rec[i]
            if r[0] == "op":
                self.op(r[1], r[6], r[2], r[3])
            else:
                self.dma(r[1], r[6], r[7], r[8], r[2], r[3], r[4])

    rec = None

    def op(self, eng, fn, reads=(), writes=(), cost=None):
        if self.rec is not None:
            c = cost
            if c is None:
                try:
                    stub = _CostStub({"act": 0.00083, "dve": 0.00104, "pool": 0.0023}.get(eng, 0.001))
                    fn(stub)
                    c = stub.t if stub.t > 0 else 0.3
                except Exception:
                    c = {"pe": 0.3, "act": 0.5, "dve": 0.5, "pool": 0.7, "sp": 0.5}[eng]
            self.rec.append(("op", eng, tuple(reads), tuple(writes), False, (c, c), fn))
            return
        self.tot[eng] += 1
        me = (eng, self.tot[eng])
        waits = self._track(eng, me, reads, writes, False)
        self.ops[eng].append((waits, fn, (eng, 1)))
        return me

    def dma(self, q, out_ap, in_ap, sb, reads=(), writes=(), acc=False):
        if self.rec is not None:
            self.rec.append(("dma", q, tuple(reads), tuple(writes), acc, (0.1, 2.5), out_ap, in_ap, sb))
            return
        if sb.dsem is None or sb.dsem[2] != self.epoch:
            if self.free_dsems:
                kk = self.free_dsems.pop()
            else:
                kk = ("d", len(self.dsems))
                self.dsems.append(kk)
                self.tot[kk] = 0
            sb.dsem = (kk[0], kk[1], self.epoch)
        key = (sb.dsem[0], sb.dsem[1])
        self.tot[key] += 16
        me = (key, self.tot[key])
        waits = self._track(q, me, reads, writes, acc)

        def fn(e, out_ap=out_ap, in_ap=in_ap):
            return e.dma_start(out=out_ap, in_=in_ap)
        self.ops[q].append((waits, fn, (key, 16)))
        return me

    def barrier(self):
        if self.rec is not None:
            self.end()
        allk = [(k, v) for k, v in self.tot.items() if v > 0]
        for e in ENGS:
            waits = self._need(e, allk)
            if waits:
                self.ops[e].append((waits, None, None))
        self.epoch += 1
        self.free_dsems = list(self.dsems)

    def emit(self):
        nc = self.nc
        with contextlib.ExitStack() as st:
            sems = {}
            for e in ENGS:
                sems[e] = st.enter_context(nc.semaphore("s_" + e))
            for k in self.dsems:
                sems[k] = st.enter_context(nc.semaphore("d%d" % k[1]))
            block = st.enter_context(nc.Block())
            handles = {"pe": block.tensor, "act": block.scalar, "dve": block.vector,
                       "pool": block.gpsimd, "sp": block.sync}

            def mk(ename):
                def body(eng):
                    for (waits, fn, inc) in self.ops[ename]:
                        for k, v in waits.items():
                            eng.wait_ge(sems[k], v)
                        if fn is None:
                            continue
                        ins = fn(eng)
                        ins.then_inc(sems[inc[0]], inc[1])
                return body
            for e in ENGS:
                handles[e](mk(e))


def build(S, DEPTH, dbg=False, phases="0ABCD"):
    nc = bass.Bass("TRN2", target_bir_lowering=False)
    P = Prog(nc)
    NT = S // 128
    NT5 = S // 512
    L = DEPTH

    def din(name, shape):
        return nc.dram_tensor(name, list(shape), F32, kind="ExternalInput").ap()

    def dscr(name, shape, dt):
        kind = "ExternalOutput" if dbg else "Internal"
        return nc.dram_tensor(name, list(shape), dt, kind=kind).ap()

    x = din("x", [S, 1024])
    ln0_g = din("ln0_g", [1024]); ln0_b = din("ln0_b", [1024])
    w_in = din("w_in_e", [L, 1024, WINE])
    w_uq = din("w_uq_e", [L, 384, 1024])
    w_k = din("w_k", [L, 256, 512]); w_v = din("w_v", [L, 256, 512])
    bdp = din("bdp", [L, 128, 2, 128]); bdq = din("bdq", [L, 128, 2, 128]); bdk = din("bdk", [L, 128, 2, 128])
    gq = din("gq", [L, 128, 3]); gkv = din("gkv", [L, 128, 2]); spool = din("spool", [L, 128, 2])
    wmc = din("wmc", [L, 128, 2, 4]); bmc = din("bmc", [L, 128, 2])
    bif = din("bif", [L, 8]); gmh = din("gmh", [L, 256])
    w_out = din("w_out", [L, 1024, 1024])
    ln1_g = din("ln1_g", [L, 1024]); ln1_b = din("ln1_b", [L, 1024])
    w_up = din("w_up", [L, 1024, 2 * D_FF])
    wfc = din("wfc", [L, 128, NFC, 3]); bfc = din("bfc", [L, 128, NFC])
    w_down = din("w_down", [L, D_FF, 1024])
    ln2_g = din("ln2_g", [L, 1024]); ln2_b = din("ln2_b", [L, 1024])
    c_ident = din("c_ident", [128, 128]); c_tri = din("c_tri", [128, 128])
    c_cs = din("c_cs", [64, S]); c_sn = din("c_sn", [64, S])
    c_pinv = din("c_pinv", [2, 128, 2, 512])
    y = nc.dram_tensor("y", [S, 1024], F32, kind="ExternalOutput").ap()

    hA = dscr("hA", [S, 1024], F32); hB = dscr("hB", [S, 1024], F32)
    QN = dscr("QN", [4, 128, S], BF16); QR = dscr("QR", [4, 64, S], BF16)
    KT = dscr("KT", [4, 128, S], BF16); KR = dscr("KR", [64, S], BF16)
    VV = dscr("VV", [S, 512], BF16); MIXT = dscr("MIXT", [1024, S], BF16)
    B_hA, B_hB, B_QN, B_QR, B_KT, B_KR, B_VV, B_MIX = P.bufs(8)

    arena = nc.alloc_sbuf_tensor("arena", [128, 53000], F32)
    ps_all = nc.alloc_psum_tensor("ps", [128, 4096], F32)
    off = [0]

    def reset():
        P.barrier()
        off[0] = 0

    def carve(n, dt=F32):
        nb = n * (2 if dt == BF16 else 4)
        nw = (nb + 3) // 4
        nw = (nw + 7) // 8 * 8
        a = arena[:, off[0]:off[0] + nw]
        off[0] += nw
        assert off[0] <= 53000, "SBUF overflow %d" % off[0]
        if dt == BF16:
            return a.bitcast(BF16)[:, 0:n]
        return a[:, 0:n]

    def c3(n1, n2, dt=F32):
        return carve(n1 * n2, dt).rearrange("p (a b) -> p a b", b=n2)

    banks = [ps_all[:, i * 512:(i + 1) * 512] for i in range(8)]
    Bbank = P.bufs(8)
    ring_state = {"i": 0, "lst": list(range(8))}

    def ring():
        lst = ring_state["lst"]
        i = lst[ring_state["i"] % len(lst)]
        ring_state["i"] += 1
        return banks[i], Bbank[i]

    def set_ring(lst):
        ring_state["lst"] = list(lst)
        ring_state["i"] = 0

    def mm(groups, reads, Bps):
        def fn(e, groups=groups):
            ins = None
            for (o, pairs) in groups:
                n = len(pairs)
                for i, (l, r) in enumerate(pairs):
                    ins = e.matmul(o, l, r, start=(i == 0), stop=(i == n - 1))
            return ins
        cst = 0.0
        for (o, pairs) in groups:
            cst += len(pairs) * (o.shape[-1] / 2400.0 + 0.04)
        P.op("pe", fn, reads=reads, writes=[Bps], cost=cst)

    def tr(groups, reads, Bps, ident):
        def fn(e, groups=groups):
            ins = None
            for (o, i_) in groups:
                ins = e.transpose(o, i_, ident)
            return ins
        P.op("pe", fn, reads=reads, writes=[Bps], cost=0.1 * len(groups))

    CONST_W = 0

    def load_consts():
        ident = carve(128, BF16); tri_b = carve(128, BF16); tri_f = carve(128)
        ones_b = carve(128, BF16); ones_f = carve(128); nh = carve(512)
        Bc = P.buf("consts")
        P.dma("pool", ident, c_ident, Bc, writes=[Bc], acc=True)
        P.dma("pool", tri_b, c_tri, Bc, writes=[Bc], acc=True)
        P.dma("sp", tri_f, c_tri, Bc, writes=[Bc], acc=True)
        B1 = P.buf()
        P.op("pool", lambda e: e.memset(ones_b, 1.0), writes=[B1])
        B2 = P.buf()
        P.op("pool", lambda e: e.memset(ones_f, 1.0), writes=[B2])
        B3 = P.buf()
        P.op("pool", lambda e: e.memset(nh, -0.5), writes=[B3])
        return dict(ident=ident, tri_b=tri_b, tri_f=tri_f, ones_b=ones_b, ones_f=ones_f, nh=nh,
                    B=[Bc, B1, B2, B3])

    C = load_consts()
    const_off = off[0]
    BC = C["B"]

    def reset_phase():
        P.barrier()
        off[0] = const_off

    def ln_tile(src, Bsrc, dst, Bdst, gb, bb, Bgb, tmp):
        st, mv, ve, rs, Bst0, Bst1, Bmv, Bve, Brs = tmp
        P.op("dve", lambda e: e.bn_stats(st[:, 0:6], src[:, 0:512]), reads=[Bsrc], writes=[Bst0])
        P.op("dve", lambda e: e.bn_stats(st[:, 6:12], src[:, 512:1024]), reads=[Bsrc], writes=[Bst1])
        P.op("dve", lambda e: e.bn_aggr(mv, st), reads=[Bst0, Bst1], writes=[Bmv])
        P.op("dve", lambda e: e.tensor_scalar_add(ve, mv[:, 1:2], LN_EPS), reads=[Bmv], writes=[Bve])
        P.op("pool", lambda e: e.tensor_tensor(rs, ve, C["nh"][:, 0:1], ALU.pow), reads=[Bve] + BC, writes=[Brs])
        P.op("dve", lambda e: e.tensor_scalar(dst, src, mv[:, 0:1], rs, ALU.subtract, ALU.mult),
             reads=[Bsrc, Bmv, Brs], writes=[Bdst])
        P.op("pool", lambda e: e.tensor_tensor(dst, dst, gb, ALU.mult), reads=[Bgb], writes=[Bdst])
        P.op("pool", lambda e: e.tensor_tensor(dst, dst, bb, ALU.add), reads=[Bgb], writes=[Bdst])

    def ln_tmp():
        return (carve(12), carve(2), carve(1), carve(1)) + tuple(P.bufs(5))

    def phase0(dst, Bdst_dram):
        reset_phase()
        gb = carve(1024); bb = carve(1024); Bgb = P.buf()
        P.dma("sp", gb, ln0_g.partition_broadcast(128), Bgb, writes=[Bgb], acc=True)
        P.dma("sp", bb, ln0_b.partition_broadcast(128), Bgb, writes=[Bgb], acc=True)
        xt = [carve(1024) for _ in range(2)]; Bx = P.bufs(2)
        yt = [carve(1024) for _ in range(2)]; By = P.bufs(2)
        tmps = [ln_tmp() for _ in range(2)]
        if "0" in os.environ.get("SCHED_PH", "0ACD"):
            P.begin()
        for i in range(NT):
            s = i % 2
            P.dma("sp", xt[s], x[i * 128:(i + 1) * 128, :], Bx[s], writes=[Bx[s]])
            ln_tile(xt[s], Bx[s], yt[s], By[s], gb, bb, Bgb, tmps[s])
            P.dma("sp", dst[i * 128:(i + 1) * 128, :], yt[s], By[s], reads=[By[s]], writes=[Bdst_dram], acc=True)

    def phaseA(l, hin, Bhin):
        reset_phase()
        set_ring(range(8))
        ident = C["ident"]
        win = c3(8, WINE, BF16); wuq = c3(3, 1024, BF16); wk = c3(2, 512, BF16); wv = c3(2, 512, BF16)
        bp = c3(2, 128, BF16); bq = c3(2, 128, BF16); bk = c3(2, 128, BF16)
        Bw = P.buf("wA")
        for k in range(8):
            P.dma("pool", win[:, k, :], w_in[l, k * 128:(k + 1) * 128, :], Bw, writes=[Bw], acc=True)
        P.dma("pool", wuq, w_uq[l].rearrange("(k p) n -> p k n", p=128), Bw, writes=[Bw], acc=True)
        P.dma("pool", wk, w_k[l].rearrange("(k p) n -> p k n", p=128), Bw, writes=[Bw], acc=True)
        P.dma("pool", wv, w_v[l].rearrange("(k p) n -> p k n", p=128), Bw, writes=[Bw], acc=True)
        P.dma("pool", bp, bdp[l], Bw, writes=[Bw], acc=True)
        P.dma("pool", bq, bdq[l], Bw, writes=[Bw], acc=True)
        P.dma("pool", bk, bdk[l], Bw, writes=[Bw], acc=True)
        gq_t = carve(3); gkv_t = carve(2); sp_t = carve(2); wmc_t = c3(2, 4); bmc_t = carve(2)
        bif_t = carve(8); gmh_t = carve(256); pinv = [c3(2, 512) for _ in range(2)]
        Bsm = P.buf("smallA")
        P.dma("sp", gq_t, gq[l], Bsm, writes=[Bsm], acc=True)
        P.dma("sp", gkv_t, gkv[l], Bsm, writes=[Bsm], acc=True)
        P.dma("sp", sp_t, spool[l], Bsm, writes=[Bsm], acc=True)
        P.dma("sp", wmc_t, wmc[l], Bsm, writes=[Bsm], acc=True)
        P.dma("sp", bmc_t, bmc[l], Bsm, writes=[Bsm], acc=True)
        P.dma("sp", bif_t, bif[l].partition_broadcast(128), Bsm, writes=[Bsm], acc=True)
        P.dma("sp", gmh_t, gmh[l].partition_broadcast(128), Bsm, writes=[Bsm], acc=True)
        P.dma("sp", pinv[0], c_pinv[0], Bsm, writes=[Bsm], acc=True)
        P.dma("sp", pinv[1], c_pinv[1], Bsm, writes=[Bsm], acc=True)
        gqs = carve(3); Bgqs = P.buf()
        P.op("dve", lambda e: e.tensor_scalar_mul(gqs, gq_t, QSCALE), reads=[Bsm], writes=[Bgqs])

        if int(os.environ.get('ASTOP', '99')) == 1:
            return
        hx = c3(4, 1024); Bhx = P.buf()
        hb = c3(4, 1024, BF16); Bhb = P.bufs(2)
        hT = c3(8, 512, BF16); BhT = P.bufs(4)
        cs_t = [carve(512) for _ in range(1)]; sn_t = [carve(512) for _ in range(1)]; Bcs = P.bufs(1)
        craw = c3(5, 512); Bcraw = P.bufs(5)
        sq = c3(5, 512, BF16); Bsq = P.bufs(5)
        cn = c3(5, 512, BF16); Bcn = P.bufs(5)
        rr = c3(2, 512); Brr = P.bufs(2)
        t1 = [carve(512) for _ in range(1)]; t2 = [carve(512) for _ in range(1)]; Bt1 = P.bufs(1); Bt2 = P.bufs(1)
        xpb = c3(2, 528); Bxp = P.buf(); T1 = c3(2, 528); T2 = c3(2, 528); BT1 = P.buf(); BT2 = P.buf()
        ybf = c3(2, 512, BF16); Bybf = P.buf()
        xqb = c3(2, 515); Bxq = P.buf(); acc = c3(2, 512); Bacc = P.bufs(2)
        cT = c3(2, 512, BF16); BcT = P.bufs(2)
        mqT = c3(2, 512, BF16); mkTe = c3(2, 512, BF16); mkTo = c3(2, 512, BF16); BmqT = P.bufs(2); BmkT = P.bufs(2)
        mktok = c3(4, 256, BF16); Bmktok = P.bufs(4)
        xvo = c3(4, 512); Bxvo = P.bufs(4)
        gz = c3(4, 8); Bgz = P.bufs(4)
        QNs = [c3(4, 512, BF16) for _ in range(1)]; QRs = [c3(4, 512, BF16) for _ in range(1)]
        KTs = [c3(4, 512, BF16) for _ in range(1)]; KRs = [carve(512, BF16) for _ in range(1)]
        Vs = [c3(4, 512, BF16) for _ in range(1)]; PTs = [c3(2, 512, BF16) for _ in range(1)]
        MTs = [c3(2, 512, BF16) for _ in range(1)]
        BQN = P.bufs(2); BQR = P.bufs(2); BKT = P.bufs(2); BKR = P.bufs(2); BV = P.bufs(2); BPT = P.bufs(2); BMT = P.bufs(2)
        Sf = c3(2, 66); BSf = P.buf(); Sbe = [c3(2, 66, BF16) for _ in range(2)]; Sbo = [c3(2, 66, BF16) for _ in range(2)]; BSb = P.bufs(2)
        Stmp = c3(2, 66); BStmp = P.buf()
        vaug = [c3(4, 66, BF16) for _ in range(4)]; Bvaug = P.bufs(4)
        scm = [c3(4, 128, BF16) for _ in range(4)]; Bscm = P.bufs(4)
        sm = [carve(64) for _ in range(4)]; Bsm2 = [P.bufs(12) for _ in range(4)]
        hh = [c3(4, 64) for _ in range(4)]; Bhh = P.bufs(4); hc = [c3(4, 64) for _ in range(4)]; Bhc = P.bufs(4)
        hsq = [c3(4, 64) for _ in range(4)]; Bhsq = P.bufs(4)
        sig = [carve(256) for _ in range(4)]; Bsig = P.bufs(4)
        mtok = [carve(256, BF16) for _ in range(4)]; Bmtok = P.bufs(4)
        afull = [carve(2) for _ in range(4)]; Baf = P.bufs(4)

        P.op("dve", lambda e: e.memset(Sf, 0.0), writes=[BSf])
        for i_ in range(2):
            P.op("pool", lambda e, i_=i_: e.memset(Sbe[i_], 0.0), writes=[BSb[i_]])
            P.op("pool", lambda e, i_=i_: e.memset(Sbo[i_], 0.0), writes=[BSb[i_]])
        P.op("pool", lambda e: e.memset(mkTe, 0.0), writes=BmkT)
        P.op("pool", lambda e: e.memset(mkTo, 0.0), writes=BmkT)
        P.op("pool", lambda e: e.memset(xqb[:, :, 0:3], 0.0), writes=[Bxq])
        P.op("pool", lambda e: e.memset(xpb[:, :, 0:16], 0.0), writes=[Bxp])
        sb_i = [0]

        def proj_fm(col0, ncols):
            ps, Bps = ring()
            mm([(ps[0:ncols, :], [(win[:, k, col0:col0 + ncols], hT[:, k, :]) for k in range(8)])],
               [Bw] + BhT, Bps)
            return ps, Bps

        if "A" in os.environ.get("SCHED_PH", "0ACD"):
            P.begin()
        for ti in range(NT5):
            t0 = ti * 512
            s2 = 0
            P.dma("sp", hx, hin[t0:t0 + 512, :].rearrange("(j p) d -> p j d", p=128), Bhx,
                  reads=[Bhin], writes=[Bhx])
            P.dma("sp", cs_t[s2][0:64, :], c_cs[:, t0:t0 + 512], Bcs[s2], writes=[Bcs[s2]], acc=False)
            P.dma("sp", sn_t[s2][0:64, :], c_sn[:, t0:t0 + 512], Bcs[s2], reads=[], writes=[Bcs[s2]], acc=True)
            cs = cs_t[s2][0:64, :]; sn = sn_t[s2][0:64, :]
            P.op("act", lambda e: e.copy(hb[:, 0:2, :], hx[:, 0:2, :]), reads=[Bhx], writes=[Bhb[0]])
            P.op("dve", lambda e: e.tensor_copy(hb[:, 2:4, :], hx[:, 2:4, :]), reads=[Bhx], writes=[Bhb[1]])
            for kp in range(4):
                ps, Bps = ring()
                psb = ps.bitcast(BF16)
                groups = []
                for kk in range(2):
                    k = 2 * kp + kk
                    for j in range(4):
                        groups.append((psb[:, kk * 512 + j * 128: kk * 512 + (j + 1) * 128],
                                       hb[:, j, k * 128:(k + 1) * 128]))
                tr(groups, Bhb + BC, Bps, ident)
                dstv = hT[:, 2 * kp:2 * kp + 2, :]
                srcv = psb.rearrange("p (a b) -> p a b", b=512)
                if kp % 2 == 0:
                    P.op("act", lambda e, d=dstv, s_=srcv: e.copy(d, s_), reads=[Bps], writes=[BhT[kp]])
                else:
                    P.op("dve", lambda e, d=dstv, s_=srcv: e.tensor_copy(d, s_), reads=[Bps], writes=[BhT[kp]])
            if int(os.environ.get('ASTOP', '99')) == 2:
                return
            for c in range(5):
                ps, Bps = proj_fm(c * 128, 128)
                P.op("act", lambda e, c=c, ps=ps: e.copy(craw[:, c, :], ps), reads=[Bps], writes=[Bcraw[c]])
                P.op("act", lambda e, c=c, ps=ps: e.activation(sq[:, c, :], ps, AF.Square), reads=[Bps], writes=[Bsq[c]])
            for (i, cl, n) in ((0, (0, 1, 2), 384.0), (1, (3, 4), 256.0)):
                ps, Bps = ring()
                mm([(ps, [(C["ones_b"], sq[:, c, :]) for c in cl])], [Bsq[c] for c in cl] + BC, Bps)
                P.op("dve", lambda e, i=i, ps=ps, n=n: e.tensor_scalar(rr[:, i, :], ps, 1.0 / n, RMS_EPS, ALU.mult, ALU.add),
                     reads=[Bps], writes=[Brr[i]])
                P.op("act", lambda e, i=i: e.activation(rr[:, i, :], rr[:, i, :], AF.Sqrt), reads=[Brr[i]], writes=[Brr[i]])
                P.op("dve", lambda e, i=i: e.reciprocal(rr[:, i, :], rr[:, i, :]), reads=[Brr[i]], writes=[Brr[i]])
            if int(os.environ.get('ASTOP', '99')) == 3:
                return
            psa, Bpsa = proj_fm(640, 64)
            psb_, Bpsb = proj_fm(1736, 64)
            P.op("dve", lambda e, psa=psa: e.tensor_tensor(t1[0][0:64, :], psa[0:64, :], cs, ALU.mult), reads=[Bpsa, Bcs[s2]], writes=[Bt1[0]])
            P.op("dve", lambda e, psb_=psb_: e.tensor_tensor(t2[0][0:64, :], psb_[0:64, :], sn, ALU.mult), reads=[Bpsb, Bcs[s2]], writes=[Bt2[0]])
            P.op("dve", lambda e: e.tensor_tensor(KRs[s2][0:64, :], t1[0][0:64, :], t2[0][0:64, :], ALU.add),
                 reads=[Bt1[0], Bt2[0]], writes=[BKR[s2]])
            if int(os.environ.get('ASTOP', '99')) == 4:
                return
            if ti > 0:
                P.op("pool", lambda e: e.tensor_copy(xpb[:, :, 0:16], xpb[:, :, 512:528]), reads=[Bxp], writes=[Bxp])
                P.op("pool", lambda e: e.tensor_copy(xqb[:, :, 0:3], xqb[:, :, 512:515]), reads=[Bxq], writes=[Bxq])
            for c in range(2):
                ps, Bps = proj_fm(704 + c * 128, 128)
                P.op("act", lambda e, c=c, ps=ps: e.copy(xpb[:, c, 16:528], ps), reads=[Bps], writes=[Bxp])
            P.op("dve", lambda e: e.tensor_tensor(T1[:, :, 1:528], xpb[:, :, 1:528], xpb[:, :, 0:527], ALU.add), reads=[Bxp], writes=[BT1])
            P.op("dve", lambda e: e.tensor_tensor(T2[:, :, 3:528], T1[:, :, 3:528], T1[:, :, 1:526], ALU.add), reads=[BT1], writes=[BT2])
            P.op("dve", lambda e: e.tensor_tensor(T1[:, 1, 7:528], T2[:, 1, 7:528], T2[:, 1, 3:524], ALU.add), reads=[BT2], writes=[BT1])
            P.op("dve", lambda e: e.tensor_tensor(T2[64:128, 1, 15:528], T1[64:128, 1, 15:528], T1[64:128, 1, 7:520], ALU.add), reads=[BT1], writes=[BT2])
            pv = pinv[0] if ti == 0 else pinv[1]
            P.op("dve", lambda e, pv=pv: e.tensor_tensor(T1[0:64, :, 16:528], T1[0:64, :, 16:528], pv[0:64, :, :], ALU.mult), reads=[BT1, Bsm], writes=[BT1])
            P.op("dve", lambda e, pv=pv: e.tensor_tensor(T2[64:128, :, 16:528], T2[64:128, :, 16:528], pv[64:128, :, :], ALU.mult), reads=[BT2, Bsm], writes=[BT2])
            P.op("dve", lambda e: e.tensor_tensor(ybf[0:64, :, :], T1[0:64, :, 16:528], xpb[0:64, :, 16:528], ALU.subtract), reads=[BT1, Bxp], writes=[Bybf])
            P.op("dve", lambda e: e.tensor_tensor(ybf[64:128, :, :], T2[64:128, :, 16:528], xpb[64:128, :, 16:528], ALU.subtract), reads=[BT2, Bxp], writes=[Bybf])
            if int(os.environ.get('ASTOP', '99')) == 5:
                return
            for c in range(2):
                ps, Bps = proj_fm(960 + c * 128, 128)
                P.op("act", lambda e, c=c, ps=ps: e.copy(xqb[:, c, 3:515], ps), reads=[Bps], writes=[Bxq])
            for c in range(2):
                P.op("dve", lambda e, c=c: e.tensor_scalar(acc[:, c, :], xqb[:, c, 0:512], wmc_t[:, c, 0:1], bmc_t[:, c:c + 1], ALU.mult, ALU.add),
                     reads=[Bxq, Bsm], writes=[Bacc[c]])
                for k in range(1, 4):
                    P.op("dve", lambda e, c=c, k=k: e.scalar_tensor_tensor(acc[:, c, :], xqb[:, c, k:k + 512], wmc_t[:, c, k:k + 1], acc[:, c, :], ALU.mult, ALU.add),
                         reads=[Bxq, Bsm], writes=[Bacc[c]])
                P.op("act", lambda e, c=c: e.activation(cT[:, c, :], acc[:, c, :], AF.Silu), reads=[Bacc[c]], writes=[BcT[c]])
            if int(os.environ.get('ASTOP', '99')) == 6:
                return
            for j in range(4):
                ps, Bps = ring()
                mm([(ps, [(hT[:, k, j * 128:(j + 1) * 128], win[:, k, 1216:1728]) for k in range(8)])], [Bw] + BhT, Bps)
                P.op("act", lambda e, j=j, ps=ps: e.copy(xvo[:, j, :], ps), reads=[Bps], writes=[Bxvo[j]])
                ps, Bps = ring()
                mm([(ps[:, 0:8], [(hT[:, k, j * 128:(j + 1) * 128], win[:, k, 1728:1736]) for k in range(8)])], [Bw] + BhT, Bps)
                P.op("dve", lambda e, j=j, ps=ps: e.tensor_tensor(gz[:, j, :], ps[:, 0:8], bif_t, ALU.add), reads=[Bps, Bsm], writes=[Bgz[j]])
            if int(os.environ.get('ASTOP', '99')) == 7:
                return
            for c in range(5):
                i = 0 if c < 3 else 1
                gcol = gqs[:, c:c + 1] if c < 3 else gkv_t[:, c - 3:c - 2]
                P.op("dve", lambda e, c=c, i=i, gcol=gcol: e.scalar_tensor_tensor(cn[:, c, :], craw[:, c, :], gcol, rr[:, i, :], ALU.mult, ALU.mult),
                     reads=[Bcraw[c], Brr[i], Bgqs, Bsm], writes=[Bcn[c]])
            for h in range(4):
                ps, Bps = ring()
                mm([(ps, [(wuq[:, c, h * 256:h * 256 + 128], cn[:, c, :]) for c in range(3)])], [Bw] + Bcn[0:3], Bps)
                P.op("act", lambda e, h=h, ps=ps: e.copy(QNs[s2][:, h, :], ps), reads=[Bps], writes=[BQN[s2]])
                psa, Bpsa = ring()
                mm([(psa[0:64, :], [(wuq[:, c, h * 256 + 128:h * 256 + 192], cn[:, c, :]) for c in range(3)])], [Bw] + Bcn[0:3], Bpsa)
                psb_, Bpsb = ring()
                mm([(psb_[0:64, :], [(wuq[:, c, h * 256 + 192:h * 256 + 256], cn[:, c, :]) for c in range(3)])], [Bw] + Bcn[0:3], Bpsb)
                a_ = 0
                P.op("dve", lambda e, psa=psa, a_=a_: e.tensor_tensor(t1[a_][0:64, :], psa[0:64, :], cs, ALU.mult), reads=[Bpsa, Bcs[s2]], writes=[Bt1[a_]])
                P.op("dve", lambda e, psb_=psb_, a_=a_: e.tensor_tensor(t2[a_][0:64, :], psb_[0:64, :], sn, ALU.mult), reads=[Bpsb, Bcs[s2]], writes=[Bt2[a_]])
                P.op("dve", lambda e, h=h, a_=a_: e.tensor_tensor(QRs[s2][0:64, h, :], t1[a_][0:64, :], t2[a_][0:64, :], ALU.add),
                     reads=[Bt1[a_], Bt2[a_]], writes=[BQR[s2]])
            for h in range(4):
                ps, Bps = ring()
                mm([(ps, [(wk[:, c, h * 128:(h + 1) * 128], cn[:, 3 + c, :]) for c in range(2)])], [Bw] + Bcn[3:5], Bps)
                if h % 2 == 0:
                    P.op("act", lambda e, h=h, ps=ps: e.copy(KTs[s2][:, h, :], ps), reads=[Bps], writes=[BKT[s2]])
                else:
                    P.op("dve", lambda e, h=h, ps=ps: e.tensor_copy(KTs[s2][:, h, :], ps), reads=[Bps], writes=[BKT[s2]])
            for j in range(4):
                ps, Bps = ring()
                mm([(ps, [(cn[:, 3 + c, j * 128:(j + 1) * 128], wv[:, c, :]) for c in range(2)])], [Bw] + Bcn[3:5], Bps)
                if j % 2 == 0:
                    P.op("act", lambda e, j=j, ps=ps: e.copy(Vs[s2][:, j, :], ps), reads=[Bps], writes=[BV[s2]])
                else:
                    P.op("dve", lambda e, j=j, ps=ps: e.tensor_copy(Vs[s2][:, j, :], ps), reads=[Bps], writes=[BV[s2]])
            if int(os.environ.get('ASTOP', '99')) == 8:
                return
            for c in range(2):
                ps, Bps = ring()
                mm([(ps, [(bp[:, c, :], ybf[:, c, :])])], [Bw, Bybf], Bps)
                P.op("act", lambda e, c=c, ps=ps: e.activation(PTs[s2][:, c, :], ps, AF.Copy, scale=sp_t[:, c:c + 1]),
                     reads=[Bps, Bsm], writes=[BPT[s2]])
            for c in range(2):
                ps, Bps = ring()
                mm([(ps, [(bq[:, c, :], cT[:, c, :])])], [Bw, BcT[c]], Bps)
                P.op("act", lambda e, c=c, ps=ps: e.copy(mqT[:, c, :], ps), reads=[Bps], writes=[BmqT[c]])
                ps, Bps = ring()
                mm([(ps, [(bk[:, c, :], cT[:, c, :])])], [Bw, BcT[c]], Bps)
                P.op("act", lambda e, c=c, ps=ps: e.mul(mkTe[0:64, c, :], ps[0:64, :], 0.125), reads=[Bps], writes=[BmkT[c]])
                P.op("act", lambda e, c=c, ps=ps: e.mul(mkTo[64:128, c, :], ps[64:128, :], 0.125), reads=[Bps], writes=[BmkT[c]])
            for j in range(4):
                ps, Bps = ring()
                mm([(ps[:, c * 128:(c + 1) * 128], [(cT[:, c, j * 128:(j + 1) * 128], bk[:, c, :])]) for c in range(2)], [Bw] + BcT, Bps)
                P.op("act", lambda e, j=j, ps=ps: e.mul(mktok[:, j, :], ps[:, 0:256], 0.125), reads=[Bps], writes=[Bmktok[j]])
            NJ = 4
            X = [dict() for _ in range(NJ)]
            for j in range(NJ):
                u = j
                x_ = X[j]
                smv = sm[u]
                x_.update(u=u, b=Bsm2[u], e1=smv[:, 0:4], spv=smv[:, 4:8], qsc=smv[:, 8:12], ksc=smv[:, 12:16], ktm=smv[:, 16:20],
                          dnt=smv[:, 20:24], rec=smv[:, 24:28], ff=smv[:, 28:32], msum=smv[:, 32:36], vsum=smv[:, 36:40],
                          rst=smv[:, 40:44], vtm=smv[:, 44:48], ngt=smv[:, 48:52], zi=gz[:, j, 0:4], zf=gz[:, j, 4:8],
                          af=afull[u], va=vaug[u], sc=scm[u], hv=hh[u], hcv=hc[u], hq=hsq[u])

            def st_gate1(j):
                x_ = X[j]; b_ = x_["b"]
                P.op("act", lambda e, x_=x_: e.activation(x_["e1"], x_["zf"], AF.Exp, scale=-1.0), reads=[Bgz[j]], writes=[b_[0]])

            def st_gate2(j):
                x_ = X[j]; b_ = x_["b"]
                P.op("act", lambda e, x_=x_: e.activation(x_["spv"], x_["e1"], AF.Ln, bias=1.0), reads=[b_[0]], writes=[b_[1]])

            def st_cum(j):
                x_ = X[j]; b_ = x_["b"]
                psc, Bpsc = ring()
                mm([(psc[:, 0:4], [(C["tri_f"], x_["spv"])]), (psc[:, 8:12], [(C["ones_f"], x_["spv"])])], [b_[1]] + BC, Bpsc)
                x_["psc"] = psc; x_["Bpsc"] = Bpsc

            def st_scales(j):
                x_ = X[j]; b_ = x_["b"]; psc = x_["psc"]; Bpsc = x_["Bpsc"]; u = x_["u"]
                P.op("act", lambda e, x_=x_, psc=psc: e.activation(x_["qsc"], psc[:, 0:4], AF.Exp, scale=-1.0), reads=[Bpsc], writes=[b_[2]])
                P.op("dve", lambda e, x_=x_, psc=psc: e.tensor_tensor(x_["ktm"], psc[:, 0:4], x_["zi"], ALU.add), reads=[Bpsc, Bgz[j]], writes=[b_[3]])
                P.op("act", lambda e, x_=x_, psc=psc: e.activation(x_["af"][0:64, 0:2], psc[0:64, 8:12:2], AF.Exp, scale=-1.0), reads=[Bpsc], writes=[Baf[u]])
                P.op("act", lambda e, x_=x_, psc=psc: e.activation(x_["af"][64:128, 0:2], psc[64:128, 9:12:2], AF.Exp, scale=-1.0), reads=[Bpsc], writes=[Baf[u]])

            def st_ksc(j):
                x_ = X[j]; b_ = x_["b"]
                P.op("act", lambda e, x_=x_: e.activation(x_["ksc"], x_["ktm"], AF.Exp), reads=[b_[3]], writes=[b_[4]])

            def st_vaug(j):
                x_ = X[j]; b_ = x_["b"]; u = x_["u"]; va = x_["va"]; ksc = x_["ksc"]
                P.op("dve", lambda e, va=va, j=j, ksc=ksc: e.tensor_tensor(va[:, :, 0:64], xvo[:, j, 0:256].rearrange("p (h d) -> p h d", d=64),
                                                                            ksc.unsqueeze(2).broadcast_to([128, 4, 64]), ALU.mult),
                     reads=[Bxvo[j], b_[4]], writes=[Bvaug[u]])
                P.op("dve", lambda e, va=va, ksc=ksc: e.tensor_copy(va[:, :, 64:66], ksc.unsqueeze(2).broadcast_to([128, 4, 2])), reads=[b_[4]], writes=[Bvaug[u]])
                P.op("act", lambda e, u=u, j=j: e.activation(sig[u], xvo[:, j, 256:512], AF.Sigmoid), reads=[Bxvo[j]], writes=[Bsig[u]])

            def st_scores(j):
                x_ = X[j]; u = x_["u"]; sc = x_["sc"]
                pss, Bpss = ring()
                groups = []
                for h in range(4):
                    c = h // 2
                    mkx = mkTe if h % 2 == 0 else mkTo
                    groups.append((pss[:, h * 128:(h + 1) * 128],
                                   [(mkx[:, c, j * 128:(j + 1) * 128], mqT[:, c, j * 128:(j + 1) * 128])]))
                mm(groups, BmkT + BmqT, Bpss)
                P.op("dve", lambda e, sc=sc, pss=pss: e.tensor_tensor(sc, pss.rearrange("p (h t) -> p h t", t=128),
                                                                      C["tri_b"].unsqueeze(1).broadcast_to([128, 4, 128]), ALU.mult),
                     reads=[Bpss] + BC, writes=[Bscm[u]])

            def st_state(j):
                x_ = X[j]; u = x_["u"]; sc = x_["sc"]; va = x_["va"]; af = x_["af"]
                cur = sb_i[0]
                psn, Bpsn = ring()
                groups = []
                for h in range(4):
                    c = h // 2
                    groups.append((psn[:, h * 66:(h + 1) * 66],
                                   [(sc[:, h, :], va[:, h, :]),
                                    (mqT[:, c, j * 128:(j + 1) * 128], (Sbe if h % 2 == 0 else Sbo)[cur][:, c, :])]))
                mm(groups, [Bscm[u], Bvaug[u], BSb[cur]] + BmqT, Bpsn)
                x_["psn"] = psn; x_["Bpsn"] = Bpsn
                psu, Bpsu = ring()
                va2 = va.rearrange("p h n -> p (h n)")
                mm([(psu[:, c * 132:(c + 1) * 132], [(mktok[:, j, c * 128:(c + 1) * 128], va2[:, c * 132:(c + 1) * 132])]) for c in range(2)],
                   [Bmktok[j], Bvaug[u]], Bpsu)
                psu3 = psu[:, 0:264].rearrange("p (c n) -> p c n", n=132)
                P.op("dve", lambda e, psu3=psu3: e.tensor_tensor(Stmp[0:64, :, :], psu3[0:64, :, 0:66], Sf[0:64, :, :], ALU.add), reads=[Bpsu, BSf], writes=[BStmp])
                P.op("dve", lambda e, psu3=psu3: e.tensor_tensor(Stmp[64:128, :, :], psu3[64:128, :, 66:132], Sf[64:128, :, :], ALU.add), reads=[Bpsu, BSf], writes=[BStmp])
                P.op("dve", lambda e, af=af: e.tensor_tensor(Sf, Stmp, af.unsqueeze(2).broadcast_to([128, 2, 66]), ALU.mult), reads=[BStmp, Baf[u]], writes=[BSf])
                nxt = 1 - cur
                P.op("pool", lambda e, nxt=nxt: e.tensor_copy(Sbe[nxt][0:64, :, :], Sf[0:64, :, :]), reads=[BSf], writes=[BSb[nxt]])
                P.op("pool", lambda e, nxt=nxt: e.tensor_copy(Sbo[nxt][64:128, :, :], Sf[64:128, :, :]), reads=[BSf], writes=[BSb[nxt]])
                sb_i[0] = nxt

            def mk_post():
                steps = []

                def s1(j):
                    x_ = X[j]; b_ = x_["b"]
                    psn3 = x_["psn"][:, 0:264].rearrange("p (h n) -> p h n", n=66)
                    x_["psn3"] = psn3
                    P.op("dve", lambda e, x_=x_, psn3=psn3: e.tensor_tensor(x_["dnt"].unsqueeze(2), psn3[:, :, 64:65], x_["qsc"].unsqueeze(2), ALU.mult), reads=[x_["Bpsn"], b_[2]], writes=[b_[5]])
                steps.append(s1)
                steps.append(lambda j: P.op("dve", lambda e, x_=X[j]: e.tensor_scalar_mul(x_["ngt"], x_["dnt"], -1.0), reads=[X[j]["b"][5]], writes=[X[j]["b"][6]]))
                steps.append(lambda j: P.op("dve", lambda e, x_=X[j]: e.tensor_tensor(x_["dnt"], x_["dnt"], x_["ngt"], ALU.max), reads=[X[j]["b"][6]], writes=[X[j]["b"][5]]))
                steps.append(lambda j: P.op("dve", lambda e, x_=X[j]: e.tensor_scalar_max(x_["dnt"], x_["dnt"], 1.0), reads=[X[j]["b"][5]], writes=[X[j]["b"][5]]))
                steps.append(lambda j: P.op("dve", lambda e, x_=X[j]: e.reciprocal(x_["rec"], x_["dnt"]), reads=[X[j]["b"][5]], writes=[X[j]["b"][6]]))
                steps.append(lambda j: P.op("dve", lambda e, x_=X[j]: e.tensor_tensor(x_["ff"], x_["rec"], x_["qsc"], ALU.mult), reads=[X[j]["b"][6], X[j]["b"][2]], writes=[X[j]["b"][7]]))
                steps.append(lambda j: P.op("dve", lambda e, x_=X[j]: e.tensor_tensor(x_["hv"], x_["psn3"][:, :, 0:64], x_["ff"].unsqueeze(2).broadcast_to([128, 4, 64]), ALU.mult),
                                            reads=[X[j]["Bpsn"], X[j]["b"][7]], writes=[Bhh[X[j]["u"]]]))
                steps.append(lambda j: P.op("dve", lambda e, x_=X[j]: e.reduce_sum(x_["msum"], x_["hv"], AX.X), reads=[Bhh[X[j]["u"]]], writes=[X[j]["b"][8]]))
                steps.append(lambda j: P.op("dve", lambda e, x_=X[j]: e.tensor_scalar_mul(x_["msum"], x_["msum"], 1.0 / 64.0), reads=[X[j]["b"][8]], writes=[X[j]["b"][8]]))
                steps.append(lambda j: P.op("dve", lambda e, x_=X[j]: e.tensor_tensor(x_["hcv"], x_["hv"], x_["msum"].unsqueeze(2).broadcast_to([128, 4, 64]), ALU.subtract),
                                            reads=[Bhh[X[j]["u"]], X[j]["b"][8]], writes=[Bhc[X[j]["u"]]]))
                steps.append(lambda j: P.op("pool", lambda e, x_=X[j]: e.tensor_tensor(x_["hq"], x_["hcv"], x_["hcv"], ALU.mult), reads=[Bhc[X[j]["u"]]], writes=[Bhsq[X[j]["u"]]]))
                steps.append(lambda j: P.op("dve", lambda e, x_=X[j]: e.reduce_sum(x_["vsum"], x_["hq"], AX.X), reads=[Bhsq[X[j]["u"]]], writes=[X[j]["b"][9]]))
                steps.append(lambda j: P.op("dve", lambda e, x_=X[j]: e.tensor_scalar(x_["vtm"], x_["vsum"], 1.0 / 64.0, LN_EPS, ALU.mult, ALU.add), reads=[X[j]["b"][9]], writes=[X[j]["b"][10]]))
                steps.append(lambda j: P.op("pool", lambda e, x_=X[j]: e.tensor_tensor(x_["rst"], x_["vtm"], C["nh"][:, 0:4], ALU.pow), reads=[X[j]["b"][10]] + BC, writes=[X[j]["b"][11]]))
                steps.append(lambda j: P.op("dve", lambda e, x_=X[j]: e.tensor_tensor(x_["hcv"], x_["hcv"], x_["rst"].unsqueeze(2).broadcast_to([128, 4, 64]), ALU.mult),
                                            reads=[X[j]["b"][11]], writes=[Bhc[X[j]["u"]]]))
                steps.append(lambda j: P.op("pool", lambda e, x_=X[j]: e.tensor_tensor(x_["hcv"].rearrange("p h d -> p (h d)"), x_["hcv"].rearrange("p h d -> p (h d)"), gmh_t, ALU.mult),
                                            reads=[Bsm], writes=[Bhc[X[j]["u"]]]))
                steps.append(lambda j: P.op("pool", lambda e, x_=X[j]: e.tensor_tensor(mtok[x_["u"]], x_["hcv"].rearrange("p h d -> p (h d)"), sig[x_["u"]], ALU.mult),
                                            reads=[Bhc[X[j]["u"]], Bsig[X[j]["u"]]], writes=[Bmtok[X[j]["u"]]]))

                def s_tr(j):
                    u = X[j]["u"]
                    pst, Bpst = ring()
                    pstb = pst.bitcast(BF16)
                    tr([(pstb[:, c * 128:(c + 1) * 128], mtok[u][:, c * 128:(c + 1) * 128]) for c in range(2)], [Bmtok[u]] + BC, Bpst, ident)
                    P.op("act", lambda e, j=j, pstb=pstb: e.copy(MTs[s2][:, :, j * 128:(j + 1) * 128], pstb[:, 0:256].rearrange("p (c t) -> p c t", t=128)),
                         reads=[Bpst], writes=[BMT[s2]])
                steps.append(s_tr)
                return steps

            for st in (st_gate1, st_gate2, st_cum, st_scales, st_ksc, st_vaug, st_scores):
                for j in range(NJ):
                    st(j)
            for j in range(NJ):
                st_state(j)
            for st in mk_post():
                for j in range(NJ):
                    st(j)
            if int(os.environ.get('ASTOP', '99')) == 15:
                return
            P.dma("sp", QN[:, :, t0:t0 + 512].rearrange("h p t -> p h t"), QNs[s2], BQN[s2], reads=[BQN[s2]], writes=[B_QN], acc=True)
            P.dma("sp", QR[:, :, t0:t0 + 512].rearrange("h p t -> p h t"), QRs[s2][0:64, :, :], BQR[s2], reads=[BQR[s2]], writes=[B_QR], acc=True)
            P.dma("sp", KT[:, :, t0:t0 + 512].rearrange("h p t -> p h t"), KTs[s2], BKT[s2], reads=[BKT[s2]], writes=[B_KT], acc=True)
            P.dma("sp", KR[:, t0:t0 + 512], KRs[s2][0:64, :], BKR[s2], reads=[BKR[s2]], writes=[B_KR], acc=True)
            P.dma("sp", VV[t0:t0 + 512, :].rearrange("(j p) c -> p j c", p=128), Vs[s2], BV[s2], reads=[BV[s2]], writes=[B_VV], acc=True)
            P.dma("sp", MIXT[512:768, t0:t0 + 512].rearrange("(c p) t -> p c t", p=128), PTs[s2], BPT[s2], reads=[BPT[s2]], writes=[B_MIX], acc=True)
            P.dma("sp", MIXT[768:1024, t0:t0 + 512].rearrange("(c p) t -> p c t", p=128), MTs[s2], BMT[s2], reads=[BMT[s2]], writes=[B_MIX], acc=True)

    def phaseB(l):
        reset_phase()
        NKT = S // 128
        Kt = [carve(S, BF16) for _ in range(2)]; Vt = [c3(NKT, 128, BF16) for _ in range(2)]
        BK = P.bufs(2); BVt = P.bufs(2)
        Kr = carve(S, BF16); BKr = P.buf()
        qn = [carve(512, BF16) for _ in range(2)]; qr = [carve(512, BF16) for _ in range(2)]; Bq = P.bufs(2)
        NP = int(os.environ.get('B_NP', '8'))
        Pt = [carve(512, BF16) for _ in range(NP)]; BPt = P.bufs(NP)
        rec = carve(512); Brec = P.buf()
        Pacc = [[carve(512) for _ in range(2)] for _ in range(2)]
        BPacc = [P.bufs(2) for _ in range(2)]
        ost = [carve(512, BF16) for _ in range(2)]; Bost = P.bufs(2)
        P.dma("sp", Kr[0:64, :], KR, BKr, reads=[B_KR], writes=[BKr])
        bcfg = [int(v) for v in os.environ.get("B_BANKS", "0,1,2,3,4,5,6,7").split(",")]
        set_ring(bcfg[4:8])
        pi = 0
        qi = 0
        if "B" in os.environ.get("SCHED_PH", "0ACD"):
            P.begin()
        def loadkv(h_):
            hs_ = h_ % 2
            P.dma("sp", Kt[hs_], KT[h_], BK[hs_], reads=[B_KT], writes=[BK[hs_]])
            P.dma("sp", Vt[hs_], VV[:, h_ * 128:(h_ + 1) * 128].rearrange("(n p) c -> p n c", p=128), BVt[hs_], reads=[B_VV], writes=[BVt[hs_]])
        loadkv(0)
        for h in range(4):
            hs = h % 2
            if h + 1 < 4:
                loadkv(h + 1)
            def loadq(h_, qb_, qs_):
                q0_ = qb_ * 512
                P.dma("sp", qn[qs_], QN[h_, :, q0_:q0_ + 512], Bq[qs_], reads=[B_QN], writes=[Bq[qs_]])
                P.dma("sp", qr[qs_][0:64, :], QR[h_, :, q0_:q0_ + 512], Bq[qs_], reads=[B_QR], writes=[Bq[qs_]], acc=True)
            NQB = S // 512
            if h == 0:
                loadq(0, 0, 0)
            for qb in range(NQB):
                q0 = qb * 512
                qs = qi % 2
                qi += 1
                if qb + 1 < NQB:
                    loadq(h, qb + 1, qi % 2)
                elif h + 1 < 4:
                    loadq(h + 1, 0, qi % 2)
                Obank = banks[bcfg[qs]]; BO = Bbank[bcfg[qs]]
                Dbank = banks[bcfg[2 + qs]]; BD = Bbank[bcfg[2 + qs]]
                nkt = 4 * qb + 4
                LA = int(os.environ.get('B_LA', '4'))
                DEN_PE = os.environ.get('B_DEN', 'pe') == 'pe'
                for ch in range(2):
                    P.op("pool", lambda e, qs=qs, ch=ch: e.memset(Pacc[qs][ch], 0.0), writes=[BPacc[qs][ch]])
                info = {}

                def stage1(kt):
                    nonlocal pi
                    jd = kt - 4 * qb
                    c0 = 128 * jd if jd > 0 else 0
                    ps, Bps = ring()
                    mm([(ps[:, c0:512], [(Kt[hs][:, kt * 128:(kt + 1) * 128], qn[qs][:, c0:512]),
                                         (Kr[0:64, kt * 128:(kt + 1) * 128], qr[qs][0:64, c0:512])])],
                       [BK[hs], BKr, Bq[qs]], Bps)
                    p_ = pi % NP
                    pi += 1
                    P.op("act", lambda e, p_=p_, ps=ps, c0=c0: e.activation(Pt[p_][:, c0:512], ps[:, c0:512], AF.Exp), reads=[Bps], writes=[BPt[p_]])
                    if jd >= 0:
                        P.op("pool", lambda e, p_=p_, c0=c0: e.tensor_tensor(Pt[p_][:, c0:c0 + 128], Pt[p_][:, c0:c0 + 128], C["tri_b"], ALU.mult),
                             reads=BC, writes=[BPt[p_]])
                    info[kt] = (p_, c0)

                def stage2(kt):
                    p_, c0 = info.pop(kt)
                    first = (kt == 0); last = (kt == nkt - 1)

                    def fnO(e, p_=p_, c0=c0, first=first, last=last, kt=kt, Obank=Obank, hs=hs):
                        return e.matmul(Obank[:, c0:512], Vt[hs][:, kt, :], Pt[p_][:, c0:512], start=first, stop=last)
                    P.op("pe", fnO, reads=[BVt[hs], BPt[p_]], writes=[BO])

                    ch = kt % 2
                    if DEN_PE:
                        def fnD(e, p_=p_, c0=c0, first=first, last=last, Dbank=Dbank):
                            return e.matmul(Dbank[:, c0:512], C["ones_b"], Pt[p_][:, c0:512], start=first, stop=last)
                        P.op("pe", fnD, reads=[BPt[p_]] + BC, writes=[BD])
                    else:
                        P.op("dve", lambda e, p_=p_, c0=c0, ch=ch, qs=qs: e.tensor_tensor(Pacc[qs][ch][:, c0:512], Pacc[qs][ch][:, c0:512], Pt[p_][:, c0:512], ALU.add),
                             reads=[BPt[p_]], writes=[BPacc[qs][ch]])

                for kt in range(nkt):
                    stage1(kt)
                    if kt >= LA:
                        stage2(kt - LA)
                for kt in range(max(0, nkt - LA), nkt):
                    stage2(kt)
                def fnD(e, Dbank=Dbank, qs=qs):
                    e.matmul(Dbank, C["ones_f"], Pacc[qs][0], start=True, stop=False)
                    return e.matmul(Dbank, C["ones_f"], Pacc[qs][1], start=False, stop=True)
                if not DEN_PE:
                    P.op("pe", fnD, reads=BPacc[qs] + BC, writes=[BD])
                P.op("dve", lambda e, Dbank=Dbank: e.reciprocal(rec, Dbank), reads=[BD], writes=[Brec])
                os_ = qi % 2
                P.op("dve", lambda e, os_=os_, Obank=Obank: e.tensor_tensor(ost[os_], Obank, rec, ALU.mult), reads=[BO, Brec], writes=[Bost[os_]])
                P.dma("sp", MIXT[h * 128:(h + 1) * 128, q0:q0 + 512], ost[os_], Bost[os_], reads=[Bost[os_]], writes=[B_MIX], acc=True)

    def phaseC1(l, hin, Bhin, hmid, Bhmid):
        reset_phase()
        set_ring(range(8))
        wo = c3(8, 1024, BF16)
        Bw = P.buf("wC1")
        for k in range(8):
            P.dma("pool", wo[:, k, :], w_out[l, k * 128:(k + 1) * 128, :], Bw, writes=[Bw], acc=True)
        g1 = carve(1024); b1 = carve(1024)
        Bsm = P.buf("smallC1")
        P.dma("sp", g1, ln1_g[l].partition_broadcast(128), Bsm, writes=[Bsm], acc=True)
        P.dma("sp", b1, ln1_b[l].partition_broadcast(128), Bsm, writes=[Bsm], acc=True)
        mx = [c3(8, 512, BF16) for _ in range(2)]; Bmx = P.bufs(2)
        hx = [c3(4, 1024) for _ in range(2)]; Bhx = P.bufs(2)
        r1 = [carve(1024) for _ in range(2)]; Br1 = P.bufs(2)
        h1 = [carve(1024) for _ in range(2)]; Bh1 = P.bufs(2)
        tmps = [ln_tmp() for _ in range(2)]

        def load(ti):
            s = ti % 2
            t0 = ti * 512
            P.dma("sp", mx[s], MIXT[:, t0:t0 + 512].rearrange("(k p) t -> p k t", p=128), Bmx[s], reads=[B_MIX], writes=[Bmx[s]])
            P.dma("sp", hx[s], hin[t0:t0 + 512, :].rearrange("(j p) d -> p j d", p=128), Bhx[s], reads=[Bhin], writes=[Bhx[s]])

        if "C" in os.environ.get("SCHED_PH", "0ACD"):
            P.begin()
        load(0)
        n = 0
        for ti in range(NT5):
            s = ti % 2
            t0 = ti * 512
            if ti + 1 < NT5:
                load(ti + 1)
            for j in range(4):
                u = n % 2
                n += 1
                for hf in range(2):
                    ps, Bps = ring()
                    mm([(ps, [(mx[s][:, k, j * 128:(j + 1) * 128], wo[:, k, hf * 512:(hf + 1) * 512]) for k in range(8)])], [Bw, Bmx[s]], Bps)
                    P.op("dve", lambda e, j=j, hf=hf, ps=ps, s=s, u=u: e.scalar_tensor_tensor(r1[u][:, hf * 512:(hf + 1) * 512], hx[s][:, j, hf * 512:(hf + 1) * 512], ALPHA, ps, ALU.mult, ALU.add),
                         reads=[Bhx[s], Bps], writes=[Br1[u]])
                ln_tile(r1[u], Br1[u], h1[u], Bh1[u], g1, b1, Bsm, tmps[u])
                P.dma("sp", hmid[t0 + j * 128:t0 + (j + 1) * 128, :], h1[u], Bh1[u], reads=[Bh1[u]], writes=[Bhmid], acc=True)

    def phaseC2(l, hin, Bhin, hout, Bhout):
        reset_phase()
        set_ring(range(8))
        ident = C["ident"]
        TW = 256
        wu = c3(8, 2 * D_FF, BF16); wd = c3(NFC, 1024, BF16)
        Bw = P.buf("wC2")
        for k in range(8):
            for hf in range(2):
                P.dma("pool", wu[:, k, hf * D_FF:(hf + 1) * D_FF], w_up[l, k * 128:(k + 1) * 128, hf * D_FF:(hf + 1) * D_FF], Bw, writes=[Bw], acc=True)
        for c in range(NFC):
            P.dma("pool", wd[:, c, :], w_down[l, c * 128:(c + 1) * 128, :], Bw, writes=[Bw], acc=True)
        g2 = carve(1024); b2 = carve(1024)
        wf = c3(NFC, 3); bf_ = carve(NFC)
        Bsm = P.buf("smallC2")
        P.dma("sp", g2, ln2_g[l].partition_broadcast(128), Bsm, writes=[Bsm], acc=True)
        P.dma("sp", b2, ln2_b[l].partition_broadcast(128), Bsm, writes=[Bsm], acc=True)
        P.dma("sp", wf, wfc[l], Bsm, writes=[Bsm], acc=True)
        P.dma("sp", bf_, bfc[l], Bsm, writes=[Bsm], acc=True)
        hx = [c3(2, 1024) for _ in range(2)]; Bhx = P.bufs(2)
        h1b = carve(1024, BF16); Bh1b = P.buf()
        h1T = c3(8, TW, BF16); Bh1T = P.bufs(2)
        uT = [c3(NFC, TW, BF16) for _ in range(2)]; BuT = [P.bufs(NFC) for _ in range(2)]
        carry = c3(NFC, 2); Bcar = P.bufs(NFC)
        NG = 3
        gb_ = [carve(TW + 2) for _ in range(NG)]; Bgb = P.bufs(NG)
        tt = [carve(TW) for _ in range(NG)]; Btt = P.bufs(NG)
        ss = [carve(TW) for _ in range(NG)]; Bss = P.bufs(NG)
        r2 = [carve(1024) for _ in range(2)]; Br2 = P.bufs(2)
        tmpB = [ln_tmp() for _ in range(2)]
        P.op("pool", lambda e: e.memset(carry, 0.0), writes=Bcar)
        NTC = S // TW
        gi = [0]

        def load(ti):
            s = ti % 2
            t0 = ti * TW
            P.dma("sp", hx[s], hin[t0:t0 + TW, :].rearrange("(j p) d -> p j d", p=128), Bhx[s], reads=[Bhin], writes=[Bhx[s]])

        def up(ti):
            s = ti % 2
            us = ti % 2
            stA = {}
            for j in range(2):
                P.op("act", lambda e, j=j, s=s: e.copy(h1b, hx[s][:, j, :]), reads=[Bhx[s]], writes=[Bh1b])
                ps, Bps = ring()
                psb = ps.bitcast(BF16)
                tr([(psb[:, k * 128:(k + 1) * 128], h1b[:, k * 128:(k + 1) * 128]) for k in range(8)], [Bh1b] + BC, Bps, ident)
                P.op("act", lambda e, j=j, psb=psb: e.copy(h1T[:, :, j * 128:(j + 1) * 128], psb.rearrange("p (k t) -> p k t", t=128)),
                     reads=[Bps], writes=[Bh1T[j]])

            def s_mm(c):
                ps, Bps = ring()
                mm([(ps[:, 0:TW], [(wu[:, k, c * 128:(c + 1) * 128], h1T[:, k, :]) for k in range(8)]),
                    (ps[:, TW:2 * TW], [(wu[:, k, D_FF + c * 128:D_FF + (c + 1) * 128], h1T[:, k, :]) for k in range(8)])],
                   [Bw] + Bh1T, Bps)
                g_ = gi[0] % NG
                gi[0] += 1
                stA[c] = (ps, Bps, g_)

            def s_copy(c):
                ps, Bps, g_ = stA[c]
                gbv = gb_[g_]
                P.op("pool", lambda e, gbv=gbv, c=c: e.tensor_copy(gbv[:, 0:2], carry[:, c, :]), reads=[Bcar[c]], writes=[Bgb[g_]])
                P.op("act", lambda e, gbv=gbv, ps=ps: e.copy(gbv[:, 2:TW + 2], ps[:, TW:2 * TW]), reads=[Bps], writes=[Bgb[g_]])
                P.op("pool", lambda e, gbv=gbv, c=c: e.tensor_copy(carry[:, c, :], gbv[:, TW:TW + 2]), reads=[Bgb[g_]], writes=[Bcar[c]])

            def s_conv(c):
                ps, Bps, g_ = stA[c]
                gbv = gb_[g_]; tv = tt[g_]
                P.op("dve", lambda e, tv=tv, gbv=gbv, c=c: e.tensor_scalar(tv, gbv[:, 0:TW], wf[:, c, 0:1], bf_[:, c:c + 1], ALU.mult, ALU.add),
                     reads=[Bgb[g_], Bsm], writes=[Btt[g_]])
                for k in (1, 2):
                    P.op("dve", lambda e, tv=tv, gbv=gbv, c=c, k=k: e.scalar_tensor_tensor(tv, gbv[:, k:k + TW], wf[:, c, k:k + 1], tv, ALU.mult, ALU.add),
                         reads=[Bgb[g_], Bsm], writes=[Btt[g_]])

            def s_silu(c):
                ps, Bps, g_ = stA[c]
                P.op("act", lambda e, sv=ss[g_], tv=tt[g_]: e.activation(sv, tv, AF.Silu), reads=[Btt[g_]], writes=[Bss[g_]])

            def s_mult(c):
                ps, Bps, g_ = stA[c]
                P.op("dve", lambda e, sv=ss[g_], ps=ps, c=c: e.tensor_tensor(uT[us][:, c, :], ps[:, 0:TW], sv, ALU.mult), reads=[Bps, Bss[g_]], writes=[BuT[us][c]])

            for step in range(NFC + 3):
                if step < NFC:
                    s_mm(step)
                    s_copy(step)
                if 0 <= step - 1 < NFC:
                    s_conv(step - 1)
                if 0 <= step - 2 < NFC:
                    s_silu(step - 2)
                if 0 <= step - 3 < NFC:
                    s_mult(step - 3)

        def down(ti):
            us = ti % 2
            t0 = ti * TW
            for j in range(2):
                P.dma("sp", r2[j], hin[t0 + j * 128:t0 + (j + 1) * 128, :], Br2[j], reads=[Bhin], writes=[Br2[j]])
                for hf in range(2):
                    ps, Bps = ring()
                    mm([(ps, [(uT[us][:, c, j * 128:(j + 1) * 128], wd[:, c, hf * 512:(hf + 1) * 512]) for c in range(NFC)])], [Bw] + BuT[us], Bps)
                    P.op("dve", lambda e, j=j, hf=hf, ps=ps: e.scalar_tensor_tensor(r2[j][:, hf * 512:(hf + 1) * 512], r2[j][:, hf * 512:(hf + 1) * 512], ALPHA, ps, ALU.mult, ALU.add),
                         reads=[Bps], writes=[Br2[j]])
                ln_tile(r2[j], Br2[j], r2[j], Br2[j], g2, b2, Bsm, tmpB[j])
                P.dma("sp", hout[t0 + j * 128:t0 + (j + 1) * 128, :], r2[j], Br2[j], reads=[Br2[j]], writes=[Bhout], acc=True)

        if "D" in os.environ.get("SCHED_PH", "0ACD"):
            P.begin()
        load(0)
        for ti in range(NTC):
            if ti + 1 < NTC:
                load(ti + 1)
            up(ti)
            if ti >= 1:
                down(ti - 1)
        down(NTC - 1)

    if "0" in phases:
        phase0(hA, B_hA)
    B_y = P.buf("y")
    for l in range(L):
        if "A" in phases:
            phaseA(l, hA, B_hA)
        if "B" in phases:
            phaseB(l)
        if "C" in phases:
            phaseC1(l, hA, B_hA, hB, B_hB)
        if "D" in phases:
            if l == L - 1:
                phaseC2(l, hB, B_hB, y, B_y)
            else:
                phaseC2(l, hB, B_hB, hA, B_hA)
    P.barrier()
    P.emit()
    return nc


def prep_weights(inp, S, L):
    f = np.float32
    g = lambda k: np.asarray(inp[k], dtype=f)
    d = {}
    d["ln0_g"] = g("ln0_g"); d["ln0_b"] = g("ln0_b")
    w_in = g("w_in")[:L]
    kr = w_in[:, :, 640:704]
    krsw = np.concatenate([kr[:, :, 32:64], kr[:, :, 0:32]], axis=-1)
    d["w_in_e"] = np.ascontiguousarray(np.concatenate([w_in, krsw], axis=-1))
    w_uq = g("w_uq")[:L].reshape(L, 384, 4, 192)
    nope = w_uq[..., 0:128]; rope = w_uq[..., 128:192]
    ropesw = np.concatenate([rope[..., 32:64], rope[..., 0:32]], axis=-1)
    d["w_uq_e"] = np.ascontiguousarray(np.concatenate([nope, rope, ropesw], axis=-1).reshape(L, 384, 1024))
    w_ukv = g("w_ukv")[:L].reshape(L, 256, 4, 256)
    d["w_k"] = np.ascontiguousarray(w_ukv[..., 0:128].reshape(L, 256, 512))
    d["w_v"] = np.ascontiguousarray(w_ukv[..., 128:256].reshape(L, 256, 512))

    def blockdiag(w):
        o = np.zeros((L, 128, 2, 128), f)
        for h in range(4):
            c, a = h // 2, (h % 2) * 64
            o[:, a:a + 64, c, a:a + 64] = w[:, h]
        return o
    d["bdp"] = blockdiag(g("w_pool")[:L]); d["bdq"] = blockdiag(g("w_mq")[:L]); d["bdk"] = blockdiag(g("w_mk")[:L])

    def pk(v, n):
        return np.ascontiguousarray(v.reshape(L, n, 128).transpose(0, 2, 1))
    d["gq"] = pk(g("g_qn")[:L], 3); d["gkv"] = pk(g("g_kvn")[:L], 2); d["spool"] = pk(g("s_pool")[:L], 2)
    d["wmc"] = np.ascontiguousarray(g("w_mconv")[:L].reshape(L, 4, 2, 128).transpose(0, 3, 2, 1))
    d["bmc"] = pk(g("b_mconv")[:L], 2)
    d["bif"] = np.ascontiguousarray(np.concatenate([g("b_i")[:L], g("b_f")[:L]], axis=-1))
    d["gmh"] = g("g_mh")[:L]
    d["w_out"] = g("w_out")[:L]; d["ln1_g"] = g("ln1_g")[:L]; d["ln1_b"] = g("ln1_b")[:L]
    d["w_up"] = g("w_up")[:L]
    d["wfc"] = np.ascontiguousarray(g("w_fconv")[:L].reshape(L, 3, NFC, 128).transpose(0, 3, 2, 1))
    d["bfc"] = pk(g("b_fconv")[:L], NFC)
    d["w_down"] = g("w_down")[:L]; d["ln2_g"] = g("ln2_g")[:L]; d["ln2_b"] = g("ln2_b")[:L]
    d["c_ident"] = np.eye(128, dtype=f)
    d["c_tri"] = np.triu(np.ones((128, 128), f))
    inv = (1.0 / (10000.0 ** (np.arange(0, 64, 2, dtype=f) / f(64)))).astype(f)
    ang = np.arange(S, dtype=f)[:, None] * inv[None, :]
    cos = np.cos(ang).astype(f).T; sin = np.sin(ang).astype(f).T
    d["c_cs"] = np.ascontiguousarray(np.concatenate([cos, cos], axis=0))
    d["c_sn"] = np.ascontiguousarray(np.concatenate([-sin, sin], axis=0))
    pinv = np.zeros((2, 128, 2, 512), f)
    wins = (2, 4, 8, 16)
    t = np.arange(512)
    for gi in range(4):
        c, a = gi // 2, (gi % 2) * 64
        pinv[0, a:a + 64, c, :] = (1.0 / np.minimum(t + 1, wins[gi]).astype(f))[None, :]
        pinv[1, a:a + 64, c, :] = f(1.0) / f(wins[gi])
    d["c_pinv"] = pinv
    return d


_CACHE = {}


def kernel(**inputs):
    x = np.asarray(inputs["x"], dtype=np.float32)
    B, S, D = x.shape
    L = DEPTH_FULL
    key = (S, L)
    if key not in _CACHE:
        _CACHE[key] = build(S, L)
    nc = _CACHE[key]
    w = prep_weights(inputs, S, L)
    in_maps = []
    for b in range(B):
        m = dict(w)
        m["x"] = np.ascontiguousarray(x[b])
        in_maps.append(m)
    res = run_bass_kernel_spmd(nc, in_maps, core_ids=list(range(B)))
    return np.stack([np.asarray(r["y"], dtype=np.float32) for r in res.results], axis=0)
```
